# Optimizing a Trainium2 kernel written in Bass

```python
import numpy as np
import jax
import jax.numpy as jnp
from jax import lax

D_MODEL = 4096
BATCH = 2
SEQ = 4096
DEPTH = 2

HEAD_DIM = 128
D_MIX = D_MODEL
N_MIX_HEADS = D_MIX // HEAD_DIM
H_MIX = N_MIX_HEADS // 4
NSA_KV = max(1, H_MIX // 4)
NSA_CMP_LEN = 32
NSA_CMP_STRIDE = 16
NSA_CMP_HIDDEN = 256
NSA_SLC_LEN = 64
NSA_TOPK = 16
NSA_WINDOW = 512
DIL_CONFIGS = ((128, 1), (512, 4), (2048, 16))
D_FF = 4 * D_MODEL
PLE_DIM = 256
ROPE_THETA = 10000.0
Q_BLOCK = 128
RMS_EPS = 1e-6
NEG = -1e30
FORCED_SCORE = 1e9

SPLITS = (
    H_MIX * HEAD_DIM,
    NSA_KV * HEAD_DIM, NSA_KV * HEAD_DIM,
    NSA_KV * HEAD_DIM, NSA_KV * HEAD_DIM,
    NSA_KV * HEAD_DIM, NSA_KV * HEAD_DIM,
    H_MIX * 3,
    H_MIX * HEAD_DIM, H_MIX * HEAD_DIM, H_MIX * HEAD_DIM, H_MIX,
    H_MIX * HEAD_DIM, H_MIX * HEAD_DIM, H_MIX * HEAD_DIM,
    H_MIX * HEAD_DIM, H_MIX * HEAD_DIM, H_MIX * HEAD_DIM,
)
N_IN = sum(SPLITS)

kernel_name = "hybrid_nsa_fox_dilated_stickbreak_trunk"


def _rmsnorm(x, g):
    xf = x.astype(jnp.float32)
    y = xf * lax.rsqrt(jnp.mean(xf * xf, axis=-1, keepdims=True) + RMS_EPS)
    return (y * g.astype(jnp.float32)).astype(x.dtype)


def _rope(t, positions):
    half = HEAD_DIM // 2
    inv_freq = ROPE_THETA ** (-jnp.arange(half, dtype=jnp.float32) / half)
    ang = positions.astype(jnp.float32)[..., None] * inv_freq
    cos = jnp.cos(ang)[:, :, None, :]
    sin = jnp.sin(ang)[:, :, None, :]
    tf = t.astype(jnp.float32)
    t1, t2 = tf[..., :half], tf[..., half:]
    return jnp.concatenate([t1 * cos - t2 * sin, t2 * cos + t1 * sin], axis=-1).astype(t.dtype)


def _heads(t, n):
    return t.reshape(t.shape[0], t.shape[1], n, HEAD_DIM)


def _qblock(t, i, axis=1):
    return lax.dynamic_slice_in_dim(t, i * Q_BLOCK, Q_BLOCK, axis=axis)


def _unblock(o):
    o = jnp.moveaxis(o, 0, 1)
    return o.reshape((o.shape[0], o.shape[1] * o.shape[2]) + o.shape[3:])


def _nsa_attention(q, k_cmp, v_cmp, k_slc, v_slc, k_win, v_win, gate_logits,
                   cmp_pe_k, cmp_w1_k, cmp_w2_k, cmp_pe_v, cmp_w1_v, cmp_w2_v):
    B, S, H, D = q.shape
    G = k_cmp.shape[2]
    hpg = H // G
    scale = D ** -0.5
    qg = q.reshape(B, S, G, hpg, D)
    tpos = jnp.arange(S)

    n_cmp = (S - NSA_CMP_LEN) // NSA_CMP_STRIDE + 1
    blk = np.arange(n_cmp)[:, None] * NSA_CMP_STRIDE + np.arange(NSA_CMP_LEN)[None, :]

    def compress(t, pe, w1, w2):
        tb = jnp.take(t, blk, axis=1) + pe[None, None, :, None, :]
        tb = tb.transpose(0, 1, 3, 2, 4).reshape(B, n_cmp, G, NSA_CMP_LEN * D)
        return jax.nn.gelu(tb @ w1) @ w2

    kc = compress(k_cmp, cmp_pe_k, cmp_w1_k, cmp_w2_k)
    vc = compress(v_cmp, cmp_pe_v, cmp_w1_v, cmp_w2_v)
    visible = jnp.asarray(blk[:, -1])[None, :] <= tpos[:, None]
    s = jnp.einsum('bsghd,bcgd->bghsc', qg, kc).astype(jnp.float32) * scale
    s = jnp.where(visible, s, NEG)
    p_cmp = jnp.where(visible, jax.nn.softmax(s, axis=-1), 0.0)
    o_cmp = jnp.einsum('bghsc,bcgd->bsghd', p_cmp.astype(vc.dtype), vc).reshape(B, S, H, D)

    n_slc = S // NSA_SLC_LEN
    ratio = NSA_SLC_LEN // NSA_CMP_STRIDE
    span = NSA_CMP_LEN // NSA_CMP_STRIDE
    jj, mm, nn = np.meshgrid(np.arange(n_slc), np.arange(ratio), np.arange(span), indexing='ij')
    cc = ratio * jj + mm + nn
    keep = cc < n_cmp
    cmp_to_slc = np.zeros((n_cmp, n_slc), np.float32)
    np.add.at(cmp_to_slc, (cc[keep], jj[keep]), 1.0)
    imp = jnp.einsum('bghsc,cj->bgsj', p_cmp, jnp.asarray(cmp_to_slc))
    jblk = jnp.arange(n_slc)[None, :]
    cur = (tpos // NSA_SLC_LEN)[:, None]
    forced = (jblk == 0) | (jblk == cur) | (jblk == cur - 1)
    causal_blk = jblk * NSA_SLC_LEN <= tpos[:, None]
    score = jnp.where(causal_blk, jnp.where(forced, FORCED_SCORE, imp), -1.0)
    k_top = min(NSA_TOPK, n_slc)
    _, sel = lax.top_k(score, k_top)

    k_slc_t = k_slc.transpose(0, 2, 1, 3)
    v_slc_t = v_slc.transpose(0, 2, 1, 3)
    bi = jnp.arange(B)[:, None, None]
    gi = jnp.arange(G)[None, :, None]
    n_sel_tok = k_top * NSA_SLC_LEN
    k_win_p = jnp.pad(k_win, ((0, 0), (NSA_WINDOW, 0), (0, 0), (0, 0)))
    v_win_p = jnp.pad(v_win, ((0, 0), (NSA_WINDOW, 0), (0, 0), (0, 0)))

    def block(i):
        t0 = i * Q_BLOCK
        tq = t0 + jnp.arange(Q_BLOCK)
        qb = _qblock(qg, i)
        selb = _qblock(sel, i, axis=2)
        pos = (selb[..., None] * NSA_SLC_LEN + jnp.arange(NSA_SLC_LEN)).reshape(B, G, Q_BLOCK * n_sel_tok)
        kg = k_slc_t[bi, gi, pos].reshape(B, G, Q_BLOCK, n_sel_tok, D)
        vg = v_slc_t[bi, gi, pos].reshape(B, G, Q_BLOCK, n_sel_tok, D)
        m_sel = pos.reshape(B, G, Q_BLOCK, n_sel_tok) <= tq[None, None, :, None]
        ss = jnp.einsum('bqghd,bgqnd->bghqn', qb, kg).astype(jnp.float32) * scale
        ps = jax.nn.softmax(jnp.where(m_sel[:, :, None], ss, NEG), axis=-1)
        o_s = jnp.einsum('bghqn,bgqnd->bqghd', ps.astype(vg.dtype), vg)
        kw = lax.dynamic_slice_in_dim(k_win_p, t0, NSA_WINDOW + Q_BLOCK, axis=1)
        vw = lax.dynamic_slice_in_dim(v_win_p, t0, NSA_WINDOW + Q_BLOCK, axis=1)
        kwpos = t0 - NSA_WINDOW + jnp.arange(NSA_WINDOW + Q_BLOCK)
        dist = tq[:, None] - kwpos[None, :]
        m_win = (dist >= 0) & (dist < NSA_WINDOW) & (kwpos[None, :] >= 0)
        sw = jnp.einsum('bqghd,bkgd->bghqk', qb, kw).astype(jnp.float32) * scale
        pw = jax.nn.softmax(jnp.where(m_win, sw, NEG), axis=-1)
        o_w = jnp.einsum('bghqk,bkgd->bqghd', pw.astype(vw.dtype), vw)
        return o_s.reshape(B, Q_BLOCK, H, D), o_w.reshape(B, Q_BLOCK, H, D)

    o_slc, o_win = lax.map(block, jnp.arange(S // Q_BLOCK))
    o_slc, o_win = _unblock(o_slc), _unblock(o_win)
    g = jax.nn.sigmoid(gate_logits.astype(jnp.float32))
    out = g[..., 0:1] * o_cmp + g[..., 1:2] * o_slc + g[..., 2:3] * o_win
    return out.astype(q.dtype)


def _fox_attention(q, k, v, log_f):
    B, S, H, D = q.shape
    scale = D ** -0.5
    c = jnp.cumsum(log_f, axis=1).transpose(0, 2, 1)
    kpos = jnp.arange(S)

    def block(i):
        tq = i * Q_BLOCK + jnp.arange(Q_BLOCK)
        qb = _qblock(q, i)
        cb = _qblock(c, i, axis=2)
        s = jnp.einsum('bqhd,bkhd->bhqk', qb, k).astype(jnp.float32) * scale
        s = s + cb[..., None] - c[:, :, None, :]
        s = jnp.where(tq[:, None] >= kpos[None, :], s, NEG)
        p = jax.nn.softmax(s, axis=-1)
        return jnp.einsum('bhqk,bkhd->bqhd', p.astype(v.dtype), v)

    return _unblock(lax.map(block, jnp.arange(S // Q_BLOCK)))


def _dilated_attention(q, k, v):
    B, S, H, D = q.shape
    scale = D ** -0.5

    def block(i):
        tq = i * Q_BLOCK + jnp.arange(Q_BLOCK)
        qb = _qblock(q, i)
        lses, outs = [], []
        for window, dil in DIL_CONFIGS:
            offs = dil * jnp.arange(window // dil + 1)
            idx = tq[:, None] - offs[None, :]
            valid = idx >= 0
            idx = jnp.maximum(idx, 0)
            kg = jnp.take(k, idx, axis=1)
            vg = jnp.take(v, idx, axis=1)
            s = jnp.einsum('bqhd,bqwhd->bhqw', qb, kg).astype(jnp.float32) * scale
            s = jnp.where(valid[None, None], s, NEG)
            lse = jax.nn.logsumexp(s, axis=-1)
            p = jnp.exp(s - lse[..., None])
            outs.append(jnp.einsum('bhqw,bqwhd->bqhd', p.astype(vg.dtype), vg))
            lses.append(lse)
        w = jax.nn.softmax(jnp.stack(lses, axis=0), axis=0)
        w = w.transpose(0, 1, 3, 2)[..., None]
        o = jnp.sum(jnp.stack(outs, axis=0).astype(jnp.float32) * w, axis=0)
        return o.astype(v.dtype)

    return _unblock(lax.map(block, jnp.arange(S // Q_BLOCK)))


def _stick_breaking_attention(q, k, v):
    B, S, H, D = q.shape
    scale = D ** -0.5
    kpos = jnp.arange(S)

    def block(i):
        tq = i * Q_BLOCK + jnp.arange(Q_BLOCK)
        qb = _qblock(q, i)
        z = jnp.einsum('bqhd,bkhd->bhqk', qb, k).astype(jnp.float32) * scale
        strict = tq[:, None] > kpos[None, :]
        log1m = jnp.where(strict, jax.nn.log_sigmoid(-z), 0.0)
        after = lax.cumsum(log1m, axis=3, reverse=True) - log1m
        a = jnp.where(strict, jnp.exp(jax.nn.log_sigmoid(z) + after), 0.0)
        return jnp.einsum('bhqk,bkhd->bqhd', a.astype(v.dtype), v)

    return _unblock(lax.map(block, jnp.arange(S // Q_BLOCK)))


def _layer(h, p_i, positions, norm_attn, w_in, fox_bf, cmp_pe_k, cmp_w1_k, cmp_w2_k,
           cmp_pe_v, cmp_w1_v, cmp_w2_v, w_o, norm_mlp, w_up, w_down, norm_ple, w_ple_gate, w_ple_proj):
    B, S, _ = h.shape
    x = _rmsnorm(h, norm_attn)
    proj = x @ w_in
    (q_a, kc_a, vc_a, ks_a, vs_a, kw_a, vw_a, g_a,
     q_b, k_b, v_b, f_b,
     q_c, k_c, v_c,
     q_d, k_d, v_d) = jnp.split(proj, np.cumsum(SPLITS)[:-1].tolist(), axis=-1)

    o_a = _nsa_attention(
        _rope(_heads(q_a, H_MIX), positions),
        _heads(kc_a, NSA_KV), _heads(vc_a, NSA_KV),
        _rope(_heads(ks_a, NSA_KV), positions), _heads(vs_a, NSA_KV),
        _rope(_heads(kw_a, NSA_KV), positions), _heads(vw_a, NSA_KV),
        g_a.reshape(B, S, H_MIX, 3),
        cmp_pe_k, cmp_w1_k, cmp_w2_k, cmp_pe_v, cmp_w1_v, cmp_w2_v)
    log_f = jax.nn.log_sigmoid(f_b.astype(jnp.float32) + fox_bf.astype(jnp.float32))
    o_b = _fox_attention(_heads(q_b, H_MIX), _heads(k_b, H_MIX), _heads(v_b, H_MIX), log_f)
    o_c = _dilated_attention(_rope(_heads(q_c, H_MIX), positions),
                             _rope(_heads(k_c, H_MIX), positions), _heads(v_c, H_MIX))
    o_d = _stick_breaking_attention(_heads(q_d, H_MIX), _heads(k_d, H_MIX), _heads(v_d, H_MIX))

    mix = jnp.concatenate([o.reshape(B, S, H_MIX * HEAD_DIM) for o in (o_a, o_b, o_c, o_d)], axis=-1)
    h = h + mix @ w_o

    x2 = _rmsnorm(h, norm_mlp)
    h = h + jnp.square(jax.nn.relu(x2 @ w_up)) @ w_down

    gate = jax.nn.sigmoid(_rmsnorm(h, norm_ple) @ w_ple_gate)
    h = h + gate * (p_i @ w_ple_proj)
    return h


def setup_inputs(seed: int = 0) -> dict:
    key = jax.random.key(seed)
    ks = jax.random.split(key, 24)
    f32 = jnp.float32

    def nrm(k, shape, fan_in):
        return jax.random.normal(k, shape, f32) * (fan_in ** -0.5)

    def gain(k, shape):
        return 1.0 + 0.05 * jax.random.normal(k, shape, f32)

    L, Dh = NSA_CMP_LEN, HEAD_DIM
    positions = (jnp.arange(SEQ, dtype=jnp.int32)[None, :]
                 + jax.random.randint(ks[2], (BATCH, 1), 0, 1024, dtype=jnp.int32))
    return {
        "x": jax.random.normal(ks[0], (BATCH, SEQ, D_MODEL), f32),
        "p": jax.random.normal(ks[1], (DEPTH, BATCH, SEQ, PLE_DIM), f32),
        "positions": positions,
        "norm_attn": gain(ks[3], (DEPTH, D_MODEL)),
        "w_in": nrm(ks[4], (DEPTH, D_MODEL, N_IN), D_MODEL),
        "fox_bf": 3.0 + 0.5 * jax.random.normal(ks[5], (DEPTH, H_MIX), f32),
        "cmp_pe_k": 0.5 * jax.random.normal(ks[6], (DEPTH, L, Dh), f32),
        "cmp_w1_k": nrm(ks[7], (DEPTH, L * Dh, NSA_CMP_HIDDEN), L * Dh),
        "cmp_w2_k": nrm(ks[8], (DEPTH, NSA_CMP_HIDDEN, Dh), NSA_CMP_HIDDEN),
        "cmp_pe_v": 0.5 * jax.random.normal(ks[9], (DEPTH, L, Dh), f32),
        "cmp_w1_v": nrm(ks[10], (DEPTH, L * Dh, NSA_CMP_HIDDEN), L * Dh),
        "cmp_w2_v": nrm(ks[11], (DEPTH, NSA_CMP_HIDDEN, Dh), NSA_CMP_HIDDEN),
        "w_o": nrm(ks[12], (DEPTH, D_MIX, D_MODEL), D_MIX),
        "norm_mlp": gain(ks[13], (DEPTH, D_MODEL)),
        "w_up": nrm(ks[14], (DEPTH, D_MODEL, D_FF), D_MODEL),
        "w_down": nrm(ks[15], (DEPTH, D_FF, D_MODEL), D_FF),
        "norm_ple": gain(ks[16], (DEPTH, D_MODEL)),
        "w_ple_gate": nrm(ks[17], (DEPTH, D_MODEL, D_MODEL), D_MODEL),
        "w_ple_proj": nrm(ks[18], (DEPTH, PLE_DIM, D_MODEL), PLE_DIM),
        "norm_final": gain(ks[19], (D_MODEL,)),
    }


def reference(x, p, positions, norm_attn, w_in, fox_bf, cmp_pe_k, cmp_w1_k, cmp_w2_k,
              cmp_pe_v, cmp_w1_v, cmp_w2_v, w_o, norm_mlp, w_up, w_down, norm_ple,
              w_ple_gate, w_ple_proj, norm_final):
    h = x
    for i in range(DEPTH):
        h = _layer(h, p[i], positions, norm_attn[i], w_in[i], fox_bf[i],
                   cmp_pe_k[i], cmp_w1_k[i], cmp_w2_k[i], cmp_pe_v[i], cmp_w1_v[i], cmp_w2_v[i],
                   w_o[i], norm_mlp[i], w_up[i], w_down[i], norm_ple[i], w_ple_gate[i], w_ple_proj[i])
    return _rmsnorm(h, norm_final)
```

```python
import contextlib
import numpy as np
import concourse.bass as bass
import concourse.mybir as mybir
from concourse.bass_utils import run_bass_kernel_spmd

F32 = mybir.dt.float32
BF16 = mybir.dt.bfloat16
I32 = mybir.dt.int32
AF = mybir.ActivationFunctionType
ALU = mybir.AluOpType

CFG = dict(D=4096, S=4096, B=2, L=2, HM=8, G=2, FF=16384, PLE=256)
CMP_LEN, CMP_STRIDE, CMP_HID, SLC_LEN, TOPK, WIN = 32, 16, 256, 64, 16, 512
DIL = ((128, 1), (512, 4), (2048, 16))
EPS = 1e-6
T = 512


class Prog:
    def __init__(self, nc):
        self.nc = nc
        self.ops = []
        self.lastw = {}
        self.readers = {}

    def op(self, eng, fn, reads=(), writes=(), dma_key=None):
        i = len(self.ops)
        deps = set()
        for k in reads:
            if k in self.lastw:
                deps.add(self.lastw[k])
        for k in writes:
            if k in self.lastw:
                deps.add(self.lastw[k])
            deps.update(self.readers.get(k, ()))
        for k in reads:
            self.readers.setdefault(k, []).append(i)
        for k in writes:
            self.lastw[k] = i
            self.readers[k] = []
        self.ops.append(dict(eng=eng, fn=fn, deps=deps, dma_key=dma_key, sig=False))
        return i

    def barrier(self):
        deps = set(self.lastw.values())
        for r in self.readers.values():
            deps.update(r)
        for e in ["pe", "act", "dve", "pool", "sp"]:
            self.ops.append(dict(eng=e, fn=None, deps=set(deps), dma_key=None, sig=False))
        self.lastw.clear()
        self.readers.clear()

    def emit(self, es):
        nc = self.nc
        ops = self.ops
        for o in ops:
            pruned = set()
            for d in o["deps"]:
                od = ops[d]
                if od["dma_key"] is None and od["eng"] == "pe" and o["eng"] == "pe" and o["dma_key"] is None:
                    continue
                pruned.add(d)
                od["sig"] = True
            o["deps"] = pruned
        engs = ["pe", "act", "dve", "pool", "sp"]
        esem = {e: es.enter_context(nc.semaphore("sem_" + e)) for e in engs}
        dsem = {}
        ecnt = {e: 0 for e in engs}
        dcnt = {}
        for o in ops:
            if o["dma_key"] is not None:
                k = o["dma_key"]
                if k not in dsem:
                    dsem[k] = es.enter_context(nc.semaphore("dsem%d" % len(dsem)))
                    dcnt[k] = 0
                dcnt[k] += 16
                o["sem"], o["cnt"] = dsem[k], dcnt[k]
            elif o["sig"]:
                ecnt[o["eng"]] += 1
                o["sem"], o["cnt"] = esem[o["eng"]], ecnt[o["eng"]]
        block = es.enter_context(nc.Block())

        def run(engname, e):
            waited = {}
            for o in ops:
                if o["eng"] != engname:
                    continue
                need = {}
                for d in o["deps"]:
                    od = ops[d]
                    s = od["sem"]
                    need[id(s)] = (s, max(need.get(id(s), (s, 0))[1], od["cnt"]))
                for sid, (s, c) in need.items():
                    if waited.get(sid, 0) < c:
                        e.wait_ge(s, c)
                        waited[sid] = c
                if o["fn"] is None:
                    continue
                ins = o["fn"](e)
                if o["dma_key"] is not None:
                    ins.then_inc(o["sem"], 16)
                elif o["sig"]:
                    ins.then_inc(o["sem"], 1)

        @block.tensor
        def _(e):
            run("pe", e)

        @block.scalar
        def _(e):
            run("act", e)

        @block.vector
        def _(e):
            run("dve", e)

        @block.gpsimd
        def _(e):
            run("pool", e)

        @block.sync
        def _(e):
            run("sp", e)


def col_plan(cfg):
    HM, G = cfg["HM"], cfg["G"]
    splits = [HM * 128, G * 128, G * 128, G * 128, G * 128, G * 128, G * 128, HM * 3,
              HM * 128, HM * 128, HM * 128, HM, HM * 128, HM * 128, HM * 128, HM * 128, HM * 128, HM * 128]
    off = np.concatenate([[0], np.cumsum(splits)])
    names = ["qa", "kc", "vc", "ks", "vs", "kw", "vw", "ga", "qb", "kb", "vb", "fb", "qc", "kcc", "vcc", "qd", "kd", "vd"]
    o = {n: int(off[i]) for i, n in enumerate(names)}
    fm = []
    for n, cnt, rope in [("qa", HM, 1), ("kc", G, 0), ("vc", G, 0), ("ks", G, 1), ("kw", G, 1), ("qb", HM, 0), ("kb", HM, 0),
                         ("qc", HM, 1), ("kcc", HM, 1), ("qd", HM, 0), ("kd", HM, 0)]:
        for i in range(cnt):
            fm.append((n, i, o[n] + 128 * i, rope))
    fmidx = {(n, i): j for j, (n, i, _, _) in enumerate(fm)}
    small_cols = list(range(o["ga"], o["ga"] + HM * 3)) + list(range(o["fb"], o["fb"] + HM))
    tm = []
    for n, cnt in [("vs", G), ("vw", G), ("vb", HM), ("vcc", HM), ("vd", HM)]:
        for i in range(cnt):
            tm.append((n, i, o[n] + 128 * i))
    tmidx = {(n, i): j for j, (n, i, _) in enumerate(tm)}
    tm_cols = []
    for (_, _, c0) in tm:
        tm_cols += list(range(c0, c0 + 128))
    fcol0 = len(tm_cols)
    tm_cols += list(range(o["fb"], o["fb"] + HM))
    ntm = (len(tm_cols) + 255) // 256
    return dict(fm=fm, fmidx=fmidx, small_cols=small_cols, tm=tm, tmidx=tmidx, tm_cols=tm_cols, fcol0=fcol0,
                ntm=ntm, nfm=len(fm) + 1, nin=int(off[-1]))


def tile_w(W, ns):
    K, N = W.shape
    return np.ascontiguousarray(W.reshape(K // 128, 128, N // ns, ns).transpose(2, 1, 0, 3))


def host_consts(cfg):
    S = cfg["S"]
    c = {}
    half = 64
    inv = (10000.0 ** (-np.arange(half, dtype=np.float32) / half)).astype(np.float32)
    c["invf"] = np.concatenate([inv, inv]).reshape(128, 1).astype(np.float32)
    perm = np.zeros((128, 128), np.float32)
    for d in range(64):
        perm[d + 64, d] = -1.0
        perm[d, d + 64] = 1.0
    c["perm"] = perm
    c["ident"] = np.eye(128, dtype=np.float32)
    k = np.arange(128)[:, None]
    q = np.arange(512)[None, :]
    masks = []
    midx = {}

    def add(name, r, fn):
        d = q - k - r
        m = fn(d).astype(np.float32)
        if m.max() == 0:
            return
        if m.min() == 1 and m.max() == 1:
            midx[(name, r)] = (None, True)
            return
        midx[(name, r)] = (len(masks), False)
        masks.append(m)
    for r in range(384, -4096, -128):
        add("causal", r, lambda d: d >= 0)
        add("strict", r, lambda d: d > 0)
        add("win", r, lambda d: (d >= 0) & (d < WIN))
        add("dil", r, lambda d: sum(((d >= 0) & (d <= w) & (d % dl == 0)).astype(np.int32) for w, dl in DIL))
    n_cmp = (S - CMP_LEN) // CMP_STRIDE + 1
    ncb = (n_cmp + 127) // 128
    for cb in range(ncb):
        for qc in range(S // 512):
            cc = cb * 128 + k
            t = qc * 512 + q
            m = ((cc * CMP_STRIDE + CMP_LEN - 1 <= t) & (cc < n_cmp)).astype(np.float32)
            if m.max() == 0:
                continue
            midx[("cmp", cb, qc)] = (len(masks), False)
            masks.append(m)
    c["masks"] = np.stack(masks).astype(np.float32)
    c["midx"] = midx
    n_slc = S // SLC_LEN
    ratio, span = SLC_LEN // CMP_STRIDE, CMP_LEN // CMP_STRIDE
    c2s = np.zeros((ncb * 128, n_slc), np.float32)
    for j in range(n_slc):
        for m_ in range(ratio):
            for n_ in range(span):
                cc = ratio * j + m_ + n_
                if cc < n_cmp:
                    c2s[cc, j] += 1.0
    c["c2s"] = c2s.reshape(ncb, 128, n_slc).transpose(1, 0, 2).copy()
    tpos = np.arange(S)
    cur = tpos // SLC_LEN
    jb = np.arange(n_slc)[None, :]
    forced = (jb == 0) | (jb == cur[:, None]) | (jb == cur[:, None] - 1)
    causal = jb * SLC_LEN <= tpos[:, None]
    Am = (causal & ~forced).astype(np.float32)
    Bm = np.where(causal, np.where(forced, 1e9, 0.0), -1.0).astype(np.float32)
    c["selA"] = Am.reshape(S // 128, 128, n_slc).transpose(1, 0, 2).copy()
    c["selB"] = Bm.reshape(S // 128, 128, n_slc).transpose(1, 0, 2).copy()
    ex = np.zeros((n_slc, S), np.float32)
    ex[np.arange(S) // SLC_LEN, np.arange(S)] = 1.0
    c["expand"] = ex
    tri = (np.arange(128)[:, None] <= np.arange(128)[None, :]).astype(np.float32)
    c["triL"] = tri
    c["triU"] = (np.arange(128)[:, None] > np.arange(128)[None, :]).astype(np.float32)
    s127 = np.zeros((128, 128), np.float32)
    s127[127, :] = 1.0
    c["sel127"] = s127
    c["n_cmp"], c["ncb"], c["n_slc"] = n_cmp, ncb, n_slc
    return c


def build(cfg):
    D, S, B, L, HM, G, FF, PLE = (cfg[k] for k in ["D", "S", "B", "L", "HM", "G", "FF", "PLE"])
    NT = B * S
    DC = D // 128
    NFF = FF // 128
    FFG = min(32, NFF)
    NGRP = NFF // FFG
    PC = PLE // 128
    plan = col_plan(cfg)
    hc = host_consts(cfg)
    midx = hc["midx"]
    NFM, NTM = plan["nfm"], plan["ntm"]
    TMW = NTM * 256
    n_cmp, ncb, n_slc = hc["n_cmp"], hc["ncb"], hc["n_slc"]
    NQC = S // 512
    NKB = S // 128
    HPG = HM // G
    scale = 128 ** -0.5

    nc = bass.Bass("TRN2", target_bir_lowering=False)
    es = contextlib.ExitStack()
    P = Prog(nc)

    def din(name, shape):
        return nc.dram_tensor(name, list(shape), F32, kind="ExternalInput").ap()

    def dint(name, shape, dt=F32):
        return nc.dram_tensor(name, list(shape), dt, kind="ExternalOutput" if cfg.get("debug") else "Internal").ap()

    xT = din("xT", [DC, 128, NT])
    pT = din("pT", [L, PC, 128, NT])
    posd = nc.dram_tensor("pos", [1, NT], I32, kind="ExternalInput").ap()
    w_fm = din("w_fm", [L, NFM, 128, DC, 128])
    w_tm = din("w_tm", [L, NTM, 128, DC, 256])
    w_o = din("w_o", [L, DC, 128, DC, 128])
    w_up = din("w_up", [L, NFF, 128, DC, 128])
    w_dn = din("w_dn", [L, DC, 128, NFF, 128])
    w_pg = din("w_pg", [L, DC, 128, DC, 128])
    w_pp = din("w_pp", [L, DC, 128, PC, 128])
    gvec = din("gvec", [128, 3 * L + 1, DC])
    foxb = din("foxb", [128, L * HM])
    cw1 = din("cw1", [L, 2, 128, CMP_LEN, CMP_HID])
    cw2 = din("cw2", [L, 2, 128, 2, 128])
    cpe = din("cpe", [L, 2, 128, CMP_LEN])
    masks_d = din("masks", list(hc["masks"].shape))
    c2s_d = din("c2s", [128, ncb, n_slc])
    selA_d = din("selA", [128, S // 128, n_slc])
    selB_d = din("selB", [128, S // 128, n_slc])
    expand_d = din("expand", [n_slc, S])
    cst_d = din("cst", [6, 128, 128])
    invf_d = din("invf", [128, 1])
    yT = nc.dram_tensor("yT", [DC, 128, NT], F32, kind="ExternalOutput").ap()
    hTd = dint("hTd", [DC, 128, NT])
    mixT = dint("mixT", [DC, 128, NT])
    pfm_b = [dint("pfm%d" % b_, [NFM, 128, S]) for b_ in range(B)]
    ptm = dint("ptm", [NT, TMW])

    def sb(name, shape, dt=F32):
        return es.enter_context(nc.sbuf_tensor("s_" + name, list(shape), dt))

    ps = [es.enter_context(nc.psum_tensor("ps%d" % i, [128, 512], F32)) for i in range(8)]
    cst = sb("cst", [128, 6, 128], BF16)
    cstf = sb("cstf", [128, 6, 128], F32)
    perm_b, ident_b, triL_b, triU_b, s127_b, ones_b = (cst[:, i, :] for i in range(6))
    ident_f, triL_f, s127_f, ones_f = cstf[:, 1, :], cstf[:, 2, :], cstf[:, 4, :], cstf[:, 5, :]
    gv = sb("gv", [128, 3 * L + 1, DC])
    invf = sb("invf", [128, 1])
    ARB = 168 * 1024
    big = sb("big", [128, ARB], mybir.dt.uint8)
    stg = sb("stg", [128, 6, 512])
    rstd = sb("rstd", [128, 512])
    cos2 = sb("cos2", [128, 512])
    sin2 = sb("sin2", [128, 512])
    tmpc = sb("tmpc", [128, 512])
    sqb = sb("sqb", [128, 2, 512], BF16)
    ub = sb("ub", [128, 2, 512], BF16)
    rcol = sb("rcol", [128, 8])

    def arena(off, shape, dt):
        n = int(np.prod(shape))
        bsz = 2 if dt == BF16 else 4
        a = big[:, off:off + n * bsz].bitcast(dt)
        if len(shape) == 2:
            return a.rearrange("p (a b) -> p a b", b=shape[1])
        return a.rearrange("p (a b c) -> p a b c", b=shape[1], c=shape[2])
    o = 0
    hT = arena(o, [DC, 512], F32); o += DC * 512 * 4
    xb = arena(o, [DC, 512], BF16); o += DC * 512 * 2
    ab = arena(o, [FFG, 512], BF16); o += FFG * 512 * 2
    pTb = arena(o, [PC, 512], BF16); o += PC * 512 * 2
    slabs = big[:, o:o + 32768].bitcast(BF16).rearrange("p (a c n) -> p a c n", a=4, c=32); o += 32768
    assert o <= ARB, o

    def mm(out, lhsT, rhs, start, stop, reads, writes):
        P.op("pe", lambda e: e.matmul(out, lhsT, rhs, start=start, stop=stop), reads, writes)

    def act(out, in_, func, reads, writes, scale=None, bias=None):
        kw = {}
        if scale is not None:
            kw["scale"] = scale
        if bias is not None:
            kw["bias"] = bias
        P.op("act", lambda e: e.activation(out=out, in_=in_, func=func, **kw), reads, writes)

    def tt(eng, out, in0, in1, op, reads, writes):
        P.op(eng, lambda e: e.tensor_tensor(out=out, in0=in0, in1=in1, op=op), reads, writes)

    def ts(out, in0, s1, s2, op0, op1, reads, writes, eng="dve"):
        if op1 is None:
            P.op(eng, lambda e: e.tensor_scalar(out=out, in0=in0, scalar1=s1, scalar2=None, op0=op0), reads, writes)
        else:
            P.op(eng, lambda e: e.tensor_scalar(out=out, in0=in0, scalar1=s1, scalar2=s2, op0=op0, op1=op1), reads, writes)

    def cp(eng, out, in_, reads, writes):
        P.op(eng, lambda e: e.tensor_copy(out=out, in_=in_), reads, writes)

    def recip(out, in_, reads, writes):
        P.op("dve", lambda e: e.reciprocal(out=out, in_=in_), reads, writes)

    def mset(out, val, writes):
        P.op("dve", lambda e: e.memset(out, val), [], writes)

    def trn(out, in_, ident, reads, writes):
        P.op("pe", lambda e: e.transpose(out=out, in_=in_, identity=ident), reads, writes)

    def stt(out, in0, scalar, in1, op0, op1, reads, writes, eng="dve"):
        P.op(eng, lambda e: e.scalar_tensor_tensor(out=out, in0=in0, scalar=scalar, in1=in1, op0=op0, op1=op1), reads, writes)

    def dma(q, out, in_, reads, writes, key, **kw):
        P.op(q, lambda e: e.dma_start(out=out, in_=in_, **kw), reads, writes, dma_key=key)

    dma("pool", cst[:], cst_d.rearrange("a p b -> p a b"), [], ["cst"], "cst")
    dma("sp", cstf[:], cst_d.rearrange("a p b -> p a b"), [], ["cstf"], "cstf")
    dma("sp", gv[:], gvec, [], ["gv"], "gv")
    dma("sp", invf[:], invf_d, [], ["invf"], "invf")

    state = dict(slab=0, bank=0, stg=0)

    def next_stg():
        state["stg"] = (state["stg"] + 1) % 6
        return state["stg"]

    def gemm(slab_src, nk, rhs_fn, rhs_key, n_out, epilogue, ncols=512, extra_reads=()):
        for n in range(n_out):
            si = state["slab"]
            state["slab"] = (si + 1) % 4
            bank = state["bank"]
            state["bank"] = (bank + 1) % 3
            dma("pool", slabs[:, si, :nk, :], slab_src(n), [], [("slab", si)], ("slab", si))
            for c in range(nk):
                mm(ps[bank][:, :ncols], slabs[:, si, c, :], rhs_fn(c), c == 0, c == nk - 1,
                   [("slab", si), rhs_key(c), "cst"] + list(extra_reads), [("ps", bank)])
            epilogue(n, bank)

    SSB = 3

    def norm_pass(which):
        for c in range(DC):
            b2 = c % 2
            act(sqb[:, b2, :], hT[:, c, :], AF.Square, [("hT", c)], [("sq", b2)])
            mm(ps[SSB][:, :], ones_b, sqb[:, b2, :], c == 0, c == DC - 1, [("sq", b2), "cst"], [("ps", SSB)])
            ts(xb[:, c, :], hT[:, c, :], gv[:, which, c:c + 1], None, ALU.mult, None, [("hT", c), "gv"], [("xb", c)])
        ts(rstd[:], ps[SSB][:, :], 1.0 / D, EPS, ALU.mult, ALU.add, [("ps", SSB)], ["rstd"])
        act(rstd[:], rstd[:], AF.Sqrt, ["rstd"], ["rstd"])
        P.op("dve", lambda e: e.reciprocal(out=rstd[:], in_=rstd[:]), ["rstd"], ["rstd"])

    def hT_keys():
        return [("hT", c) for c in range(DC)]

    def load_hT(src, t0):
        for c0 in range(0, DC, 8):
            c1 = min(DC, c0 + 8)
            dma("sp", hT[:, c0:c1, :], src[c0:c1, :, t0:t0 + T].rearrange("c p t -> p c t"),
                [("dram_h", t0)], [("hT", c) for c in range(c0, c1)], ("hTld", c0))

    def store_hT(dst, t0, wkey):
        for c0 in range(0, DC, 8):
            c1 = min(DC, c0 + 8)
            dma("sp", dst[c0:c1, :, t0:t0 + T].rearrange("c p t -> p c t"), hT[:, c0:c1, :],
                [("hT", c) for c in range(c0, c1)], [wkey], ("hTst", c0))

    def phase_C(l, t0):
        for c0 in range(0, DC, 8):
            c1 = min(DC, c0 + 8)
            dma("pool", xb[:, c0:c1, :], mixT[c0:c1, :, t0:t0 + T].rearrange("c p t -> p c t"),
                [("mix", l)], [("xb", c) for c in range(c0, c1)], ("xbld", c0))

        def ep_res(n, bank):
            tt("dve", hT[:, n, :], hT[:, n, :], ps[bank][:, :], ALU.add, [("hT", n), ("ps", bank)], [("hT", n)])
        gemm(lambda n: w_o[l, n], DC, lambda c: xb[:, c, :], lambda c: ("xb", c), DC, ep_res)
        norm_pass(3 * l + 1)
        for g in range(NGRP):
            def ep_up(j, bank):
                si = next_stg()
                stt(stg[:, si, :], ps[bank][:, :], 0.0, rstd[:], ALU.max, ALU.mult, [("ps", bank), "rstd"], [("stg", si)])
                act(ab[:, j, :], stg[:, si, :], AF.Square, [("stg", si)], [("ab", j)])
            gemm(lambda j: w_up[l, g * FFG + j], DC, lambda c: xb[:, c, :], lambda c: ("xb", c), FFG, ep_up)
            gemm(lambda n: w_dn[l, n, :, g * FFG:(g + 1) * FFG, :], FFG, lambda c: ab[:, c, :], lambda c: ("ab", c), DC, ep_res)
        norm_pass(3 * l + 2)
        dma("pool", pTb[:, :, :], pT[l, :, :, t0:t0 + T].rearrange("c p t -> p c t"), [], ["pTb"], "pTb")
        for n in range(DC):
            si = state["slab"]
            state["slab"] = (si + 1) % 4
            dma("pool", slabs[:, si, :PC, :], w_pp[l, n], [], [("slab", si)], ("slab", si))
            for c in range(PC):
                mm(ps[4][:, :], slabs[:, si, c, :], pTb[:, c, :], c == 0, c == PC - 1, [("slab", si), "pTb"], [("ps", 4)])

            def ep_gate(n_, bank):
                s1, s2 = next_stg(), next_stg()
                tt("dve", stg[:, s1, :], ps[bank][:, :], rstd[:], ALU.mult, [("ps", bank), "rstd"], [("stg", s1)])
                act(stg[:, s2, :], stg[:, s1, :], AF.Sigmoid, [("stg", s1)], [("stg", s2)])
                tt("dve", stg[:, s1, :], stg[:, s2, :], ps[4][:, :], ALU.mult, [("stg", s2), ("ps", 4)], [("stg", s1)])
                tt("dve", hT[:, n, :], hT[:, n, :], stg[:, s1, :], ALU.add, [("hT", n), ("stg", s1)], [("hT", n)])
            gemm(lambda n_: w_pg[l, n], DC, lambda c: xb[:, c, :], lambda c: ("xb", c), 1, ep_gate)

    def phase_A(l, t0):
        norm_pass(3 * l + 0)
        pi = sb_pos
        dma("sp", pi[:, :], posd[0:1, t0:t0 + T].partition_broadcast(128), [], ["posi"], "posi")
        P.op("dve", lambda e: e.tensor_copy(out=tmpc[:], in_=pi[:, :]), ["posi"], ["tmpc"])
        ts(tmpc[:], tmpc[:], invf[:, 0:1], None, ALU.mult, None, ["tmpc", "invf"], ["tmpc"])
        TWO_PI = 2.0 * np.pi

        def sincos(dst, shift):
            ts(dst[:], tmpc[:], 1.0 / TWO_PI, shift, ALU.mult, ALU.add, ["tmpc"], [dst_key[id(dst)]])
            P.op("dve", lambda e: e.tensor_copy(out=pi[:, :], in_=dst[:]), [dst_key[id(dst)]], ["posi"])
            P.op("dve", lambda e: e.tensor_copy(out=stg[:, 0, :], in_=pi[:, :]), ["posi"], [("stg", 0)])
            tt("dve", dst[:], dst[:], stg[:, 0, :], ALU.subtract, [dst_key[id(dst)], ("stg", 0)], [dst_key[id(dst)]])
            ts(stg[:, 0, :], dst[:], 0.0, None, ALU.is_lt, None, [dst_key[id(dst)]], [("stg", 0)])
            tt("dve", dst[:], dst[:], stg[:, 0, :], ALU.add, [dst_key[id(dst)], ("stg", 0)], [dst_key[id(dst)]])
            ts(stg[:, 0, :], dst[:], 1.0, None, ALU.is_ge, None, [dst_key[id(dst)]], [("stg", 0)])
            tt("dve", dst[:], dst[:], stg[:, 0, :], ALU.subtract, [dst_key[id(dst)], ("stg", 0)], [dst_key[id(dst)]])
            ts(dst[:], dst[:], TWO_PI, -np.pi, ALU.mult, ALU.add, [dst_key[id(dst)]], [dst_key[id(dst)]])
            ts(dst[:], dst[:], -3.1415925, 3.1415925, ALU.max, ALU.min, [dst_key[id(dst)]], [dst_key[id(dst)]])
            act(dst[:], dst[:], AF.Sin, [dst_key[id(dst)]], [dst_key[id(dst)]])
        dst_key = {id(sin2): "sin2", id(cos2): "cos2"}
        sincos(sin2, 0.5)
        sincos(cos2, 0.75)

        fm = plan["fm"]

        def ep_fm(n, bank):
            s1 = next_stg()
            tt("dve", stg[:, s1, :], ps[bank][:, :], rstd[:], ALU.mult, [("ps", bank), "rstd"], [("stg", s1)])
            if n < len(fm) and fm[n][3]:
                b2 = n % 2
                act(ub[:, b2, :], stg[:, s1, :], AF.Copy, [("stg", s1)], [("ub", b2)])
                mm(ps[5][:, :], perm_b, ub[:, b2, :], True, True, [("ub", b2), "cst"], [("ps", 5)])
                s2 = next_stg()
                tt("dve", stg[:, s2, :], ps[5][:, :], sin2[:], ALU.mult, [("ps", 5), "sin2"], [("stg", s2)])
                tt("pool", stg[:, s1, :], stg[:, s1, :], cos2[:], ALU.mult, [("stg", s1), "cos2"], [("stg", s1)])
                tt("dve", stg[:, s1, :], stg[:, s1, :], stg[:, s2, :], ALU.add, [("stg", s1), ("stg", s2)], [("stg", s1)])
            dma("sp", pfm_b[t0 // S][n, :, t0 % S:t0 % S + T], stg[:, s1, :], [("stg", s1)], [("pfm", l)], ("stgo", s1))
        gemm(lambda n: w_fm[l, n], DC, lambda c: xb[:, c, :], lambda c: ("xb", c), NFM, ep_fm)
        for q4 in range(4):
            mm(ps[6][:, q4:q4 + 1], rstd[:, q4 * 128:(q4 + 1) * 128], cstf[:, 1, 0:1], True, True, ["rstd", "cstf"], [("ps", 6)])
        P.op("dve", lambda e: e.tensor_copy(out=rcol[:, 0:4], in_=ps[6][:, 0:4]), [("ps", 6)], ["rcol"])
        state["slab"] = (state["slab"] + 1) // 2 * 2 % 4
        for n in range(NTM):
            si = state["slab"]
            state["slab"] = (si + 2) % 4
            wv = slabs[:, si:si + 2, :, :].rearrange("p a c n -> p (a c n)").rearrange("p (c n) -> p c n", n=256)
            dma("pool", wv[:, :DC, :], w_tm[l, n], [], [("slab", si), ("slab", si + 1)], ("slab", si))
            for q4 in range(4):
                bank = state["bank"]
                state["bank"] = (bank + 1) % 3
                for c in range(DC):
                    mm(ps[bank][:, :256], xb[:, c, q4 * 128:(q4 + 1) * 128], wv[:, c, :], c == 0, c == DC - 1,
                       [("slab", si), ("slab", si + 1), ("xb", c)], [("ps", bank)])
                s1 = next_stg()
                act(stg[:, s1, :256], ps[bank][:, :256], AF.Copy, [("ps", bank), "rcol"], [("stg", s1)], scale=rcol[:, q4:q4 + 1])
                dma("sp", ptm[t0 + q4 * 128:t0 + (q4 + 1) * 128, n * 256:(n + 1) * 256], stg[:, s1, :256],
                    [("stg", s1)], [("ptm", l)], ("stgo", s1))

    sb_pos = sb("posi", [128, 512], I32)

    def phase_final(t0):
        norm_pass(3 * L)
        for c in range(DC):
            s1 = next_stg()
            stt(stg[:, s1, :], hT[:, c, :], gv[:, 3 * L, c:c + 1], rstd[:], ALU.mult, ALU.mult,
                [("hT", c), "gv", "rstd"], [("stg", s1)])
            dma("sp", yT[c, :, t0:t0 + T], stg[:, s1, :], [("stg", s1)], ["yT"], ("stgo", s1))

    o = 0
    mk_all = arena(o, [hc["masks"].shape[0], 512], BF16); o += hc["masks"].shape[0] * 1024
    qTb = arena(o, [1, S], BF16); o += S * 2
    kTb = arena(o, [2, S], BF16); o += 2 * S * 2
    vb = arena(o, [2, NKB, 128], BF16); o += 2 * S * 2
    PTb = arena(o, [4, 512], BF16); o += 4 * 1024
    msb = arena(o, [2, 512], BF16); o += 2 * 1024
    selT = arena(o, [1, S], BF16); o += S * 2
    expd = arena(o, [1, S], BF16); o += S * 2
    c2s = arena(o, [ncb, n_slc], BF16); o += ncb * n_slc * 2
    kccT = arena(o, [1, 256], BF16); o += 512
    vccb = arena(o, [ncb, 128], BF16); o += ncb * 256
    hid = arena(o, [2, 2, 256], BF16); o += 2048
    w1b = arena(o, [1, CMP_LEN, CMP_HID], BF16); o += CMP_LEN * CMP_HID * 2
    w2b = arena(o, [2, 2, 128], BF16); o += 1024
    peb = arena(o, [2, CMP_LEN], BF16); o += 2 * CMP_LEN * 2
    o = (o + 3) // 4 * 4
    selA = arena(o, [S // 128, n_slc], F32); o += S // 128 * n_slc * 4
    selB = arena(o, [S // 128, n_slc], F32); o += S // 128 * n_slc * 4
    gat = arena(o, [3, 512], F32); o += 3 * 2048
    accs = arena(o, [1, 512], F32); o += 2048
    lf = arena(o, [4, NKB], F32); o += 4 * NKB * 4
    fbias = arena(o, [1, NKB], F32); o += NKB * 4
    biaspe = arena(o, [1, 4], F32); o += 16
    sc = arena(o, [4, n_slc], F32); o += 4 * n_slc * 4
    mx8 = arena(o, [1, 16], F32); o += 64
    rsum = arena(o, [1, 512], F32); o += 2048
    foxb_s = arena(o, [1, L * HM], F32); o += L * HM * 4
    assert o <= ARB, o
    ARENA = ["arena"]

    def load_head(slot, which, idx_fm, b, l):
        buf = qTb if which == "q" else kTb
        dma("pool", buf[:, slot, :], pfm_b[b][idx_fm, :, :], [("pfm", l)], [(which, slot)], (which, slot))

    def load_v(slot, idx_tm, b, l):
        src = ptm[b * S:(b + 1) * S, idx_tm * 128:(idx_tm + 1) * 128].rearrange("(j p) d -> p j d", p=128)
        dma("pool", vb[:, slot, :, :], src, [("ptm", l)], [("v", slot)], ("v", slot))

    def finish(o_bank, d_bank, first, gate_i):
        s1 = next_stg()
        if d_bank is None:
            P.op("dve", lambda e: e.tensor_copy(out=accs[:, 0, :], in_=ps[o_bank][:, :]), [("ps", o_bank)], ["accs"])
            return
        ts(stg[:, s1, :], ps[d_bank][:, :], 1e-30, None, ALU.max, None, [("ps", d_bank)], [("stg", s1)])
        recip(stg[:, s1, :], stg[:, s1, :], [("stg", s1)], [("stg", s1)])
        if gate_i is not None:
            tt("dve", stg[:, s1, :], stg[:, s1, :], gat[:, gate_i, :], ALU.mult, [("stg", s1), "gat"], [("stg", s1)])
        if first:
            tt("dve", accs[:, 0, :], ps[o_bank][:, :], stg[:, s1, :], ALU.mult, [("ps", o_bank), ("stg", s1)], ["accs"])
        else:
            tt("dve", stg[:, s1, :], ps[o_bank][:, :], stg[:, s1, :], ALU.mult, [("ps", o_bank), ("stg", s1)], [("stg", s1)])
            tt("dve", accs[:, 0, :], accs[:, 0, :], stg[:, s1, :], ALU.add, ["accs", ("stg", s1)], ["accs"])

    def store_acc(chunk, b, qc, l):
        dma("sp", mixT[chunk, :, b * S + qc * 512: b * S + (qc + 1) * 512], accs[:, 0, :], ["accs"], [("mix", l)], "accst")

    pstate = dict(s=0, pt=0)

    def softmax_tiles(qslot, kslot, vslot, qc, tiles, o_bank, d_bank, bias_fn=None, vsrc=None, ksrc=None):
        nt = len(tiles)
        for i, (jb, mi, dyn) in enumerate(tiles):
            sbk = 6 + pstate["s"]
            pstate["s"] ^= 1
            pt = pstate["pt"]
            pstate["pt"] = (pt + 1) % 4
            lhs_k = kTb[:, kslot, jb * 128:(jb + 1) * 128] if ksrc is None else ksrc(jb)
            mm(ps[sbk][:, :], lhs_k, qTb[:, qslot, qc * 512:(qc + 1) * 512], True, True,
               [("k", kslot), ("q", qslot), "kcc"], [("ps", sbk)])
            if bias_fn is None:
                act(PTb[:, pt, :], ps[sbk][:, :], AF.Exp, [("ps", sbk)], [("pt", pt)], scale=scale)
            else:
                for q4 in range(4):
                    act(PTb[:, pt, q4 * 128:(q4 + 1) * 128], ps[sbk][:, q4 * 128:(q4 + 1) * 128], AF.Exp,
                        [("ps", sbk), "fbias"], [("pt", pt)], scale=scale, bias=bias_fn(jb, qc * 4 + q4))
            if dyn is not None:
                dyn(jb, pt)
            if mi is not None:
                tt("pool", PTb[:, pt, :], PTb[:, pt, :], mk_all[:, mi, :], ALU.mult, [("pt", pt), "masks"], [("pt", pt)])
            lhs_v = vb[:, vslot, jb, :] if vsrc is None else vsrc(jb)
            mm(ps[o_bank][:, :], lhs_v, PTb[:, pt, :], i == 0, i == nt - 1, [("pt", pt), ("v", vslot), "vcc"], [("ps", o_bank)])
            if d_bank is not None:
                mm(ps[d_bank][:, :], ones_b, PTb[:, pt, :], i == 0, i == nt - 1, [("pt", pt), "cst"], [("ps", d_bank)])

    def band_tiles(name, qc, lo_blocks):
        out = []
        for jb in range(NKB):
            r = jb * 128 - qc * 512
            if r > 384:
                break
            ent = midx.get((name, r))
            if ent is None:
                continue
            out.append((jb, None if ent[1] else ent[0], None))
        return out

    def attention(l):
        P.barrier()
        for m0 in range(0, hc["masks"].shape[0], 16):
            m1 = min(hc["masks"].shape[0], m0 + 16)
            dma("pool", mk_all[:, m0:m1, :], masks_d[m0:m1].rearrange("m p q -> p m q"), [],
                ["masks"], ("mk", m0))
        dma("pool", expd[:n_slc, 0, :], expand_d, [], ["expd"], "expd")
        dma("pool", c2s[:, :, :], c2s_d, [], ["c2s"], "c2s")
        dma("sp", selA[:, :, :], selA_d, [], ["selA"], "selA")
        dma("sp", selB[:, :, :], selB_d, [], ["selB"], "selB")
        dma("sp", foxb_s[:, 0, :], foxb, [], ["foxb"], "foxb")
        for kv in range(2):
            dma("pool", w2b[:, kv, :, :], cw2[l, kv], [], ["w2"], ("w2", kv))
        dma("pool", peb[:, :, :], cpe[l].rearrange("a p c -> p a c"), [], ["pe"], "pe")
        fmi, tmi = plan["fmidx"], plan["tmidx"]
        small = NFM - 1
        for b in range(B):
            for h in range(HM):
                load_head(0, "q", fmi[("qb", h)], b, l)
                load_head(0, "k", fmi[("kb", h)], b, l)
                load_v(0, tmi[("vb", h)], b, l)
                fcol = plan["fcol0"] + h
                src = ptm[b * S:(b + 1) * S, fcol:fcol + 1].rearrange("(j p) o -> p (j o)", p=128)
                dma("sp", lf[:, 0, :], src, [("ptm", l)], ["lf0"], "lf0", allow_slow_non_contiguous=True)
                ts(lf[:, 0, :], lf[:, 0, :], foxb_s[:, 0, l * HM + h:l * HM + h + 1], None, ALU.add, None, ["lf0", "foxb"], ["lf0"])
                ts(lf[:, 0, :], lf[:, 0, :], -1.0, None, ALU.mult, None, ["lf0"], ["lf0"])
                act(lf[:, 0, :], lf[:, 0, :], AF.Exp, ["lf0"], ["lf0"])
                act(lf[:, 0, :], lf[:, 0, :], AF.Ln, ["lf0"], ["lf0"], bias=1.0)
                ts(lf[:, 0, :], lf[:, 0, :], -1.0, None, ALU.mult, None, ["lf0"], ["lf0"])
                P.op("dve", lambda e: e.memset(lf[:, 1, 0:1], 0.0), [], ["lf1"])
                for j in range(1, NKB):
                    tt("dve", lf[:, 1, j:j + 1], lf[:, 1, j - 1:j], lf[:, 0, j - 1:j], ALU.add, ["lf1", "lf0"], ["lf1"])
                mm(ps[4][:, :NKB], triL_f, lf[:, 0, :], True, False, ["lf0", "cstf"], [("ps", 4)])
                mm(ps[4][:, :NKB], ones_f, lf[:, 1, :], False, True, ["lf1", "cstf"], [("ps", 4)])
                P.op("dve", lambda e: e.tensor_copy(out=lf[:, 2, :], in_=ps[4][:, :NKB]), [("ps", 4)], ["lf2"])
                mm(ps[4][:, :NKB], s127_f, lf[:, 2, :], True, True, ["lf2", "cstf"], [("ps", 4)])
                P.op("dve", lambda e: e.tensor_copy(out=lf[:, 3, :], in_=ps[4][:, :NKB]), [("ps", 4)], ["lf3"])
                for qc in range(NQC):
                    tiles = band_tiles("causal", qc, None)

                    def bias_fn(jb, i, _=None):
                        return fbias[:, 0, i:i + 1]
                    new_tiles = []
                    for (jb, mi, _) in tiles:
                        new_tiles.append((jb, mi, None))
                    for ti, (jb, mi, _) in enumerate(new_tiles):
                        stt(fbias[:, 0, qc * 4:qc * 4 + 4], lf[:, 3, qc * 4:qc * 4 + 4], 1.0, lf[:, 2, jb:jb + 1].to_broadcast([128, 4]),
                            ALU.mult, ALU.subtract, ["lf3", "lf2"], ["fbias"])
                        ts(fbias[:, 0, qc * 4:qc * 4 + 4], fbias[:, 0, qc * 4:qc * 4 + 4], 0.0, None, ALU.min, None, ["fbias"], ["fbias"])
                        softmax_tiles_one(0, 0, 0, qc, jb, mi, ti, len(new_tiles), 4, 5, bias_fn)
                    finish(4, 5, True, None)
                    store_acc(HM + h, b, qc, l)
            for h in range(HM):
                load_head(0, "q", fmi[("qc", h)], b, l)
                load_head(0, "k", fmi[("kcc", h)], b, l)
                load_v(0, tmi[("vcc", h)], b, l)
                for qc in range(NQC):
                    softmax_tiles(0, 0, 0, qc, band_tiles("dil", qc, None), 4, 5)
                    finish(4, 5, True, None)
                    store_acc(2 * HM + h, b, qc, l)
            for h in range(HM):
                load_head(0, "q", fmi[("qd", h)], b, l)
                load_head(0, "k", fmi[("kd", h)], b, l)
                load_v(0, tmi[("vd", h)], b, l)
                for qc in range(NQC):
                    tiles = band_tiles("strict", qc, None)[::-1]
                    nt = len(tiles)
                    for i, (jb, mi, _) in enumerate(tiles):
                        sbk = 6 + pstate["s"]
                        pstate["s"] ^= 1
                        pt = pstate["pt"]
                        pstate["pt"] = (pt + 1) % 4
                        s1, s2 = next_stg(), next_stg()
                        mm(ps[sbk][:, :], kTb[:, 0, jb * 128:(jb + 1) * 128], qTb[:, 0, qc * 512:(qc + 1) * 512], True, True,
                           [("k", 0), ("q", 0)], [("ps", sbk)])
                        act(stg[:, s1, :], ps[sbk][:, :], AF.Exp, [("ps", sbk)], [("stg", s1)], scale=scale)
                        act(stg[:, s1, :], stg[:, s1, :], AF.Ln, [("stg", s1)], [("stg", s1)], bias=1.0)
                        if mi is not None:
                            tt("pool", stg[:, s1, :], stg[:, s1, :], mk_f(mi), ALU.mult, [("stg", s1), "masks"], [("stg", s1)])
                        act(PTb[:, pt, :], stg[:, s1, :], AF.Copy, [("stg", s1)], [("pt", pt)])
                        mm(ps[5][:, :], triU_b, PTb[:, pt, :], True, i == 0, [("pt", pt), "cst"], [("ps", 5)])
                        if i > 0:
                            mm(ps[5][:, :], ones_b, ub[:, 0, :], False, True, [("ub", 0), "cst"], [("ps", 5)])
                        stt(stg[:, s2, :], ps[sbk][:, :], scale, stg[:, s1, :], ALU.mult, ALU.subtract, [("ps", sbk), ("stg", s1)], [("stg", s2)])
                        tt("dve", stg[:, s2, :], stg[:, s2, :], ps[5][:, :], ALU.subtract, [("stg", s2), ("ps", 5)], [("stg", s2)])
                        if i == 0:
                            cp("pool", rsum[:, 0, :], stg[:, s1, :], [("stg", s1)], ["rsum"])
                        else:
                            tt("pool", rsum[:, 0, :], rsum[:, 0, :], stg[:, s1, :], ALU.add, ["rsum", ("stg", s1)], ["rsum"])
                        P.op("pool", lambda e: e.tensor_copy(out=ub[:, 0, :], in_=rsum[:, 0, :]), ["rsum"], [("ub", 0)])
                        pt2 = pstate["pt"]
                        pstate["pt"] = (pt2 + 1) % 4
                        act(PTb[:, pt2, :], stg[:, s2, :], AF.Exp, [("stg", s2)], [("pt", pt2)])
                        if mi is not None:
                            tt("pool", PTb[:, pt2, :], PTb[:, pt2, :], mk_all[:, mi, :], ALU.mult, [("pt", pt2), "masks"], [("pt", pt2)])
                        mm(ps[4][:, :], vb[:, 0, jb, :], PTb[:, pt2, :], i == 0, i == nt - 1, [("pt", pt2), ("v", 0)], [("ps", 4)])
                    if nt == 0:
                        P.op("dve", lambda e: e.memset(accs[:, 0, :], 0.0), [], ["accs"])
                    else:
                        finish(4, None, True, None)
                    store_acc(3 * HM + h, b, qc, l)
            for g in range(G):
                nsa_group(l, b, g)

    def mk_f(mi):
        return mk_all[:, mi, :]

    def softmax_tiles_one(qslot, kslot, vslot, qc, jb, mi, i, nt, o_bank, d_bank, bias_fn):
        sbk = 6 + pstate["s"]
        pstate["s"] ^= 1
        pt = pstate["pt"]
        pstate["pt"] = (pt + 1) % 4
        mm(ps[sbk][:, :], kTb[:, kslot, jb * 128:(jb + 1) * 128], qTb[:, qslot, qc * 512:(qc + 1) * 512], True, True,
           [("k", kslot), ("q", qslot)], [("ps", sbk)])
        for q4 in range(4):
            act(PTb[:, pt, q4 * 128:(q4 + 1) * 128], ps[sbk][:, q4 * 128:(q4 + 1) * 128], AF.Exp,
                [("ps", sbk), "fbias"], [("pt", pt)], scale=scale, bias=bias_fn(jb, qc * 4 + q4))
        if mi is not None:
            tt("pool", PTb[:, pt, :], PTb[:, pt, :], mk_all[:, mi, :], ALU.mult, [("pt", pt), "masks"], [("pt", pt)])
        mm(ps[o_bank][:, :], vb[:, vslot, jb, :], PTb[:, pt, :], i == 0, i == nt - 1, [("pt", pt), ("v", vslot)], [("ps", o_bank)])
        mm(ps[d_bank][:, :], ones_b, PTb[:, pt, :], i == 0, i == nt - 1, [("pt", pt), "cst"], [("ps", d_bank)])

    def nsa_group(l, b, g):
        fmi, tmi = plan["fmidx"], plan["tmidx"]
        small = NFM - 1
        for kv in range(2):
            load_head(1, "k", fmi[("kc" if kv == 0 else "vc", g)], b, l)
            dma("pool", w1b[:, 0, :, :], cw1[l, kv], [], [("w1", 0)], ("w1", 0))
            for ht in range(2):
                for li in range(CMP_LEN):
                    mm(ps[4][:, ht:ht + 1], w1b[:, 0, li, ht * 128:(ht + 1) * 128], peb[:, kv, li:li + 1], li == 0, li == CMP_LEN - 1,
                       [("w1", 0), "pe"], [("ps", 4)])
            P.op("dve", lambda e: e.tensor_copy(out=biaspe[:, 0, 0:2], in_=ps[4][:, 0:2]), [("ps", 4)], ["biaspe"])
            for ht in range(2):
                for li in range(CMP_LEN):
                    rhs = kTb[:, 1, li:li + CMP_STRIDE * (n_cmp - 1) + 1:CMP_STRIDE]
                    mm(ps[5][:, :n_cmp], w1b[:, 0, li, ht * 128:(ht + 1) * 128], rhs, li == 0, li == CMP_LEN - 1,
                       [("w1", 0), ("k", 1)], [("ps", 5)])
                s1, s2 = next_stg(), next_stg()
                ts(stg[:, s1, :n_cmp], ps[5][:, :n_cmp], biaspe[:, 0, ht:ht + 1], None, ALU.add, None, [("ps", 5), "biaspe"], [("stg", s1)])
                tt("dve", stg[:, s2, :n_cmp], stg[:, s1, :n_cmp], stg[:, s1, :n_cmp], ALU.mult, [("stg", s1)], [("stg", s2)])
                ts(stg[:, s2, :n_cmp], stg[:, s2, :n_cmp], 0.044715, 1.0, ALU.mult, ALU.add, [("stg", s2)], [("stg", s2)])
                tt("dve", stg[:, s2, :n_cmp], stg[:, s2, :n_cmp], stg[:, s1, :n_cmp], ALU.mult, [("stg", s2), ("stg", s1)], [("stg", s2)])
                act(stg[:, s2, :n_cmp], stg[:, s2, :n_cmp], AF.Sigmoid, [("stg", s2)], [("stg", s2)], scale=1.5957691216)
                mset(hid[:, kv, ht, :], 0.0, [("hid", kv)])
                tt("dve", hid[:, kv, ht, :n_cmp], stg[:, s2, :n_cmp], stg[:, s1, :n_cmp], ALU.mult, [("stg", s2), ("stg", s1)], [("hid", kv)])
            if kv == 0:
                for ht in range(2):
                    mm(ps[5][:, :256], w2b[:, 0, ht, :], hid[:, 0, ht, :], ht == 0, ht == 1, [("hid", 0), "w2"], [("ps", 5)])
                P.op("dve", lambda e: e.tensor_copy(out=kccT[:, 0, :], in_=ps[5][:, :256]), [("ps", 5)], ["kcc"])
            else:
                for cb in range(ncb):
                    for ht in range(2):
                        mm(ps[5][:, :128], hid[:, 1, ht, cb * 128:(cb + 1) * 128], w2b[:, 1, ht, :], ht == 0, ht == 1,
                           [("hid", 1), "w2"], [("ps", 5)])
                    cp("dve", vccb[:, cb, :], ps[5][:, :128], [("ps", 5)], ["vcc"])

        def cmp_tiles(qc):
            out = []
            for cb in range(ncb):
                ent = midx.get(("cmp", cb, qc))
                if ent is not None:
                    out.append((cb, ent[0], None))
            return out
        for hh in range(HPG):
            load_head(hh % 2, "q", fmi[("qa", g * HPG + hh)], b, l) if False else None
        for qc in range(NQC):
            tl = cmp_tiles(qc)
            first_imp = True
            for hh in range(HPG):
                h = g * HPG + hh
                if qc == 0 or True:
                    pass
                qs = 0
                load_q_chunk(fmi[("qa", h)], b, l, qc)
                if not tl:
                    continue
                pts = []
                for i, (cb, mi, _) in enumerate(tl):
                    sbk = 6 + pstate["s"]
                    pstate["s"] ^= 1
                    pt = pstate["pt"]
                    pstate["pt"] = (pt + 1) % 4
                    pts.append(pt)
                    mm(ps[sbk][:, :], kccT[:, 0, cb * 128:(cb + 1) * 128], qchunk[:, 0, :], True, True, ["kcc", "qchunk"], [("ps", sbk)])
                    act(PTb[:, pt, :], ps[sbk][:, :], AF.Exp, [("ps", sbk)], [("pt", pt)], scale=scale)
                    tt("pool", PTb[:, pt, :], PTb[:, pt, :], mk_all[:, mi, :], ALU.mult, [("pt", pt), "masks"], [("pt", pt)])
                    mm(ps[5][:, :], ones_b, PTb[:, pt, :], i == 0, i == len(tl) - 1, [("pt", pt), "cst"], [("ps", 5)])
                s1 = next_stg()
                ts(stg[:, s1, :], ps[5][:, :], 1e-30, None, ALU.max, None, [("ps", 5)], [("stg", s1)])
                recip(stg[:, s1, :], stg[:, s1, :], [("stg", s1)], [("stg", s1)])
                for i, (cb, mi, _) in enumerate(tl):
                    pt = pts[i]
                    tt("dve", PTb[:, pt, :], PTb[:, pt, :], stg[:, s1, :], ALU.mult, [("pt", pt), ("stg", s1)], [("pt", pt)])
                    last = (hh == HPG - 1) and (i == len(tl) - 1)
                    mm(ps[4][:n_slc, :], c2s[:, cb, :], PTb[:, pt, :], first_imp, last, [("pt", pt), "c2s"], [("ps", 4)])
                    first_imp = False
            s1 = next_stg()
            if tl:
                cp("dve", stg[:n_slc, s1, :], ps[4][:n_slc, :], [("ps", 4)], [("stg", s1)])
            else:
                mset(stg[:n_slc, s1, :], 0.0, [("stg", s1)])
            for q4 in range(4):
                qb = qc * 4 + q4
                trn(ps[5][:, :n_slc], stg[:n_slc, s1, q4 * 128:(q4 + 1) * 128], ident_f[:n_slc, :n_slc], [("stg", s1), "cstf"], [("ps", 5)])
                tt("dve", sc[:, 0, :], ps[5][:, :n_slc], selA[:, qb, :], ALU.mult, [("ps", 5), "selA"], ["sc0"])
                tt("dve", sc[:, 0, :], sc[:, 0, :], selB[:, qb, :], ALU.add, ["sc0", "selB"], ["sc0"])
                P.op("dve", lambda e: e.max(out=mx8[:, 0, 0:8], in_=sc[:, 0, :]), ["sc0"], ["mx8"])
                P.op("dve", lambda e: e.match_replace(out=sc[:, 1, :], in_to_replace=mx8[:, 0, 0:8], in_values=sc[:, 0, :], imm_value=-1e30), ["mx8", "sc0"], ["sc1"])
                P.op("dve", lambda e: e.max(out=mx8[:, 0, 8:16], in_=sc[:, 1, :]), ["sc1"], ["mx8"])
                P.op("dve", lambda e: e.match_replace(out=sc[:, 2, :], in_to_replace=mx8[:, 0, 8:16], in_values=sc[:, 1, :], imm_value=-1e30), ["mx8", "sc1"], ["sc2"])
                tt("dve", sc[:, 3, :], sc[:, 0, :], sc[:, 2, :], ALU.not_equal, ["sc0", "sc2"], ["sc3"])
                trn(ps[5][:n_slc, 128:256], sc[:, 3, :], ident_f, ["sc3", "cstf"], [("ps", 5)])
                cp("dve", selT[:n_slc, 0, qb * 128:(qb + 1) * 128], ps[5][:n_slc, 128:256], [("ps", 5)], ["selT"])
        load_head(1, "k", fmi[("ks", g)], b, l)
        load_v(1, tmi[("vs", g)], b, l)
        load_head(0, "k", fmi[("kw", g)], b, l)
        load_v(0, tmi[("vw", g)], b, l)
        for hh in range(HPG):
            h = g * HPG + hh
            load_head(0, "q", fmi[("qa", h)], b, l)
            for qc in range(NQC):
                for br in range(3):
                    dma("sp", gat[:, br, :], pfm_b[b][small, 3 * h + br:3 * h + br + 1, qc * 512:(qc + 1) * 512].partition_broadcast(128),
                        [("pfm", l)], ["gat"], ("gat", br))
                act(gat[:, :, :], gat[:, :, :], AF.Sigmoid, ["gat"], ["gat"])
                tl = cmp_tiles(qc)
                if tl:
                    softmax_tiles(0, 0, 0, qc, tl, 4, 5, ksrc=lambda cb: kccT[:, 0, cb * 128:(cb + 1) * 128], vsrc=lambda cb: vccb[:, cb, :])
                    finish(4, 5, True, 0)
                else:
                    P.op("dve", lambda e: e.memset(accs[:, 0, :], 0.0), [], ["accs"])

                def dyn(jb, pt, qc=qc):
                    mm(ps[3][:, :], expd[:n_slc, 0, jb * 128:(jb + 1) * 128], selT[:n_slc, 0, qc * 512:(qc + 1) * 512], True, True,
                       ["expd", "selT"], [("ps", 3)])
                    tt("dve", PTb[:, pt, :], PTb[:, pt, :], ps[3][:, :], ALU.mult, [("pt", pt), ("ps", 3)], [("pt", pt)])
                tiles = [(jb, mi, dyn) for (jb, mi, _) in band_tiles("causal", qc, None)]
                softmax_tiles(0, 1, 1, qc, tiles, 4, 5)
                finish(4, 5, False, 1)
                softmax_tiles(0, 0, 0, qc, band_tiles("win", qc, None), 4, 5)
                finish(4, 5, False, 2)
                store_acc(h, b, qc, l)

    qchunk = sb("qchunk", [128, 1, 512], BF16)

    def load_q_chunk(idx_fm, b, l, qc):
        dma("pool", qchunk[:, 0, :], pfm_b[b][idx_fm, :, qc * 512:(qc + 1) * 512], [("pfm", l)], ["qchunk"], "qchunk")

    def dense_guard():
        P.barrier()

    for l in range(L + 1):
        for t0 in range(0, NT, T):
            if l == 0:
                load_hT(xT, t0)
            else:
                load_hT(hTd, t0)
                phase_C(l - 1, t0)
            if l < L:
                phase_A(l, t0)
                store_hT(hTd, t0, ("dram_h", t0))
            else:
                phase_final(t0)
        if l < L:
            attention(l)
            dense_guard()
    P.barrier()
    P.emit(es)
    es.close()
    return nc, hc, plan


def prep_inputs(cfg, hc, plan, x, p, positions, norm_attn, w_in, fox_bf, cmp_pe_k, cmp_w1_k, cmp_w2_k,
                cmp_pe_v, cmp_w1_v, cmp_w2_v, w_o, norm_mlp, w_up, w_down, norm_ple, w_ple_gate, w_ple_proj, norm_final):
    D, S, B, L, HM, G, FF, PLE = (cfg[k] for k in ["D", "S", "B", "L", "HM", "G", "FF", "PLE"])
    NT, DC = B * S, D // 128
    f = lambda a: np.ascontiguousarray(np.asarray(a, dtype=np.float32))
    m = {}
    m["xT"] = f(np.asarray(x).reshape(NT, DC, 128).transpose(1, 2, 0))
    m["pT"] = f(np.asarray(p).reshape(L, NT, PLE // 128, 128).transpose(0, 2, 3, 1))
    m["pos"] = np.ascontiguousarray(np.asarray(positions).reshape(1, NT).astype(np.int32))
    w_in = np.asarray(w_in)
    fm_cols = []
    for (_, _, c0, _) in plan["fm"]:
        fm_cols += list(range(c0, c0 + 128))
    sm = plan["small_cols"]
    wfm, wtm = [], []
    for l in range(L):
        Wf = np.zeros((D, plan["nfm"] * 128), np.float32)
        Wf[:, :len(fm_cols)] = w_in[l][:, fm_cols]
        Wf[:, len(fm_cols):len(fm_cols) + len(sm)] = w_in[l][:, sm]
        wfm.append(tile_w(Wf, 128))
        Wt = np.zeros((D, plan["ntm"] * 256), np.float32)
        Wt[:, :len(plan["tm_cols"])] = w_in[l][:, plan["tm_cols"]]
        wtm.append(tile_w(Wt, 256))
    m["w_fm"] = np.stack(wfm)
    m["w_tm"] = np.stack(wtm)
    m["w_o"] = np.stack([tile_w(np.asarray(w_o[l]), 128) for l in range(L)])
    m["w_up"] = np.stack([tile_w(np.asarray(w_up[l]), 128) for l in range(L)])
    m["w_dn"] = np.stack([tile_w(np.asarray(w_down[l]), 128) for l in range(L)])
    m["w_pg"] = np.stack([tile_w(np.asarray(w_ple_gate[l]), 128) for l in range(L)])
    m["w_pp"] = np.stack([tile_w(np.asarray(w_ple_proj[l]), 128) for l in range(L)])
    gv = np.zeros((3 * L + 1, D), np.float32)
    for l in range(L):
        gv[3 * l], gv[3 * l + 1], gv[3 * l + 2] = np.asarray(norm_attn[l]), np.asarray(norm_mlp[l]), np.asarray(norm_ple[l])
    gv[3 * L] = np.asarray(norm_final)
    m["gvec"] = f(gv.reshape(3 * L + 1, DC, 128).transpose(2, 0, 1))
    m["foxb"] = f(np.broadcast_to(np.asarray(fox_bf).reshape(1, L * HM), (128, L * HM)))
    w1 = np.stack([np.stack([np.asarray(cmp_w1_k[l]), np.asarray(cmp_w1_v[l])]) for l in range(L)])
    m["cw1"] = f(w1.reshape(L, 2, CMP_LEN, 128, CMP_HID).transpose(0, 1, 3, 2, 4))
    w2 = np.stack([np.stack([np.asarray(cmp_w2_k[l]), np.asarray(cmp_w2_v[l])]) for l in range(L)])
    m["cw2"] = f(w2.reshape(L, 2, 2, 128, 128).transpose(0, 1, 3, 2, 4))
    pe = np.stack([np.stack([np.asarray(cmp_pe_k[l]), np.asarray(cmp_pe_v[l])]) for l in range(L)])
    m["cpe"] = f(pe.transpose(0, 1, 3, 2))
    m["masks"] = f(hc["masks"])
    m["c2s"] = f(hc["c2s"])
    m["selA"] = f(hc["selA"])
    m["selB"] = f(hc["selB"])
    m["expand"] = f(hc["expand"])
    m["cst"] = f(np.stack([hc["perm"], hc["ident"], hc["triL"], hc["triU"], hc["sel127"], np.ones((128, 128), np.float32)]))
    m["invf"] = f(hc["invf"])
    return m


_CACHE = {}


def kernel(**inputs):
    full = CFG
    cfg = dict(full)
    cfg["B"] = 1
    key = tuple(sorted(cfg.items()))
    if key not in _CACHE:
        _CACHE[key] = build(cfg)
    nc, hc, plan = _CACHE[key]
    NB = full["B"]
    shared = None
    maps = []
    for b in range(NB):
        sub = dict(inputs)
        sub["x"] = np.asarray(inputs["x"])[b:b + 1]
        sub["p"] = np.asarray(inputs["p"])[:, b:b + 1]
        sub["positions"] = np.asarray(inputs["positions"])[b:b + 1]
        if shared is None:
            m = prep_inputs(cfg, hc, plan, **sub)
            shared = m
        else:
            m = dict(shared)
            D, S, L, PLE = cfg["D"], cfg["S"], cfg["L"], cfg["PLE"]
            m["xT"] = np.ascontiguousarray(sub["x"].reshape(S, D // 128, 128).transpose(1, 2, 0).astype(np.float32))
            m["pT"] = np.ascontiguousarray(sub["p"].reshape(L, S, PLE // 128, 128).transpose(0, 2, 3, 1).astype(np.float32))
            m["pos"] = np.ascontiguousarray(sub["positions"].reshape(1, S).astype(np.int32))
        maps.append(m)
    res = run_bass_kernel_spmd(nc, maps, core_ids=list(range(NB)))
    D, S = cfg["D"], cfg["S"]
    outs = []
    for b in range(NB):
        yT = res.results[b]["yT"]
        outs.append(np.ascontiguousarray(yT.reshape(D, S).T).reshape(1, S, D))
        if cfg.get("debug"):
            global DEBUG_OUT
            DEBUG_OUT = res.results[b]
    return np.concatenate(outs, axis=0).astype(np.float32)
```

```python
import contextlib
import numpy as np
import concourse.bass as bass
import concourse.mybir as mybir
from concourse.bass_utils import run_bass_kernel_spmd

F32 = mybir.dt.float32
BF16 = mybir.dt.bfloat16
I32 = mybir.dt.int32
AF = mybir.ActivationFunctionType
ALU = mybir.AluOpType

CFG = dict(D=4096, S=4096, B=2, L=2, HM=8, G=2, FF=16384, PLE=256)
CMP_LEN, CMP_STRIDE, CMP_HID, SLC_LEN, TOPK, WIN = 32, 16, 256, 64, 16, 512
DIL = ((128, 1), (512, 4), (2048, 16))
EPS = 1e-6
T = 512


class Prog:
    def __init__(self, nc):
        self.nc = nc
        self.ops = []
        self.lastw = {}
        self.readers = {}

    def op(self, eng, fn, reads=(), writes=(), dma_key=None):
        i = len(self.ops)
        deps = set()
        for k in reads:
            if k in self.lastw:
                deps.add(self.lastw[k])
        for k in writes:
            if k in self.lastw:
                deps.add(self.lastw[k])
            deps.update(self.readers.get(k, ()))
        for k in reads:
            self.readers.setdefault(k, []).append(i)
        for k in writes:
            self.lastw[k] = i
            self.readers[k] = []
        self.ops.append(dict(eng=eng, fn=fn, deps=deps, dma_key=dma_key, sig=False))
        return i

    def barrier(self):
        deps = set(self.lastw.values())
        for r in self.readers.values():
            deps.update(r)
        for e in ["pe", "act", "dve", "pool", "sp"]:
            self.ops.append(dict(eng=e, fn=None, deps=set(deps), dma_key=None, sig=False))
        self.lastw.clear()
        self.readers.clear()

    def emit(self, es):
        nc = self.nc
        ops = self.ops
        for o in ops:
            pruned = set()
            for d in o["deps"]:
                od = ops[d]
                if od["dma_key"] is None and od["eng"] == "pe" and o["eng"] == "pe" and o["dma_key"] is None:
                    continue
                pruned.add(d)
                od["sig"] = True
            o["deps"] = pruned
        engs = ["pe", "act", "dve", "pool", "sp"]
        esem = {e: es.enter_context(nc.semaphore("sem_" + e)) for e in engs}
        dsem = {}
        ecnt = {e: 0 for e in engs}
        dcnt = {}
        for o in ops:
            if o["dma_key"] is not None:
                k = o["dma_key"]
                if k not in dsem:
                    dsem[k] = es.enter_context(nc.semaphore("dsem%d" % len(dsem)))
                    dcnt[k] = 0
                dcnt[k] += 16
                o["sem"], o["cnt"] = dsem[k], dcnt[k]
            elif o["sig"]:
                ecnt[o["eng"]] += 1
                o["sem"], o["cnt"] = esem[o["eng"]], ecnt[o["eng"]]
        block = es.enter_context(nc.Block())

        def run(engname, e):
            waited = {}
            for o in ops:
                if o["eng"] != engname:
                    continue
                need = {}
                for d in o["deps"]:
                    od = ops[d]
                    s = od["sem"]
                    need[id(s)] = (s, max(need.get(id(s), (s, 0))[1], od["cnt"]))
                for sid, (s, c) in need.items():
                    if waited.get(sid, 0) < c:
                        e.wait_ge(s, c)
                        waited[sid] = c
                if o["fn"] is None:
                    continue
                ins = o["fn"](e)
                if o["dma_key"] is not None:
                    ins.then_inc(o["sem"], 16)
                elif o["sig"]:
                    ins.then_inc(o["sem"], 1)

        @block.tensor
        def _(e):
            run("pe", e)

        @block.scalar
        def _(e):
            run("act", e)

        @block.vector
        def _(e):
            run("dve", e)

        @block.gpsimd
        def _(e):
            run("pool", e)

        @block.sync
        def _(e):
            run("sp", e)


def col_plan(cfg):
    HM, G = cfg["HM"], cfg["G"]
    splits = [HM * 128, G * 128, G * 128, G * 128, G * 128, G * 128, G * 128, HM * 3,
              HM * 128, HM * 128, HM * 128, HM, HM * 128, HM * 128, HM * 128, HM * 128, HM * 128, HM * 128]
    off = np.concatenate([[0], np.cumsum(splits)])
    names = ["qa", "kc", "vc", "ks", "vs", "kw", "vw", "ga", "qb", "kb", "vb", "fb", "qc", "kcc", "vcc", "qd", "kd", "vd"]
    o = {n: int(off[i]) for i, n in enumerate(names)}
    fm = []
    for n, cnt, rope in [("qa", HM, 1), ("kc", G, 0), ("vc", G, 0), ("ks", G, 1), ("kw", G, 1), ("qb", HM, 0), ("kb", HM, 0),
                         ("qc", HM, 1), ("kcc", HM, 1), ("qd", HM, 0), ("kd", HM, 0)]:
        for i in range(cnt):
            fm.append((n, i, o[n] + 128 * i, rope))
    fmidx = {(n, i): j for j, (n, i, _, _) in enumerate(fm)}
    small_cols = list(range(o["ga"], o["ga"] + HM * 3)) + list(range(o["fb"], o["fb"] + HM))
    tm = []
    for n, cnt in [("vs", G), ("vw", G), ("vb", HM), ("vcc", HM), ("vd", HM)]:
        for i in range(cnt):
            tm.append((n, i, o[n] + 128 * i))
    tmidx = {(n, i): j for j, (n, i, _) in enumerate(tm)}
    tm_cols = []
    for (_, _, c0) in tm:
        tm_cols += list(range(c0, c0 + 128))
    fcol0 = len(tm_cols)
    tm_cols += list(range(o["fb"], o["fb"] + HM))
    ntm = (len(tm_cols) + 255) // 256
    return dict(fm=fm, fmidx=fmidx, small_cols=small_cols, tm=tm, tmidx=tmidx, tm_cols=tm_cols, fcol0=fcol0,
                ntm=ntm, nfm=len(fm) + 1, nin=int(off[-1]))


def tile_w(W, ns):
    K, N = W.shape
    return np.ascontiguousarray(W.reshape(K // 128, 128, N // ns, ns).transpose(2, 1, 0, 3))


def host_consts(cfg):
    S = cfg["S"]
    c = {}
    half = 64
    inv = (10000.0 ** (-np.arange(half, dtype=np.float32) / half)).astype(np.float32)
    c["invf"] = np.concatenate([inv, inv]).reshape(128, 1).astype(np.float32)
    perm = np.zeros((128, 128), np.float32)
    for d in range(64):
        perm[d + 64, d] = -1.0
        perm[d, d + 64] = 1.0
    c["perm"] = perm
    c["ident"] = np.eye(128, dtype=np.float32)
    k = np.arange(128)[:, None]
    q = np.arange(512)[None, :]
    masks = []
    midx = {}

    def add(name, r, fn):
        d = q - k - r
        m = fn(d).astype(np.float32)
        if m.max() == 0:
            return
        if m.min() == 1 and m.max() == 1:
            midx[(name, r)] = (None, True)
            return
        midx[(name, r)] = (len(masks), False)
        masks.append(m)
    for r in range(384, -4096, -128):
        add("causal", r, lambda d: d >= 0)
        add("strict", r, lambda d: d > 0)
        add("win", r, lambda d: (d >= 0) & (d < WIN))
        add("dil", r, lambda d: sum(((d >= 0) & (d <= w) & (d % dl == 0)).astype(np.int32) for w, dl in DIL))
    n_cmp = (S - CMP_LEN) // CMP_STRIDE + 1
    ncb = (n_cmp + 127) // 128
    for cb in range(ncb):
        for qc in range(S // 512):
            cc = cb * 128 + k
            t = qc * 512 + q
            m = ((cc * CMP_STRIDE + CMP_LEN - 1 <= t) & (cc < n_cmp)).astype(np.float32)
            if m.max() == 0:
                continue
            midx[("cmp", cb, qc)] = (len(masks), False)
            masks.append(m)
    c["masks"] = np.stack(masks).astype(np.float32)
    c["midx"] = midx
    n_slc = S // SLC_LEN
    ratio, span = SLC_LEN // CMP_STRIDE, CMP_LEN // CMP_STRIDE
    c2s = np.zeros((ncb * 128, n_slc), np.float32)
    for j in range(n_slc):
        for m_ in range(ratio):
            for n_ in range(span):
                cc = ratio * j + m_ + n_
                if cc < n_cmp:
                    c2s[cc, j] += 1.0
    c["c2s"] = c2s.reshape(ncb, 128, n_slc).transpose(1, 0, 2).copy()
    tpos = np.arange(S)
    cur = tpos // SLC_LEN
    jb = np.arange(n_slc)[None, :]
    forced = (jb == 0) | (jb == cur[:, None]) | (jb == cur[:, None] - 1)
    causal = jb * SLC_LEN <= tpos[:, None]
    Am = (causal & ~forced).astype(np.float32)
    Bm = np.where(causal, np.where(forced, 1e9, 0.0), -1.0).astype(np.float32)
    c["selA"] = Am.reshape(S // 128, 128, n_slc).transpose(1, 0, 2).copy()
    c["selB"] = Bm.reshape(S // 128, 128, n_slc).transpose(1, 0, 2).copy()
    ex = np.zeros((n_slc, S), np.float32)
    ex[np.arange(S) // SLC_LEN, np.arange(S)] = 1.0
    c["expand"] = ex
    tri = (np.arange(128)[:, None] <= np.arange(128)[None, :]).astype(np.float32)
    c["triL"] = tri
    c["triU"] = (np.arange(128)[:, None] > np.arange(128)[None, :]).astype(np.float32)
    s127 = np.zeros((128, 128), np.float32)
    s127[127, :] = 1.0
    c["sel127"] = s127
    c["n_cmp"], c["ncb"], c["n_slc"] = n_cmp, ncb, n_slc
    return c


def build(cfg):
    D, S, B, L, HM, G, FF, PLE = (cfg[k] for k in ["D", "S", "B", "L", "HM", "G", "FF", "PLE"])
    NT = B * S
    DC = D // 128
    NFF = FF // 128
    FFG = min(32, NFF)
    NGRP = NFF // FFG
    PC = PLE // 128
    plan = col_plan(cfg)
    hc = host_consts(cfg)
    midx = hc["midx"]
    NFM, NTM = plan["nfm"], plan["ntm"]
    TMW = NTM * 256
    n_cmp, ncb, n_slc = hc["n_cmp"], hc["ncb"], hc["n_slc"]
    NQC = S // 512
    NKB = S // 128
    HPG = HM // G
    scale = 128 ** -0.5

    nc = bass.Bass("TRN2", target_bir_lowering=False)
    es = contextlib.ExitStack()
    P = Prog(nc)

    def din(name, shape):
        return nc.dram_tensor(name, list(shape), F32, kind="ExternalInput").ap()

    def dint(name, shape, dt=F32):
        return nc.dram_tensor(name, list(shape), dt, kind="ExternalOutput" if cfg.get("debug") else "Internal").ap()

    xT = din("xT", [DC, 128, NT])
    pT = din("pT", [L, PC, 128, NT])
    posd = nc.dram_tensor("pos", [1, NT], I32, kind="ExternalInput").ap()
    w_fm = din("w_fm", [L, NFM, 128, DC, 128])
    w_tm = din("w_tm", [L, NTM, 128, DC, 256])
    w_o = din("w_o", [L, DC, 128, DC, 128])
    w_up = din("w_up", [L, NFF, 128, DC, 128])
    w_dn = din("w_dn", [L, DC, 128, NFF, 128])
    w_pg = din("w_pg", [L, DC, 128, DC, 128])
    w_pp = din("w_pp", [L, DC, 128, PC, 128])
    gvec = din("gvec", [128, 3 * L + 1, DC])
    foxb = din("foxb", [128, L * HM])
    cw1 = din("cw1", [L, 2, 128, CMP_LEN, CMP_HID])
    cw2 = din("cw2", [L, 2, 128, 2, 128])
    cpe = din("cpe", [L, 2, 128, CMP_LEN])
    masks_d = din("masks", list(hc["masks"].shape))
    c2s_d = din("c2s", [128, ncb, n_slc])
    selA_d = din("selA", [128, S // 128, n_slc])
    selB_d = din("selB", [128, S // 128, n_slc])
    expand_d = din("expand", [n_slc, S])
    cst_d = din("cst", [6, 128, 128])
    invf_d = din("invf", [128, 1])
    yT = nc.dram_tensor("yT", [DC, 128, NT], F32, kind="ExternalOutput").ap()
    def dbf(name, shape):
        return nc.dram_tensor(name, list(shape), BF16, kind="Internal").ap()
    wb_fm = [dbf("wb_fm%d" % l_, [NFM, 128, DC, 128]) for l_ in range(L)]
    wb_o = [dbf("wb_o%d" % l_, [DC, 128, DC, 128]) for l_ in range(L)]
    wb_up = [dbf("wb_up%d" % l_, [NFF, 128, DC, 128]) for l_ in range(L)]
    wb_dn = [dbf("wb_dn%d" % l_, [DC, 128, NFF, 128]) for l_ in range(L)]
    wb_pg = [dbf("wb_pg%d" % l_, [DC, 128, DC, 128]) for l_ in range(L)]
    hTd = dint("hTd", [DC, 128, NT])
    mixT = dint("mixT", [DC, 128, NT])
    pfm_b = [dint("pfm%d" % b_, [NFM, 128, S]) for b_ in range(B)]
    ptm = dint("ptm", [NT, TMW])

    def sb(name, shape, dt=F32):
        return es.enter_context(nc.sbuf_tensor("s_" + name, list(shape), dt))

    ps = [es.enter_context(nc.psum_tensor("ps%d" % i, [128, 512], F32)) for i in range(8)]
    cst = sb("cst", [128, 6, 128], BF16)
    cstf = sb("cstf", [128, 6, 128], F32)
    perm_b, ident_b, triL_b, triU_b, s127_b, ones_b = (cst[:, i, :] for i in range(6))
    ident_f, triL_f, s127_f, ones_f = cstf[:, 1, :], cstf[:, 2, :], cstf[:, 4, :], cstf[:, 5, :]
    gv = sb("gv", [128, 3 * L + 1, DC])
    invf = sb("invf", [128, 1])
    ARB = 168 * 1024
    big = sb("big", [128, ARB], mybir.dt.uint8)
    stg = sb("stg", [128, 6, 512])
    rstd = sb("rstd", [128, 512])
    cos2 = sb("cos2", [128, 512])
    sin2 = sb("sin2", [128, 512])
    tmpc = sb("tmpc", [128, 512])
    sqb = sb("sqb", [128, 2, 512], BF16)
    ub = sb("ub", [128, 2, 512], BF16)
    rcol = sb("rcol", [128, 8])

    def arena(off, shape, dt):
        n = int(np.prod(shape))
        bsz = 2 if dt == BF16 else 4
        a = big[:, off:off + n * bsz].bitcast(dt)
        if len(shape) == 2:
            return a.rearrange("p (a b) -> p a b", b=shape[1])
        return a.rearrange("p (a b c) -> p a b c", b=shape[1], c=shape[2])
    o = 0
    hT = arena(o, [DC, 512], F32); o += DC * 512 * 4
    xb = arena(o, [DC, 512], BF16); o += DC * 512 * 2
    ab = arena(o, [FFG, 512], BF16); o += FFG * 512 * 2
    pTb = arena(o, [PC, 512], BF16); o += PC * 512 * 2
    slabs = big[:, o:o + 32768].bitcast(BF16).rearrange("p (a c n) -> p a c n", a=4, c=32); o += 32768
    assert o <= ARB, o

    def mm(out, lhsT, rhs, start, stop, reads, writes):
        P.op("pe", lambda e: e.matmul(out, lhsT, rhs, start=start, stop=stop), reads, writes)

    def act(out, in_, func, reads, writes, scale=None, bias=None):
        kw = {}
        if scale is not None:
            kw["scale"] = scale
        if bias is not None:
            kw["bias"] = bias
        P.op("act", lambda e: e.activation(out=out, in_=in_, func=func, **kw), reads, writes)

    def tt(eng, out, in0, in1, op, reads, writes):
        P.op(eng, lambda e: e.tensor_tensor(out=out, in0=in0, in1=in1, op=op), reads, writes)

    def ts(out, in0, s1, s2, op0, op1, reads, writes, eng="dve"):
        if op1 is None:
            P.op(eng, lambda e: e.tensor_scalar(out=out, in0=in0, scalar1=s1, scalar2=None, op0=op0), reads, writes)
        else:
            P.op(eng, lambda e: e.tensor_scalar(out=out, in0=in0, scalar1=s1, scalar2=s2, op0=op0, op1=op1), reads, writes)

    def cp(eng, out, in_, reads, writes):
        P.op(eng, lambda e: e.tensor_copy(out=out, in_=in_), reads, writes)

    def recip(out, in_, reads, writes):
        P.op("dve", lambda e: e.reciprocal(out=out, in_=in_), reads, writes)

    def mset(out, val, writes):
        P.op("dve", lambda e: e.memset(out, val), [], writes)

    def trn(out, in_, ident, reads, writes):
        P.op("pe", lambda e: e.transpose(out=out, in_=in_, identity=ident), reads, writes)

    def stt(out, in0, scalar, in1, op0, op1, reads, writes, eng="dve"):
        P.op(eng, lambda e: e.scalar_tensor_tensor(out=out, in0=in0, scalar=scalar, in1=in1, op0=op0, op1=op1), reads, writes)

    def dma(q, out, in_, reads, writes, key, **kw):
        P.op(q, lambda e: e.dma_start(out=out, in_=in_, **kw), reads, writes, dma_key=key)

    dma("pool", cst[:], cst_d.rearrange("a p b -> p a b"), [], ["cst"], "cst")
    dma("sp", cstf[:], cst_d.rearrange("a p b -> p a b"), [], ["cstf"], "cstf")
    dma("sp", gv[:], gvec, [], ["gv"], "gv")
    dma("sp", invf[:], invf_d, [], ["invf"], "invf")

    state = dict(slab=0, bank=0, stg=0)

    def next_stg():
        state["stg"] = (state["stg"] + 1) % 6
        return state["stg"]

    def gemm(slab_src, nk, rhs_fn, rhs_key, n_out, epilogue, ncols=512, extra_reads=(), wb=None, wkey=None, first=True):
        for n in range(n_out):
            si = state["slab"]
            state["slab"] = (si + 1) % 4
            bank = state["bank"]
            state["bank"] = (bank + 1) % 3
            if wb is None:
                dma("pool", slabs[:, si, :nk, :], slab_src(n), [], [("slab", si)], ("slab", si))
            elif first:
                dma("pool", slabs[:, si, :nk, :], slab_src(n), [], [("slab", si)], ("slab", si))
                dma("sp", wb(n), slabs[:, si, :nk, :], [("slab", si)], [("wb",) + wkey + (n,)], ("wbst", si))
            else:
                dma("pool", slabs[:, si, :nk, :], wb(n), [("wb",) + wkey + (n,)], [("slab", si)], ("slab", si))
            for c in range(nk):
                mm(ps[bank][:, :ncols], slabs[:, si, c, :], rhs_fn(c), c == 0, c == nk - 1,
                   [("slab", si), rhs_key(c), "cst"] + list(extra_reads), [("ps", bank)])
            epilogue(n, bank)

    SSB = 3

    def norm_pass(which):
        for c in range(DC):
            b2 = c % 2
            act(sqb[:, b2, :], hT[:, c, :], AF.Square, [("hT", c)], [("sq", b2)])
            mm(ps[SSB][:, :], ones_b, sqb[:, b2, :], c == 0, c == DC - 1, [("sq", b2), "cst"], [("ps", SSB)])
            ts(xb[:, c, :], hT[:, c, :], gv[:, which, c:c + 1], None, ALU.mult, None, [("hT", c), "gv"], [("xb", c)])
        ts(rstd[:], ps[SSB][:, :], 1.0 / D, EPS, ALU.mult, ALU.add, [("ps", SSB)], ["rstd"])
        act(rstd[:], rstd[:], AF.Sqrt, ["rstd"], ["rstd"])
        P.op("dve", lambda e: e.reciprocal(out=rstd[:], in_=rstd[:]), ["rstd"], ["rstd"])

    def hT_keys():
        return [("hT", c) for c in range(DC)]

    def load_hT(src, t0):
        for c0 in range(0, DC, 8):
            c1 = min(DC, c0 + 8)
            dma("sp", hT[:, c0:c1, :], src[c0:c1, :, t0:t0 + T].rearrange("c p t -> p c t"),
                [("dram_h", t0)], [("hT", c) for c in range(c0, c1)], ("hTld", c0))

    def store_hT(dst, t0, wkey):
        for c0 in range(0, DC, 8):
            c1 = min(DC, c0 + 8)
            dma("sp", dst[c0:c1, :, t0:t0 + T].rearrange("c p t -> p c t"), hT[:, c0:c1, :],
                [("hT", c) for c in range(c0, c1)], [wkey], ("hTst", c0))

    def phase_C(l, t0):
        for c0 in range(0, DC, 8):
            c1 = min(DC, c0 + 8)
            dma("pool", xb[:, c0:c1, :], mixT[c0:c1, :, t0:t0 + T].rearrange("c p t -> p c t"),
                [("mix", l)], [("xb", c) for c in range(c0, c1)], ("xbld", c0))

        def ep_res(n, bank):
            tt("dve", hT[:, n, :], hT[:, n, :], ps[bank][:, :], ALU.add, [("hT", n), ("ps", bank)], [("hT", n)])
        first = (t0 == 0)
        gemm(lambda n: w_o[l, n], DC, lambda c: xb[:, c, :], lambda c: ("xb", c), DC, ep_res,
             wb=lambda n: wb_o[l][n], wkey=("o", l), first=first)
        norm_pass(3 * l + 1)
        for g in range(NGRP):
            def ep_up(j, bank):
                si = next_stg()
                stt(stg[:, si, :], ps[bank][:, :], 0.0, rstd[:], ALU.max, ALU.mult, [("ps", bank), "rstd"], [("stg", si)])
                act(ab[:, j, :], stg[:, si, :], AF.Square, [("stg", si)], [("ab", j)])
            gemm(lambda j: w_up[l, g * FFG + j], DC, lambda c: xb[:, c, :], lambda c: ("xb", c), FFG, ep_up,
                 wb=lambda j: wb_up[l][g * FFG + j], wkey=("up", l, g), first=first)
            gemm(lambda n: w_dn[l, n, :, g * FFG:(g + 1) * FFG, :], FFG, lambda c: ab[:, c, :], lambda c: ("ab", c), DC, ep_res,
                 wb=lambda n: wb_dn[l][n, :, g * FFG:(g + 1) * FFG, :], wkey=("dn", l, g), first=first)
        norm_pass(3 * l + 2)
        dma("pool", pTb[:, :, :], pT[l, :, :, t0:t0 + T].rearrange("c p t -> p c t"), [], ["pTb"], "pTb")
        for n in range(DC):
            si = state["slab"]
            state["slab"] = (si + 1) % 4
            dma("pool", slabs[:, si, :PC, :], w_pp[l, n], [], [("slab", si)], ("slab", si))
            for c in range(PC):
                mm(ps[4][:, :], slabs[:, si, c, :], pTb[:, c, :], c == 0, c == PC - 1, [("slab", si), "pTb"], [("ps", 4)])

            def ep_gate(n_, bank):
                s1, s2 = next_stg(), next_stg()
                tt("dve", stg[:, s1, :], ps[bank][:, :], rstd[:], ALU.mult, [("ps", bank), "rstd"], [("stg", s1)])
                act(stg[:, s2, :], stg[:, s1, :], AF.Sigmoid, [("stg", s1)], [("stg", s2)])
                tt("dve", stg[:, s1, :], stg[:, s2, :], ps[4][:, :], ALU.mult, [("stg", s2), ("ps", 4)], [("stg", s1)])
                tt("dve", hT[:, n, :], hT[:, n, :], stg[:, s1, :], ALU.add, [("hT", n), ("stg", s1)], [("hT", n)])
            gemm(lambda n_: w_pg[l, n], DC, lambda c: xb[:, c, :], lambda c: ("xb", c), 1, ep_gate,
                 wb=lambda n_: wb_pg[l][n], wkey=("pg", l, n), first=first)

    def phase_A(l, t0):
        norm_pass(3 * l + 0)
        pi = sb_pos
        dma("sp", pi[:, :], posd[0:1, t0:t0 + T].partition_broadcast(128), [], ["posi"], "posi")
        P.op("dve", lambda e: e.tensor_copy(out=tmpc[:], in_=pi[:, :]), ["posi"], ["tmpc"])
        ts(tmpc[:], tmpc[:], invf[:, 0:1], None, ALU.mult, None, ["tmpc", "invf"], ["tmpc"])
        TWO_PI = 2.0 * np.pi

        def sincos(dst, shift):
            ts(dst[:], tmpc[:], 1.0 / TWO_PI, shift, ALU.mult, ALU.add, ["tmpc"], [dst_key[id(dst)]])
            P.op("dve", lambda e: e.tensor_copy(out=pi[:, :], in_=dst[:]), [dst_key[id(dst)]], ["posi"])
            P.op("dve", lambda e: e.tensor_copy(out=stg[:, 0, :], in_=pi[:, :]), ["posi"], [("stg", 0)])
            tt("dve", dst[:], dst[:], stg[:, 0, :], ALU.subtract, [dst_key[id(dst)], ("stg", 0)], [dst_key[id(dst)]])
            ts(stg[:, 0, :], dst[:], 0.0, None, ALU.is_lt, None, [dst_key[id(dst)]], [("stg", 0)])
            tt("dve", dst[:], dst[:], stg[:, 0, :], ALU.add, [dst_key[id(dst)], ("stg", 0)], [dst_key[id(dst)]])
            ts(stg[:, 0, :], dst[:], 1.0, None, ALU.is_ge, None, [dst_key[id(dst)]], [("stg", 0)])
            tt("dve", dst[:], dst[:], stg[:, 0, :], ALU.subtract, [dst_key[id(dst)], ("stg", 0)], [dst_key[id(dst)]])
            ts(dst[:], dst[:], TWO_PI, -np.pi, ALU.mult, ALU.add, [dst_key[id(dst)]], [dst_key[id(dst)]])
            ts(dst[:], dst[:], -3.1415925, 3.1415925, ALU.max, ALU.min, [dst_key[id(dst)]], [dst_key[id(dst)]])
            act(dst[:], dst[:], AF.Sin, [dst_key[id(dst)]], [dst_key[id(dst)]])
        dst_key = {id(sin2): "sin2", id(cos2): "cos2"}
        sincos(sin2, 0.5)
        sincos(cos2, 0.75)

        fm = plan["fm"]

        def ep_fm(n, bank):
            s1 = next_stg()
            tt("dve", stg[:, s1, :], ps[bank][:, :], rstd[:], ALU.mult, [("ps", bank), "rstd"], [("stg", s1)])
            if n < len(fm) and fm[n][3]:
                b2 = n % 2
                act(ub[:, b2, :], stg[:, s1, :], AF.Copy, [("stg", s1)], [("ub", b2)])
                mm(ps[5][:, :], perm_b, ub[:, b2, :], True, True, [("ub", b2), "cst"], [("ps", 5)])
                s2 = next_stg()
                tt("dve", stg[:, s2, :], ps[5][:, :], sin2[:], ALU.mult, [("ps", 5), "sin2"], [("stg", s2)])
                tt("pool", stg[:, s1, :], stg[:, s1, :], cos2[:], ALU.mult, [("stg", s1), "cos2"], [("stg", s1)])
                tt("dve", stg[:, s1, :], stg[:, s1, :], stg[:, s2, :], ALU.add, [("stg", s1), ("stg", s2)], [("stg", s1)])
            dma("sp", pfm_b[t0 // S][n, :, t0 % S:t0 % S + T], stg[:, s1, :], [("stg", s1)], [("pfm", l)], ("stgo", s1))
        gemm(lambda n: w_fm[l, n], DC, lambda c: xb[:, c, :], lambda c: ("xb", c), NFM, ep_fm,
             wb=lambda n: wb_fm[l][n], wkey=("fm", l), first=(t0 == 0))
        for q4 in range(4):
            mm(ps[6][:, q4:q4 + 1], rstd[:, q4 * 128:(q4 + 1) * 128], cstf[:, 1, 0:1], True, True, ["rstd", "cstf"], [("ps", 6)])
        P.op("dve", lambda e: e.tensor_copy(out=rcol[:, 0:4], in_=ps[6][:, 0:4]), [("ps", 6)], ["rcol"])
        state["slab"] = (state["slab"] + 1) // 2 * 2 % 4
        for n in range(NTM):
            si = state["slab"]
            state["slab"] = (si + 2) % 4
            wv = slabs[:, si:si + 2, :, :].rearrange("p a c n -> p (a c n)").rearrange("p (c n) -> p c n", n=256)
            dma("pool", wv[:, :DC, :], w_tm[l, n], [], [("slab", si), ("slab", si + 1)], ("slab", si))
            for q4 in range(4):
                bank = state["bank"]
                state["bank"] = (bank + 1) % 3
                for c in range(DC):
                    mm(ps[bank][:, :256], xb[:, c, q4 * 128:(q4 + 1) * 128], wv[:, c, :], c == 0, c == DC - 1,
                       [("slab", si), ("slab", si + 1), ("xb", c)], [("ps", bank)])
                s1 = next_stg()
                act(stg[:, s1, :256], ps[bank][:, :256], AF.Copy, [("ps", bank), "rcol"], [("stg", s1)], scale=rcol[:, q4:q4 + 1])
                dma("sp", ptm[t0 + q4 * 128:t0 + (q4 + 1) * 128, n * 256:(n + 1) * 256], stg[:, s1, :256],
                    [("stg", s1)], [("ptm", l)], ("stgo", s1))

    sb_pos = sb("posi", [128, 512], I32)

    def phase_final(t0):
        norm_pass(3 * L)
        for c in range(DC):
            s1 = next_stg()
            stt(stg[:, s1, :], hT[:, c, :], gv[:, 3 * L, c:c + 1], rstd[:], ALU.mult, ALU.mult,
                [("hT", c), "gv", "rstd"], [("stg", s1)])
            dma("sp", yT[c, :, t0:t0 + T], stg[:, s1, :], [("stg", s1)], ["yT"], ("stgo", s1))

    o = 0
    mk_all = arena(o, [hc["masks"].shape[0], 512], BF16); o += hc["masks"].shape[0] * 1024
    qTb = arena(o, [1, S], BF16); o += S * 2
    kTb = arena(o, [2, S], BF16); o += 2 * S * 2
    vb = arena(o, [2, NKB, 128], BF16); o += 2 * S * 2
    PTb = arena(o, [4, 512], BF16); o += 4 * 1024
    msb = arena(o, [2, 512], BF16); o += 2 * 1024
    selT = arena(o, [1, S], BF16); o += S * 2
    expd = arena(o, [1, S], BF16); o += S * 2
    c2s = arena(o, [ncb, n_slc], BF16); o += ncb * n_slc * 2
    kccT = arena(o, [1, 256], BF16); o += 512
    vccb = arena(o, [ncb, 128], BF16); o += ncb * 256
    hid = arena(o, [2, 2, 256], BF16); o += 2048
    w1b = arena(o, [1, CMP_LEN, CMP_HID], BF16); o += CMP_LEN * CMP_HID * 2
    w2b = arena(o, [2, 2, 128], BF16); o += 1024
    peb = arena(o, [2, CMP_LEN], BF16); o += 2 * CMP_LEN * 2
    o = (o + 3) // 4 * 4
    selA = arena(o, [S // 128, n_slc], F32); o += S // 128 * n_slc * 4
    selB = arena(o, [S // 128, n_slc], F32); o += S // 128 * n_slc * 4
    gat = arena(o, [3, 512], F32); o += 3 * 2048
    accs = arena(o, [1, 512], F32); o += 2048
    lf = arena(o, [4, NKB], F32); o += 4 * NKB * 4
    fbias = arena(o, [1, NKB], F32); o += NKB * 4
    biaspe = arena(o, [1, 4], F32); o += 16
    sc = arena(o, [4, n_slc], F32); o += 4 * n_slc * 4
    mx8 = arena(o, [1, 16], F32); o += 64
    rsum = arena(o, [1, 512], F32); o += 2048
    foxb_s = arena(o, [1, L * HM], F32); o += L * HM * 4
    assert o <= ARB, o
    ARENA = ["arena"]

    def load_head(slot, which, idx_fm, b, l):
        buf = qTb if which == "q" else kTb
        dma("pool", buf[:, slot, :], pfm_b[b][idx_fm, :, :], [("pfm", l)], [(which, slot)], (which, slot))

    def load_v(slot, idx_tm, b, l):
        src = ptm[b * S:(b + 1) * S, idx_tm * 128:(idx_tm + 1) * 128].rearrange("(j p) d -> p j d", p=128)
        dma("pool", vb[:, slot, :, :], src, [("ptm", l)], [("v", slot)], ("v", slot))

    def finish(o_bank, d_bank, first, gate_i):
        s1 = next_stg()
        if d_bank is None:
            P.op("dve", lambda e: e.tensor_copy(out=accs[:, 0, :], in_=ps[o_bank][:, :]), [("ps", o_bank)], ["accs"])
            return
        ts(stg[:, s1, :], ps[d_bank][:, :], 1e-30, None, ALU.max, None, [("ps", d_bank)], [("stg", s1)])
        recip(stg[:, s1, :], stg[:, s1, :], [("stg", s1)], [("stg", s1)])
        if gate_i is not None:
            tt("dve", stg[:, s1, :], stg[:, s1, :], gat[:, gate_i, :], ALU.mult, [("stg", s1), "gat"], [("stg", s1)])
        if first:
            tt("dve", accs[:, 0, :], ps[o_bank][:, :], stg[:, s1, :], ALU.mult, [("ps", o_bank), ("stg", s1)], ["accs"])
        else:
            tt("dve", stg[:, s1, :], ps[o_bank][:, :], stg[:, s1, :], ALU.mult, [("ps", o_bank), ("stg", s1)], [("stg", s1)])
            tt("dve", accs[:, 0, :], accs[:, 0, :], stg[:, s1, :], ALU.add, ["accs", ("stg", s1)], ["accs"])

    def store_acc(chunk, b, qc, l):
        dma("sp", mixT[chunk, :, b * S + qc * 512: b * S + (qc + 1) * 512], accs[:, 0, :], ["accs"], [("mix", l)], "accst")

    pstate = dict(s=0, pt=0)

    def softmax_tiles(qslot, kslot, vslot, qc, tiles, o_bank, d_bank, bias_fn=None, vsrc=None, ksrc=None):
        nt = len(tiles)
        for i, (jb, mi, dyn) in enumerate(tiles):
            sbk = 6 + pstate["s"]
            pstate["s"] ^= 1
            pt = pstate["pt"]
            pstate["pt"] = (pt + 1) % 4
            lhs_k = kTb[:, kslot, jb * 128:(jb + 1) * 128] if ksrc is None else ksrc(jb)
            mm(ps[sbk][:, :], lhs_k, qTb[:, qslot, qc * 512:(qc + 1) * 512], True, True,
               [("k", kslot), ("q", qslot), "kcc"], [("ps", sbk)])
            if bias_fn is None:
                act(PTb[:, pt, :], ps[sbk][:, :], AF.Exp, [("ps", sbk)], [("pt", pt)], scale=scale)
            else:
                for q4 in range(4):
                    act(PTb[:, pt, q4 * 128:(q4 + 1) * 128], ps[sbk][:, q4 * 128:(q4 + 1) * 128], AF.Exp,
                        [("ps", sbk), "fbias"], [("pt", pt)], scale=scale, bias=bias_fn(jb, qc * 4 + q4))
            if dyn is not None:
                dyn(jb, pt)
            if mi is not None:
                tt("pool", PTb[:, pt, :], PTb[:, pt, :], mk_all[:, mi, :], ALU.mult, [("pt", pt), "masks"], [("pt", pt)])
            lhs_v = vb[:, vslot, jb, :] if vsrc is None else vsrc(jb)
            mm(ps[o_bank][:, :], lhs_v, PTb[:, pt, :], i == 0, i == nt - 1, [("pt", pt), ("v", vslot), "vcc"], [("ps", o_bank)])
            if d_bank is not None:
                mm(ps[d_bank][:, :], ones_b, PTb[:, pt, :], i == 0, i == nt - 1, [("pt", pt), "cst"], [("ps", d_bank)])

    def band_tiles(name, qc, lo_blocks):
        out = []
        for jb in range(NKB):
            r = jb * 128 - qc * 512
            if r > 384:
                break
            ent = midx.get((name, r))
            if ent is None:
                continue
            out.append((jb, None if ent[1] else ent[0], None))
        return out

    def attention(l):
        P.barrier()
        for m0 in range(0, hc["masks"].shape[0], 16):
            m1 = min(hc["masks"].shape[0], m0 + 16)
            dma("pool", mk_all[:, m0:m1, :], masks_d[m0:m1].rearrange("m p q -> p m q"), [],
                ["masks"], ("mk", m0))
        dma("pool", expd[:n_slc, 0, :], expand_d, [], ["expd"], "expd")
        dma("pool", c2s[:, :, :], c2s_d, [], ["c2s"], "c2s")
        dma("sp", selA[:, :, :], selA_d, [], ["selA"], "selA")
        dma("sp", selB[:, :, :], selB_d, [], ["selB"], "selB")
        dma("sp", foxb_s[:, 0, :], foxb, [], ["foxb"], "foxb")
        for kv in range(2):
            dma("pool", w2b[:, kv, :, :], cw2[l, kv], [], ["w2"], ("w2", kv))
        dma("pool", peb[:, :, :], cpe[l].rearrange("a p c -> p a c"), [], ["pe"], "pe")
        fmi, tmi = plan["fmidx"], plan["tmidx"]
        small = NFM - 1
        for b in range(B):
            for h in range(HM):
                load_head(0, "q", fmi[("qb", h)], b, l)
                load_head(0, "k", fmi[("kb", h)], b, l)
                load_v(0, tmi[("vb", h)], b, l)
                fcol = plan["fcol0"] + h
                src = ptm[b * S:(b + 1) * S, fcol:fcol + 1].rearrange("(j p) o -> p (j o)", p=128)
                dma("sp", lf[:, 0, :], src, [("ptm", l)], ["lf0"], "lf0", allow_slow_non_contiguous=True)
                ts(lf[:, 0, :], lf[:, 0, :], foxb_s[:, 0, l * HM + h:l * HM + h + 1], None, ALU.add, None, ["lf0", "foxb"], ["lf0"])
                ts(lf[:, 0, :], lf[:, 0, :], -1.0, None, ALU.mult, None, ["lf0"], ["lf0"])
                act(lf[:, 0, :], lf[:, 0, :], AF.Exp, ["lf0"], ["lf0"])
                act(lf[:, 0, :], lf[:, 0, :], AF.Ln, ["lf0"], ["lf0"], bias=1.0)
                ts(lf[:, 0, :], lf[:, 0, :], -1.0, None, ALU.mult, None, ["lf0"], ["lf0"])
                P.op("dve", lambda e: e.memset(lf[:, 1, 0:1], 0.0), [], ["lf1"])
                for j in range(1, NKB):
                    tt("dve", lf[:, 1, j:j + 1], lf[:, 1, j - 1:j], lf[:, 0, j - 1:j], ALU.add, ["lf1", "lf0"], ["lf1"])
                mm(ps[4][:, :NKB], triL_f, lf[:, 0, :], True, False, ["lf0", "cstf"], [("ps", 4)])
                mm(ps[4][:, :NKB], ones_f, lf[:, 1, :], False, True, ["lf1", "cstf"], [("ps", 4)])
                P.op("dve", lambda e: e.tensor_copy(out=lf[:, 2, :], in_=ps[4][:, :NKB]), [("ps", 4)], ["lf2"])
                mm(ps[4][:, :NKB], s127_f, lf[:, 2, :], True, True, ["lf2", "cstf"], [("ps", 4)])
                P.op("dve", lambda e: e.tensor_copy(out=lf[:, 3, :], in_=ps[4][:, :NKB]), [("ps", 4)], ["lf3"])
                for qc in range(NQC):
                    tiles = band_tiles("causal", qc, None)

                    def bias_fn(jb, i, _=None):
                        return fbias[:, 0, i:i + 1]
                    new_tiles = []
                    for (jb, mi, _) in tiles:
                        new_tiles.append((jb, mi, None))
                    for ti, (jb, mi, _) in enumerate(new_tiles):
                        stt(fbias[:, 0, qc * 4:qc * 4 + 4], lf[:, 3, qc * 4:qc * 4 + 4], 1.0, lf[:, 2, jb:jb + 1].to_broadcast([128, 4]),
                            ALU.mult, ALU.subtract, ["lf3", "lf2"], ["fbias"])
                        ts(fbias[:, 0, qc * 4:qc * 4 + 4], fbias[:, 0, qc * 4:qc * 4 + 4], 0.0, None, ALU.min, None, ["fbias"], ["fbias"])
                        softmax_tiles_one(0, 0, 0, qc, jb, mi, ti, len(new_tiles), 4, 5, bias_fn)
                    finish(4, 5, True, None)
                    store_acc(HM + h, b, qc, l)
            for h in range(HM):
                load_head(0, "q", fmi[("qc", h)], b, l)
                load_head(0, "k", fmi[("kcc", h)], b, l)
                load_v(0, tmi[("vcc", h)], b, l)
                for qc in range(NQC):
                    softmax_tiles(0, 0, 0, qc, band_tiles("dil", qc, None), 4, 5)
                    finish(4, 5, True, None)
                    store_acc(2 * HM + h, b, qc, l)
            for h in range(HM):
                load_head(0, "q", fmi[("qd", h)], b, l)
                load_head(0, "k", fmi[("kd", h)], b, l)
                load_v(0, tmi[("vd", h)], b, l)
                for qc in range(NQC):
                    tiles = band_tiles("strict", qc, None)[::-1]
                    nt = len(tiles)
                    for i, (jb, mi, _) in enumerate(tiles):
                        sbk = 6 + pstate["s"]
                        pstate["s"] ^= 1
                        pt = pstate["pt"]
                        pstate["pt"] = (pt + 1) % 4
                        s1, s2 = next_stg(), next_stg()
                        mm(ps[sbk][:, :], kTb[:, 0, jb * 128:(jb + 1) * 128], qTb[:, 0, qc * 512:(qc + 1) * 512], True, True,
                           [("k", 0), ("q", 0)], [("ps", sbk)])
                        act(stg[:, s1, :], ps[sbk][:, :], AF.Exp, [("ps", sbk)], [("stg", s1)], scale=scale)
                        act(stg[:, s1, :], stg[:, s1, :], AF.Ln, [("stg", s1)], [("stg", s1)], bias=1.0)
                        if mi is not None:
                            tt("pool", stg[:, s1, :], stg[:, s1, :], mk_f(mi), ALU.mult, [("stg", s1), "masks"], [("stg", s1)])
                        act(PTb[:, pt, :], stg[:, s1, :], AF.Copy, [("stg", s1)], [("pt", pt)])
                        mm(ps[5][:, :], triU_b, PTb[:, pt, :], True, i == 0, [("pt", pt), "cst"], [("ps", 5)])
                        if i > 0:
                            mm(ps[5][:, :], ones_b, ub[:, 0, :], False, True, [("ub", 0), "cst"], [("ps", 5)])
                        stt(stg[:, s2, :], ps[sbk][:, :], scale, stg[:, s1, :], ALU.mult, ALU.subtract, [("ps", sbk), ("stg", s1)], [("stg", s2)])
                        tt("dve", stg[:, s2, :], stg[:, s2, :], ps[5][:, :], ALU.subtract, [("stg", s2), ("ps", 5)], [("stg", s2)])
                        if i == 0:
                            cp("pool", rsum[:, 0, :], stg[:, s1, :], [("stg", s1)], ["rsum"])
                        else:
                            tt("pool", rsum[:, 0, :], rsum[:, 0, :], stg[:, s1, :], ALU.add, ["rsum", ("stg", s1)], ["rsum"])
                        P.op("pool", lambda e: e.tensor_copy(out=ub[:, 0, :], in_=rsum[:, 0, :]), ["rsum"], [("ub", 0)])
                        pt2 = pstate["pt"]
                        pstate["pt"] = (pt2 + 1) % 4
                        act(PTb[:, pt2, :], stg[:, s2, :], AF.Exp, [("stg", s2)], [("pt", pt2)])
                        if mi is not None:
                            tt("pool", PTb[:, pt2, :], PTb[:, pt2, :], mk_all[:, mi, :], ALU.mult, [("pt", pt2), "masks"], [("pt", pt2)])
                        mm(ps[4][:, :], vb[:, 0, jb, :], PTb[:, pt2, :], i == 0, i == nt - 1, [("pt", pt2), ("v", 0)], [("ps", 4)])
                    if nt == 0:
                        P.op("dve", lambda e: e.memset(accs[:, 0, :], 0.0), [], ["accs"])
                    else:
                        finish(4, None, True, None)
                    store_acc(3 * HM + h, b, qc, l)
            for g in range(G):
                nsa_group(l, b, g)

    def mk_f(mi):
        return mk_all[:, mi, :]

    def softmax_tiles_one(qslot, kslot, vslot, qc, jb, mi, i, nt, o_bank, d_bank, bias_fn):
        sbk = 6 + pstate["s"]
        pstate["s"] ^= 1
        pt = pstate["pt"]
        pstate["pt"] = (pt + 1) % 4
        mm(ps[sbk][:, :], kTb[:, kslot, jb * 128:(jb + 1) * 128], qTb[:, qslot, qc * 512:(qc + 1) * 512], True, True,
           [("k", kslot), ("q", qslot)], [("ps", sbk)])
        for q4 in range(4):
            act(PTb[:, pt, q4 * 128:(q4 + 1) * 128], ps[sbk][:, q4 * 128:(q4 + 1) * 128], AF.Exp,
                [("ps", sbk), "fbias"], [("pt", pt)], scale=scale, bias=bias_fn(jb, qc * 4 + q4))
        if mi is not None:
            tt("pool", PTb[:, pt, :], PTb[:, pt, :], mk_all[:, mi, :], ALU.mult, [("pt", pt), "masks"], [("pt", pt)])
        mm(ps[o_bank][:, :], vb[:, vslot, jb, :], PTb[:, pt, :], i == 0, i == nt - 1, [("pt", pt), ("v", vslot)], [("ps", o_bank)])
        mm(ps[d_bank][:, :], ones_b, PTb[:, pt, :], i == 0, i == nt - 1, [("pt", pt), "cst"], [("ps", d_bank)])

    def nsa_group(l, b, g):
        fmi, tmi = plan["fmidx"], plan["tmidx"]
        small = NFM - 1
        for kv in range(2):
            load_head(1, "k", fmi[("kc" if kv == 0 else "vc", g)], b, l)
            dma("pool", w1b[:, 0, :, :], cw1[l, kv], [], [("w1", 0)], ("w1", 0))
            for ht in range(2):
                for li in range(CMP_LEN):
                    mm(ps[4][:, ht:ht + 1], w1b[:, 0, li, ht * 128:(ht + 1) * 128], peb[:, kv, li:li + 1], li == 0, li == CMP_LEN - 1,
                       [("w1", 0), "pe"], [("ps", 4)])
            P.op("dve", lambda e: e.tensor_copy(out=biaspe[:, 0, 0:2], in_=ps[4][:, 0:2]), [("ps", 4)], ["biaspe"])
            for ht in range(2):
                for li in range(CMP_LEN):
                    rhs = kTb[:, 1, li:li + CMP_STRIDE * (n_cmp - 1) + 1:CMP_STRIDE]
                    mm(ps[5][:, :n_cmp], w1b[:, 0, li, ht * 128:(ht + 1) * 128], rhs, li == 0, li == CMP_LEN - 1,
                       [("w1", 0), ("k", 1)], [("ps", 5)])
                s1, s2 = next_stg(), next_stg()
                ts(stg[:, s1, :n_cmp], ps[5][:, :n_cmp], biaspe[:, 0, ht:ht + 1], None, ALU.add, None, [("ps", 5), "biaspe"], [("stg", s1)])
                tt("dve", stg[:, s2, :n_cmp], stg[:, s1, :n_cmp], stg[:, s1, :n_cmp], ALU.mult, [("stg", s1)], [("stg", s2)])
                ts(stg[:, s2, :n_cmp], stg[:, s2, :n_cmp], 0.044715, 1.0, ALU.mult, ALU.add, [("stg", s2)], [("stg", s2)])
                tt("dve", stg[:, s2, :n_cmp], stg[:, s2, :n_cmp], stg[:, s1, :n_cmp], ALU.mult, [("stg", s2), ("stg", s1)], [("stg", s2)])
                act(stg[:, s2, :n_cmp], stg[:, s2, :n_cmp], AF.Sigmoid, [("stg", s2)], [("stg", s2)], scale=1.5957691216)
                mset(hid[:, kv, ht, :], 0.0, [("hid", kv)])
                tt("dve", hid[:, kv, ht, :n_cmp], stg[:, s2, :n_cmp], stg[:, s1, :n_cmp], ALU.mult, [("stg", s2), ("stg", s1)], [("hid", kv)])
            if kv == 0:
                for ht in range(2):
                    mm(ps[5][:, :256], w2b[:, 0, ht, :], hid[:, 0, ht, :], ht == 0, ht == 1, [("hid", 0), "w2"], [("ps", 5)])
                P.op("dve", lambda e: e.tensor_copy(out=kccT[:, 0, :], in_=ps[5][:, :256]), [("ps", 5)], ["kcc"])
            else:
                for cb in range(ncb):
                    for ht in range(2):
                        mm(ps[5][:, :128], hid[:, 1, ht, cb * 128:(cb + 1) * 128], w2b[:, 1, ht, :], ht == 0, ht == 1,
                           [("hid", 1), "w2"], [("ps", 5)])
                    cp("dve", vccb[:, cb, :], ps[5][:, :128], [("ps", 5)], ["vcc"])

        def cmp_tiles(qc):
            out = []
            for cb in range(ncb):
                ent = midx.get(("cmp", cb, qc))
                if ent is not None:
                    out.append((cb, ent[0], None))
            return out
        for hh in range(HPG):
            load_head(hh % 2, "q", fmi[("qa", g * HPG + hh)], b, l) if False else None
        for qc in range(NQC):
            tl = cmp_tiles(qc)
            first_imp = True
            for hh in range(HPG):
                h = g * HPG + hh
                if qc == 0 or True:
                    pass
                qs = 0
                load_q_chunk(fmi[("qa", h)], b, l, qc)
                if not tl:
                    continue
                pts = []
                for i, (cb, mi, _) in enumerate(tl):
                    sbk = 6 + pstate["s"]
                    pstate["s"] ^= 1
                    pt = pstate["pt"]
                    pstate["pt"] = (pt + 1) % 4
                    pts.append(pt)
                    mm(ps[sbk][:, :], kccT[:, 0, cb * 128:(cb + 1) * 128], qchunk[:, 0, :], True, True, ["kcc", "qchunk"], [("ps", sbk)])
                    act(PTb[:, pt, :], ps[sbk][:, :], AF.Exp, [("ps", sbk)], [("pt", pt)], scale=scale)
                    tt("pool", PTb[:, pt, :], PTb[:, pt, :], mk_all[:, mi, :], ALU.mult, [("pt", pt), "masks"], [("pt", pt)])
                    mm(ps[5][:, :], ones_b, PTb[:, pt, :], i == 0, i == len(tl) - 1, [("pt", pt), "cst"], [("ps", 5)])
                s1 = next_stg()
                ts(stg[:, s1, :], ps[5][:, :], 1e-30, None, ALU.max, None, [("ps", 5)], [("stg", s1)])
                recip(stg[:, s1, :], stg[:, s1, :], [("stg", s1)], [("stg", s1)])
                for i, (cb, mi, _) in enumerate(tl):
                    pt = pts[i]
                    tt("dve", PTb[:, pt, :], PTb[:, pt, :], stg[:, s1, :], ALU.mult, [("pt", pt), ("stg", s1)], [("pt", pt)])
                    last = (hh == HPG - 1) and (i == len(tl) - 1)
                    mm(ps[4][:n_slc, :], c2s[:, cb, :], PTb[:, pt, :], first_imp, last, [("pt", pt), "c2s"], [("ps", 4)])
                    first_imp = False
            s1 = next_stg()
            if tl:
                cp("dve", stg[:n_slc, s1, :], ps[4][:n_slc, :], [("ps", 4)], [("stg", s1)])
            else:
                mset(stg[:n_slc, s1, :], 0.0, [("stg", s1)])
            for q4 in range(4):
                qb = qc * 4 + q4
                trn(ps[5][:, :n_slc], stg[:n_slc, s1, q4 * 128:(q4 + 1) * 128], ident_f[:n_slc, :n_slc], [("stg", s1), "cstf"], [("ps", 5)])
                tt("dve", sc[:, 0, :], ps[5][:, :n_slc], selA[:, qb, :], ALU.mult, [("ps", 5), "selA"], ["sc0"])
                tt("dve", sc[:, 0, :], sc[:, 0, :], selB[:, qb, :], ALU.add, ["sc0", "selB"], ["sc0"])
                P.op("dve", lambda e: e.max(out=mx8[:, 0, 0:8], in_=sc[:, 0, :]), ["sc0"], ["mx8"])
                P.op("dve", lambda e: e.match_replace(out=sc[:, 1, :], in_to_replace=mx8[:, 0, 0:8], in_values=sc[:, 0, :], imm_value=-1e30), ["mx8", "sc0"], ["sc1"])
                P.op("dve", lambda e: e.max(out=mx8[:, 0, 8:16], in_=sc[:, 1, :]), ["sc1"], ["mx8"])
                P.op("dve", lambda e: e.match_replace(out=sc[:, 2, :], in_to_replace=mx8[:, 0, 8:16], in_values=sc[:, 1, :], imm_value=-1e30), ["mx8", "sc1"], ["sc2"])
                tt("dve", sc[:, 3, :], sc[:, 0, :], sc[:, 2, :], ALU.not_equal, ["sc0", "sc2"], ["sc3"])
                trn(ps[5][:n_slc, 128:256], sc[:, 3, :], ident_f, ["sc3", "cstf"], [("ps", 5)])
                cp("dve", selT[:n_slc, 0, qb * 128:(qb + 1) * 128], ps[5][:n_slc, 128:256], [("ps", 5)], ["selT"])
        load_head(1, "k", fmi[("ks", g)], b, l)
        load_v(1, tmi[("vs", g)], b, l)
        load_head(0, "k", fmi[("kw", g)], b, l)
        load_v(0, tmi[("vw", g)], b, l)
        for hh in range(HPG):
            h = g * HPG + hh
            load_head(0, "q", fmi[("qa", h)], b, l)
            for qc in range(NQC):
                for br in range(3):
                    dma("sp", gat[:, br, :], pfm_b[b][small, 3 * h + br:3 * h + br + 1, qc * 512:(qc + 1) * 512].partition_broadcast(128),
                        [("pfm", l)], ["gat"], ("gat", br))
                act(gat[:, :, :], gat[:, :, :], AF.Sigmoid, ["gat"], ["gat"])
                tl = cmp_tiles(qc)
                if tl:
                    softmax_tiles(0, 0, 0, qc, tl, 4, 5, ksrc=lambda cb: kccT[:, 0, cb * 128:(cb + 1) * 128], vsrc=lambda cb: vccb[:, cb, :])
                    finish(4, 5, True, 0)
                else:
                    P.op("dve", lambda e: e.memset(accs[:, 0, :], 0.0), [], ["accs"])

                def dyn(jb, pt, qc=qc):
                    mm(ps[3][:, :], expd[:n_slc, 0, jb * 128:(jb + 1) * 128], selT[:n_slc, 0, qc * 512:(qc + 1) * 512], True, True,
                       ["expd", "selT"], [("ps", 3)])
                    tt("dve", PTb[:, pt, :], PTb[:, pt, :], ps[3][:, :], ALU.mult, [("pt", pt), ("ps", 3)], [("pt", pt)])
                tiles = [(jb, mi, dyn) for (jb, mi, _) in band_tiles("causal", qc, None)]
                softmax_tiles(0, 1, 1, qc, tiles, 4, 5)
                finish(4, 5, False, 1)
                softmax_tiles(0, 0, 0, qc, band_tiles("win", qc, None), 4, 5)
                finish(4, 5, False, 2)
                store_acc(h, b, qc, l)

    qchunk = sb("qchunk", [128, 1, 512], BF16)

    def load_q_chunk(idx_fm, b, l, qc):
        dma("pool", qchunk[:, 0, :], pfm_b[b][idx_fm, :, qc * 512:(qc + 1) * 512], [("pfm", l)], ["qchunk"], "qchunk")

    def dense_guard():
        P.barrier()

    for l in range(L + 1):
        for t0 in range(0, NT, T):
            if l == 0:
                load_hT(xT, t0)
            else:
                load_hT(hTd, t0)
                phase_C(l - 1, t0)
            if l < L:
                phase_A(l, t0)
                store_hT(hTd, t0, ("dram_h", t0))
            else:
                phase_final(t0)
        if l < L:
            attention(l)
            dense_guard()
    P.barrier()
    P.emit(es)
    es.close()
    return nc, hc, plan


def prep_inputs(cfg, hc, plan, x, p, positions, norm_attn, w_in, fox_bf, cmp_pe_k, cmp_w1_k, cmp_w2_k,
                cmp_pe_v, cmp_w1_v, cmp_w2_v, w_o, norm_mlp, w_up, w_down, norm_ple, w_ple_gate, w_ple_proj, norm_final):
    D, S, B, L, HM, G, FF, PLE = (cfg[k] for k in ["D", "S", "B", "L", "HM", "G", "FF", "PLE"])
    NT, DC = B * S, D // 128
    f = lambda a: np.ascontiguousarray(np.asarray(a, dtype=np.float32))
    m = {}
    m["xT"] = f(np.asarray(x).reshape(NT, DC, 128).transpose(1, 2, 0))
    m["pT"] = f(np.asarray(p).reshape(L, NT, PLE // 128, 128).transpose(0, 2, 3, 1))
    m["pos"] = np.ascontiguousarray(np.asarray(positions).reshape(1, NT).astype(np.int32))
    w_in = np.asarray(w_in)
    fm_cols = []
    for (_, _, c0, _) in plan["fm"]:
        fm_cols += list(range(c0, c0 + 128))
    sm = plan["small_cols"]
    wfm, wtm = [], []
    for l in range(L):
        Wf = np.zeros((D, plan["nfm"] * 128), np.float32)
        Wf[:, :len(fm_cols)] = w_in[l][:, fm_cols]
        Wf[:, len(fm_cols):len(fm_cols) + len(sm)] = w_in[l][:, sm]
        wfm.append(tile_w(Wf, 128))
        Wt = np.zeros((D, plan["ntm"] * 256), np.float32)
        Wt[:, :len(plan["tm_cols"])] = w_in[l][:, plan["tm_cols"]]
        wtm.append(tile_w(Wt, 256))
    m["w_fm"] = np.stack(wfm)
    m["w_tm"] = np.stack(wtm)
    m["w_o"] = np.stack([tile_w(np.asarray(w_o[l]), 128) for l in range(L)])
    m["w_up"] = np.stack([tile_w(np.asarray(w_up[l]), 128) for l in range(L)])
    m["w_dn"] = np.stack([tile_w(np.asarray(w_down[l]), 128) for l in range(L)])
    m["w_pg"] = np.stack([tile_w(np.asarray(w_ple_gate[l]), 128) for l in range(L)])
    m["w_pp"] = np.stack([tile_w(np.asarray(w_ple_proj[l]), 128) for l in range(L)])
    gv = np.zeros((3 * L + 1, D), np.float32)
    for l in range(L):
        gv[3 * l], gv[3 * l + 1], gv[3 * l + 2] = np.asarray(norm_attn[l]), np.asarray(norm_mlp[l]), np.asarray(norm_ple[l])
    gv[3 * L] = np.asarray(norm_final)
    m["gvec"] = f(gv.reshape(3 * L + 1, DC, 128).transpose(2, 0, 1))
    m["foxb"] = f(np.broadcast_to(np.asarray(fox_bf).reshape(1, L * HM), (128, L * HM)))
    w1 = np.stack([np.stack([np.asarray(cmp_w1_k[l]), np.asarray(cmp_w1_v[l])]) for l in range(L)])
    m["cw1"] = f(w1.reshape(L, 2, CMP_LEN, 128, CMP_HID).transpose(0, 1, 3, 2, 4))
    w2 = np.stack([np.stack([np.asarray(cmp_w2_k[l]), np.asarray(cmp_w2_v[l])]) for l in range(L)])
    m["cw2"] = f(w2.reshape(L, 2, 2, 128, 128).transpose(0, 1, 3, 2, 4))
    pe = np.stack([np.stack([np.asarray(cmp_pe_k[l]), np.asarray(cmp_pe_v[l])]) for l in range(L)])
    m["cpe"] = f(pe.transpose(0, 1, 3, 2))
    m["masks"] = f(hc["masks"])
    m["c2s"] = f(hc["c2s"])
    m["selA"] = f(hc["selA"])
    m["selB"] = f(hc["selB"])
    m["expand"] = f(hc["expand"])
    m["cst"] = f(np.stack([hc["perm"], hc["ident"], hc["triL"], hc["triU"], hc["sel127"], np.ones((128, 128), np.float32)]))
    m["invf"] = f(hc["invf"])
    return m


_CACHE = {}


def kernel(**inputs):
    full = CFG
    cfg = dict(full)
    cfg["B"] = 1
    key = tuple(sorted(cfg.items()))
    if key not in _CACHE:
        _CACHE[key] = build(cfg)
    nc, hc, plan = _CACHE[key]
    NB = full["B"]
    shared = None
    maps = []
    for b in range(NB):
        sub = dict(inputs)
        sub["x"] = np.asarray(inputs["x"])[b:b + 1]
        sub["p"] = np.asarray(inputs["p"])[:, b:b + 1]
        sub["positions"] = np.asarray(inputs["positions"])[b:b + 1]
        if shared is None:
            m = prep_inputs(cfg, hc, plan, **sub)
            shared = m
        else:
            m = dict(shared)
            D, S, L, PLE = cfg["D"], cfg["S"], cfg["L"], cfg["PLE"]
            m["xT"] = np.ascontiguousarray(sub["x"].reshape(S, D // 128, 128).transpose(1, 2, 0).astype(np.float32))
            m["pT"] = np.ascontiguousarray(sub["p"].reshape(L, S, PLE // 128, 128).transpose(0, 2, 3, 1).astype(np.float32))
            m["pos"] = np.ascontiguousarray(sub["positions"].reshape(1, S).astype(np.int32))
        maps.append(m)
    res = run_bass_kernel_spmd(nc, maps, core_ids=list(range(NB)))
    D, S = cfg["D"], cfg["S"]
    outs = []
    for b in range(NB):
        yT = res.results[b]["yT"]
        outs.append(np.ascontiguousarray(yT.reshape(D, S).T).reshape(1, S, D))
        if cfg.get("debug"):
            global DEBUG_OUT
            DEBUG_OUT = res.results[b]
    return np.concatenate(outs, axis=0).astype(np.float32)
```

```python
import contextlib
import numpy as np
import concourse.bass as bass
import concourse.mybir as mybir
from concourse.bass_utils import run_bass_kernel_spmd

F32 = mybir.dt.float32
BF16 = mybir.dt.bfloat16
I32 = mybir.dt.int32
AF = mybir.ActivationFunctionType
ALU = mybir.AluOpType

CFG = dict(D=4096, S=4096, B=2, L=2, HM=8, G=2, FF=16384, PLE=256)
CMP_LEN, CMP_STRIDE, CMP_HID, SLC_LEN, TOPK, WIN = 32, 16, 256, 64, 16, 512
DIL = ((128, 1), (512, 4), (2048, 16))
EPS = 1e-6
T = 512


class Prog:
    def __init__(self, nc):
        self.nc = nc
        self.ops = []
        self.lastw = {}
        self.readers = {}

    def op(self, eng, fn, reads=(), writes=(), dma_key=None):
        i = len(self.ops)
        deps = set()
        for k in reads:
            if k in self.lastw:
                deps.add(self.lastw[k])
        for k in writes:
            if k in self.lastw:
                deps.add(self.lastw[k])
            deps.update(self.readers.get(k, ()))
        for k in reads:
            self.readers.setdefault(k, []).append(i)
        for k in writes:
            self.lastw[k] = i
            self.readers[k] = []
        self.ops.append(dict(eng=eng, fn=fn, deps=deps, dma_key=dma_key, sig=False))
        return i

    def barrier(self):
        deps = set(self.lastw.values())
        for r in self.readers.values():
            deps.update(r)
        for e in ["pe", "act", "dve", "pool", "sp"]:
            self.ops.append(dict(eng=e, fn=None, deps=set(deps), dma_key=None, sig=False))
        self.lastw.clear()
        self.readers.clear()

    def emit(self, es):
        nc = self.nc
        ops = self.ops
        for o in ops:
            pruned = set()
            for d in o["deps"]:
                od = ops[d]
                if od["dma_key"] is None and od["eng"] == "pe" and o["eng"] == "pe" and o["dma_key"] is None:
                    continue
                pruned.add(d)
                od["sig"] = True
            o["deps"] = pruned
        engs = ["pe", "act", "dve", "pool", "sp"]
        esem = {e: es.enter_context(nc.semaphore("sem_" + e)) for e in engs}
        dsem = {}
        ecnt = {e: 0 for e in engs}
        dcnt = {}
        for o in ops:
            if o["dma_key"] is not None:
                k = o["dma_key"]
                if k not in dsem:
                    dsem[k] = es.enter_context(nc.semaphore("dsem%d" % len(dsem)))
                    dcnt[k] = 0
                dcnt[k] += 16
                o["sem"], o["cnt"] = dsem[k], dcnt[k]
            elif o["sig"]:
                ecnt[o["eng"]] += 1
                o["sem"], o["cnt"] = esem[o["eng"]], ecnt[o["eng"]]
        block = es.enter_context(nc.Block())

        def run(engname, e):
            waited = {}
            for o in ops:
                if o["eng"] != engname:
                    continue
                need = {}
                for d in o["deps"]:
                    od = ops[d]
                    s = od["sem"]
                    need[id(s)] = (s, max(need.get(id(s), (s, 0))[1], od["cnt"]))
                for sid, (s, c) in need.items():
                    if waited.get(sid, 0) < c:
                        e.wait_ge(s, c)
                        waited[sid] = c
                if o["fn"] is None:
                    continue
                ins = o["fn"](e)
                if o["dma_key"] is not None:
                    ins.then_inc(o["sem"], 16)
                elif o["sig"]:
                    ins.then_inc(o["sem"], 1)

        @block.tensor
        def _(e):
            run("pe", e)

        @block.scalar
        def _(e):
            run("act", e)

        @block.vector
        def _(e):
            run("dve", e)

        @block.gpsimd
        def _(e):
            run("pool", e)

        @block.sync
        def _(e):
            run("sp", e)


def col_plan(cfg):
    HM, G = cfg["HM"], cfg["G"]
    splits = [HM * 128, G * 128, G * 128, G * 128, G * 128, G * 128, G * 128, HM * 3,
              HM * 128, HM * 128, HM * 128, HM, HM * 128, HM * 128, HM * 128, HM * 128, HM * 128, HM * 128]
    off = np.concatenate([[0], np.cumsum(splits)])
    names = ["qa", "kc", "vc", "ks", "vs", "kw", "vw", "ga", "qb", "kb", "vb", "fb", "qc", "kcc", "vcc", "qd", "kd", "vd"]
    o = {n: int(off[i]) for i, n in enumerate(names)}
    fm = []
    for n, cnt, rope in [("qa", HM, 1), ("kc", G, 0), ("vc", G, 0), ("ks", G, 1), ("kw", G, 1), ("qb", HM, 0), ("kb", HM, 0),
                         ("qc", HM, 1), ("kcc", HM, 1), ("qd", HM, 0), ("kd", HM, 0)]:
        for i in range(cnt):
            fm.append((n, i, o[n] + 128 * i, rope))
    fmidx = {(n, i): j for j, (n, i, _, _) in enumerate(fm)}
    small_cols = list(range(o["ga"], o["ga"] + HM * 3)) + list(range(o["fb"], o["fb"] + HM))
    tm = []
    for n, cnt in [("vs", G), ("vw", G), ("vb", HM), ("vcc", HM), ("vd", HM)]:
        for i in range(cnt):
            tm.append((n, i, o[n] + 128 * i))
    tmidx = {(n, i): j for j, (n, i, _) in enumerate(tm)}
    tm_cols = []
    for (_, _, c0) in tm:
        tm_cols += list(range(c0, c0 + 128))
    fcol0 = len(tm_cols)
    tm_cols += list(range(o["fb"], o["fb"] + HM))
    ntm = (len(tm_cols) + 255) // 256
    return dict(fm=fm, fmidx=fmidx, small_cols=small_cols, tm=tm, tmidx=tmidx, tm_cols=tm_cols, fcol0=fcol0,
                ntm=ntm, nfm=len(fm) + 1, nin=int(off[-1]))


def tile_w(W, ns):
    K, N = W.shape
    return np.ascontiguousarray(W.reshape(K // 128, 128, N // ns, ns).transpose(2, 1, 0, 3))


def host_consts(cfg):
    S = cfg["S"]
    c = {}
    half = 64
    inv = (10000.0 ** (-np.arange(half, dtype=np.float32) / half)).astype(np.float32)
    c["invf"] = np.concatenate([inv, inv]).reshape(128, 1).astype(np.float32)
    perm = np.zeros((128, 128), np.float32)
    for d in range(64):
        perm[d + 64, d] = -1.0
        perm[d, d + 64] = 1.0
    c["perm"] = perm
    c["ident"] = np.eye(128, dtype=np.float32)
    k = np.arange(128)[:, None]
    q = np.arange(512)[None, :]
    masks = []
    midx = {}

    def add(name, r, fn):
        d = q - k - r
        m = fn(d).astype(np.float32)
        if m.max() == 0:
            return
        if m.min() == 1 and m.max() == 1:
            midx[(name, r)] = (None, True)
            return
        midx[(name, r)] = (len(masks), False)
        masks.append(m)
    for r in range(384, -4096, -128):
        add("causal", r, lambda d: d >= 0)
        add("strict", r, lambda d: d > 0)
        add("win", r, lambda d: (d >= 0) & (d < WIN))
        add("dil", r, lambda d: sum(((d >= 0) & (d <= w) & (d % dl == 0)).astype(np.int32) for w, dl in DIL))
    n_cmp = (S - CMP_LEN) // CMP_STRIDE + 1
    ncb = (n_cmp + 127) // 128
    for cb in range(ncb):
        for qc in range(S // 512):
            cc = cb * 128 + k
            t = qc * 512 + q
            m = ((cc * CMP_STRIDE + CMP_LEN - 1 <= t) & (cc < n_cmp)).astype(np.float32)
            if m.max() == 0:
                continue
            midx[("cmp", cb, qc)] = (len(masks), False)
            masks.append(m)
    c["masks"] = np.stack(masks).astype(np.float32)
    c["midx"] = midx
    n_slc = S // SLC_LEN
    ratio, span = SLC_LEN // CMP_STRIDE, CMP_LEN // CMP_STRIDE
    c2s = np.zeros((ncb * 128, n_slc), np.float32)
    for j in range(n_slc):
        for m_ in range(ratio):
            for n_ in range(span):
                cc = ratio * j + m_ + n_
                if cc < n_cmp:
                    c2s[cc, j] += 1.0
    c["c2s"] = c2s.reshape(ncb, 128, n_slc).transpose(1, 0, 2).copy()
    tpos = np.arange(S)
    cur = tpos // SLC_LEN
    jb = np.arange(n_slc)[None, :]
    forced = (jb == 0) | (jb == cur[:, None]) | (jb == cur[:, None] - 1)
    causal = jb * SLC_LEN <= tpos[:, None]
    Am = (causal & ~forced).astype(np.float32)
    Bm = np.where(causal, np.where(forced, 1e9, 0.0), -1.0).astype(np.float32)
    c["selA"] = Am.reshape(S // 128, 128, n_slc).transpose(1, 0, 2).copy()
    c["selB"] = Bm.reshape(S // 128, 128, n_slc).transpose(1, 0, 2).copy()
    ex = np.zeros((n_slc, S), np.float32)
    ex[np.arange(S) // SLC_LEN, np.arange(S)] = 1.0
    c["expand"] = ex
    tri = (np.arange(128)[:, None] <= np.arange(128)[None, :]).astype(np.float32)
    c["triL"] = tri
    c["triU"] = (np.arange(128)[:, None] > np.arange(128)[None, :]).astype(np.float32)
    s127 = np.zeros((128, 128), np.float32)
    s127[127, :] = 1.0
    c["sel127"] = s127
    c["n_cmp"], c["ncb"], c["n_slc"] = n_cmp, ncb, n_slc
    return c


def build(cfg):
    D, S, B, L, HM, G, FF, PLE = (cfg[k] for k in ["D", "S", "B", "L", "HM", "G", "FF", "PLE"])
    NT = B * S
    DC = D // 128
    NFF = FF // 128
    FFG = min(32, NFF)
    NGRP = NFF // FFG
    PC = PLE // 128
    plan = col_plan(cfg)
    hc = host_consts(cfg)
    midx = hc["midx"]
    NFM, NTM = plan["nfm"], plan["ntm"]
    TMW = NTM * 256
    n_cmp, ncb, n_slc = hc["n_cmp"], hc["ncb"], hc["n_slc"]
    NQC = S // 512
    NKB = S // 128
    HPG = HM // G
    scale = 128 ** -0.5

    nc = bass.Bass("TRN2", target_bir_lowering=False)
    es = contextlib.ExitStack()
    P = Prog(nc)

    def din(name, shape):
        return nc.dram_tensor(name, list(shape), F32, kind="ExternalInput").ap()

    def dint(name, shape, dt=F32):
        return nc.dram_tensor(name, list(shape), dt, kind="ExternalOutput" if cfg.get("debug") else "Internal").ap()

    xT = din("xT", [DC, 128, NT])
    pT = din("pT", [L, PC, 128, NT])
    posd = nc.dram_tensor("pos", [1, NT], I32, kind="ExternalInput").ap()
    w_fm = din("w_fm", [L, NFM, 128, DC, 128])
    w_tm = din("w_tm", [L, NTM, 128, DC, 256])
    w_o = din("w_o", [L, DC, 128, DC, 128])
    w_up = din("w_up", [L, NFF, 128, DC, 128])
    w_dn = din("w_dn", [L, DC, 128, NFF, 128])
    w_pg = din("w_pg", [L, DC, 128, DC, 128])
    w_pp = din("w_pp", [L, DC, 128, PC, 128])
    gvec = din("gvec", [128, 3 * L + 1, DC])
    foxb = din("foxb", [128, L * HM])
    cw1 = din("cw1", [L, 2, 128, CMP_LEN, CMP_HID])
    cw2 = din("cw2", [L, 2, 128, 2, 128])
    cpe = din("cpe", [L, 2, 128, CMP_LEN])
    masks_d = din("masks", list(hc["masks"].shape))
    c2s_d = din("c2s", [128, ncb, n_slc])
    selA_d = din("selA", [128, S // 128, n_slc])
    selB_d = din("selB", [128, S // 128, n_slc])
    expand_d = din("expand", [n_slc, S])
    cst_d = din("cst", [6, 128, 128])
    invf_d = din("invf", [128, 1])
    yT = nc.dram_tensor("yT", [DC, 128, NT], F32, kind="ExternalOutput").ap()
    def dbf(name, shape):
        return nc.dram_tensor(name, list(shape), BF16, kind="Internal").ap()
    wb_fm = [dbf("wb_fm%d" % l_, [NFM, 128, DC, 128]) for l_ in range(L)]
    wb_o = [dbf("wb_o%d" % l_, [DC, 128, DC, 128]) for l_ in range(L)]
    wb_up = [dbf("wb_up%d" % l_, [NFF, 128, DC, 128]) for l_ in range(L)]
    wb_dn = [dbf("wb_dn%d" % l_, [DC, 128, NFF, 128]) for l_ in range(L)]
    wb_pg = [dbf("wb_pg%d" % l_, [DC, 128, DC, 128]) for l_ in range(L)]
    hTd = dint("hTd", [DC, 128, NT])
    mixT = dint("mixT", [DC, 128, NT])
    pfm_b = [dint("pfm%d" % b_, [NFM, 128, S]) for b_ in range(B)]
    ptm = dint("ptm", [NT, TMW])

    def sb(name, shape, dt=F32):
        return es.enter_context(nc.sbuf_tensor("s_" + name, list(shape), dt))

    ps = [es.enter_context(nc.psum_tensor("ps%d" % i, [128, 512], F32)) for i in range(8)]
    cst = sb("cst", [128, 6, 128], BF16)
    cstf = sb("cstf", [128, 6, 128], F32)
    perm_b, ident_b, triL_b, triU_b, s127_b, ones_b = (cst[:, i, :] for i in range(6))
    ident_f, triL_f, s127_f, ones_f = cstf[:, 1, :], cstf[:, 2, :], cstf[:, 4, :], cstf[:, 5, :]
    gv = sb("gv", [128, 3 * L + 1, DC])
    invf = sb("invf", [128, 1])
    ARB = 168 * 1024
    big = sb("big", [128, ARB], mybir.dt.uint8)
    stg = sb("stg", [128, 6, 512])
    rstd = sb("rstd", [128, 512])
    cos2 = sb("cos2", [128, 512])
    sin2 = sb("sin2", [128, 512])
    tmpc = sb("tmpc", [128, 512])
    sqb = sb("sqb", [128, 2, 512], BF16)
    ub = sb("ub", [128, 2, 512], BF16)
    rcol = sb("rcol", [128, 8])

    def arena(off, shape, dt):
        n = int(np.prod(shape))
        bsz = 2 if dt == BF16 else 4
        a = big[:, off:off + n * bsz].bitcast(dt)
        if len(shape) == 2:
            return a.rearrange("p (a b) -> p a b", b=shape[1])
        return a.rearrange("p (a b c) -> p a b c", b=shape[1], c=shape[2])
    o = 0
    hT = arena(o, [DC, 512], F32); o += DC * 512 * 4
    xb = arena(o, [DC, 512], BF16); o += DC * 512 * 2
    ab = arena(o, [FFG, 512], BF16); o += FFG * 512 * 2
    pTb = arena(o, [PC, 512], BF16); o += PC * 512 * 2
    slabs = big[:, o:o + 32768].bitcast(BF16).rearrange("p (a c n) -> p a c n", a=4, c=32); o += 32768
    assert o <= ARB, o

    def mm(out, lhsT, rhs, start, stop, reads, writes):
        P.op("pe", lambda e: e.matmul(out, lhsT, rhs, start=start, stop=stop), reads, writes)

    def act(out, in_, func, reads, writes, scale=None, bias=None):
        kw = {}
        if scale is not None:
            kw["scale"] = scale
        if bias is not None:
            kw["bias"] = bias
        P.op("act", lambda e: e.activation(out=out, in_=in_, func=func, **kw), reads, writes)

    def tt(eng, out, in0, in1, op, reads, writes):
        P.op(eng, lambda e: e.tensor_tensor(out=out, in0=in0, in1=in1, op=op), reads, writes)

    def ts(out, in0, s1, s2, op0, op1, reads, writes, eng="dve"):
        if op1 is None:
            P.op(eng, lambda e: e.tensor_scalar(out=out, in0=in0, scalar1=s1, scalar2=None, op0=op0), reads, writes)
        else:
            P.op(eng, lambda e: e.tensor_scalar(out=out, in0=in0, scalar1=s1, scalar2=s2, op0=op0, op1=op1), reads, writes)

    def cp(eng, out, in_, reads, writes):
        P.op(eng, lambda e: e.tensor_copy(out=out, in_=in_), reads, writes)

    def recip(out, in_, reads, writes):
        P.op("dve", lambda e: e.reciprocal(out=out, in_=in_), reads, writes)

    def mset(out, val, writes):
        P.op("dve", lambda e: e.memset(out, val), [], writes)

    def trn(out, in_, ident, reads, writes):
        P.op("pe", lambda e: e.transpose(out=out, in_=in_, identity=ident), reads, writes)

    def stt(out, in0, scalar, in1, op0, op1, reads, writes, eng="dve"):
        P.op(eng, lambda e: e.scalar_tensor_tensor(out=out, in0=in0, scalar=scalar, in1=in1, op0=op0, op1=op1), reads, writes)

    def dma(q, out, in_, reads, writes, key, **kw):
        P.op(q, lambda e: e.dma_start(out=out, in_=in_, **kw), reads, writes, dma_key=key)

    dma("pool", cst[:], cst_d.rearrange("a p b -> p a b"), [], ["cst"], "cst")
    dma("sp", cstf[:], cst_d.rearrange("a p b -> p a b"), [], ["cstf"], "cstf")
    dma("sp", gv[:], gvec, [], ["gv"], "gv")
    dma("sp", invf[:], invf_d, [], ["invf"], "invf")

    state = dict(slab=0, bank=0, stg=0)

    def next_stg():
        state["stg"] = (state["stg"] + 1) % 6
        return state["stg"]

    def gemm(slab_src, nk, rhs_fn, rhs_key, n_out, epilogue, ncols=512, extra_reads=(), wb=None, wkey=None, first=True):
        for n in range(n_out):
            si = state["slab"]
            state["slab"] = (si + 1) % 4
            bank = state["bank"]
            state["bank"] = (bank + 1) % 3
            if wb is None:
                dma("pool", slabs[:, si, :nk, :], slab_src(n), [], [("slab", si)], ("slab", si))
            elif first:
                dma("pool", slabs[:, si, :nk, :], slab_src(n), [], [("slab", si)], ("slab", si))
                dma("sp", wb(n), slabs[:, si, :nk, :], [("slab", si)], [("wb",) + wkey + (n,)], ("wbst", si))
            else:
                dma("pool", slabs[:, si, :nk, :], wb(n), [("wb",) + wkey + (n,)], [("slab", si)], ("slab", si))
            for c in range(nk):
                mm(ps[bank][:, :ncols], slabs[:, si, c, :], rhs_fn(c), c == 0, c == nk - 1,
                   [("slab", si), rhs_key(c), "cst"] + list(extra_reads), [("ps", bank)])
            epilogue(n, bank)

    SSB = 3

    def norm_pass(which):
        for c in range(DC):
            b2 = c % 2
            act(sqb[:, b2, :], hT[:, c, :], AF.Square, [("hT", c)], [("sq", b2)])
            mm(ps[SSB][:, :], ones_b, sqb[:, b2, :], c == 0, c == DC - 1, [("sq", b2), "cst"], [("ps", SSB)])
            ts(xb[:, c, :], hT[:, c, :], gv[:, which, c:c + 1], None, ALU.mult, None, [("hT", c), "gv"], [("xb", c)])
        ts(rstd[:], ps[SSB][:, :], 1.0 / D, EPS, ALU.mult, ALU.add, [("ps", SSB)], ["rstd"])
        act(rstd[:], rstd[:], AF.Sqrt, ["rstd"], ["rstd"])
        P.op("dve", lambda e: e.reciprocal(out=rstd[:], in_=rstd[:]), ["rstd"], ["rstd"])

    def hT_keys():
        return [("hT", c) for c in range(DC)]

    def load_hT(src, t0):
        for c0 in range(0, DC, 8):
            c1 = min(DC, c0 + 8)
            dma("sp", hT[:, c0:c1, :], src[c0:c1, :, t0:t0 + T].rearrange("c p t -> p c t"),
                [("dram_h", t0)], [("hT", c) for c in range(c0, c1)], ("hTld", c0))

    def store_hT(dst, t0, wkey):
        for c0 in range(0, DC, 8):
            c1 = min(DC, c0 + 8)
            dma("sp", dst[c0:c1, :, t0:t0 + T].rearrange("c p t -> p c t"), hT[:, c0:c1, :],
                [("hT", c) for c in range(c0, c1)], [wkey], ("hTst", c0))

    def phase_C(l, t0):
        for c0 in range(0, DC, 8):
            c1 = min(DC, c0 + 8)
            dma("pool", xb[:, c0:c1, :], mixT[c0:c1, :, t0:t0 + T].rearrange("c p t -> p c t"),
                [("mix", l)], [("xb", c) for c in range(c0, c1)], ("xbld", c0))

        def ep_res(n, bank):
            tt("dve", hT[:, n, :], hT[:, n, :], ps[bank][:, :], ALU.add, [("hT", n), ("ps", bank)], [("hT", n)])
        first = (t0 == 0)
        gemm(lambda n: w_o[l, n], DC, lambda c: xb[:, c, :], lambda c: ("xb", c), DC, ep_res,
             wb=lambda n: wb_o[l][n], wkey=("o", l), first=first)
        norm_pass(3 * l + 1)
        for g in range(NGRP):
            def ep_up(j, bank):
                si = next_stg()
                stt(stg[:, si, :], ps[bank][:, :], 0.0, rstd[:], ALU.max, ALU.mult, [("ps", bank), "rstd"], [("stg", si)])
                act(ab[:, j, :], stg[:, si, :], AF.Square, [("stg", si)], [("ab", j)])
            gemm(lambda j: w_up[l, g * FFG + j], DC, lambda c: xb[:, c, :], lambda c: ("xb", c), FFG, ep_up,
                 wb=lambda j: wb_up[l][g * FFG + j], wkey=("up", l, g), first=first)
            gemm(lambda n: w_dn[l, n, :, g * FFG:(g + 1) * FFG, :], FFG, lambda c: ab[:, c, :], lambda c: ("ab", c), DC, ep_res,
                 wb=lambda n: wb_dn[l][n, :, g * FFG:(g + 1) * FFG, :], wkey=("dn", l, g), first=first)
        norm_pass(3 * l + 2)
        dma("pool", pTb[:, :, :], pT[l, :, :, t0:t0 + T].rearrange("c p t -> p c t"), [], ["pTb"], "pTb")
        for n in range(DC):
            si = state["slab"]
            state["slab"] = (si + 1) % 4
            dma("pool", slabs[:, si, :PC, :], w_pp[l, n], [], [("slab", si)], ("slab", si))
            for c in range(PC):
                mm(ps[4][:, :], slabs[:, si, c, :], pTb[:, c, :], c == 0, c == PC - 1, [("slab", si), "pTb"], [("ps", 4)])

            def ep_gate(n_, bank):
                s1, s2 = next_stg(), next_stg()
                tt("dve", stg[:, s1, :], ps[bank][:, :], rstd[:], ALU.mult, [("ps", bank), "rstd"], [("stg", s1)])
                act(stg[:, s2, :], stg[:, s1, :], AF.Sigmoid, [("stg", s1)], [("stg", s2)])
                tt("dve", stg[:, s1, :], stg[:, s2, :], ps[4][:, :], ALU.mult, [("stg", s2), ("ps", 4)], [("stg", s1)])
                tt("dve", hT[:, n, :], hT[:, n, :], stg[:, s1, :], ALU.add, [("hT", n), ("stg", s1)], [("hT", n)])
            gemm(lambda n_: w_pg[l, n], DC, lambda c: xb[:, c, :], lambda c: ("xb", c), 1, ep_gate,
                 wb=lambda n_: wb_pg[l][n], wkey=("pg", l, n), first=first)

    def phase_A(l, t0):
        norm_pass(3 * l + 0)
        pi = sb_pos
        dma("sp", pi[:, :], posd[0:1, t0:t0 + T].partition_broadcast(128), [], ["posi"], "posi")
        P.op("dve", lambda e: e.tensor_copy(out=tmpc[:], in_=pi[:, :]), ["posi"], ["tmpc"])
        ts(tmpc[:], tmpc[:], invf[:, 0:1], None, ALU.mult, None, ["tmpc", "invf"], ["tmpc"])
        TWO_PI = 2.0 * np.pi

        def sincos(dst, shift):
            ts(dst[:], tmpc[:], 1.0 / TWO_PI, shift, ALU.mult, ALU.add, ["tmpc"], [dst_key[id(dst)]])
            P.op("dve", lambda e: e.tensor_copy(out=pi[:, :], in_=dst[:]), [dst_key[id(dst)]], ["posi"])
            P.op("dve", lambda e: e.tensor_copy(out=stg[:, 0, :], in_=pi[:, :]), ["posi"], [("stg", 0)])
            tt("dve", dst[:], dst[:], stg[:, 0, :], ALU.subtract, [dst_key[id(dst)], ("stg", 0)], [dst_key[id(dst)]])
            ts(stg[:, 0, :], dst[:], 0.0, None, ALU.is_lt, None, [dst_key[id(dst)]], [("stg", 0)])
            tt("dve", dst[:], dst[:], stg[:, 0, :], ALU.add, [dst_key[id(dst)], ("stg", 0)], [dst_key[id(dst)]])
            ts(stg[:, 0, :], dst[:], 1.0, None, ALU.is_ge, None, [dst_key[id(dst)]], [("stg", 0)])
            tt("dve", dst[:], dst[:], stg[:, 0, :], ALU.subtract, [dst_key[id(dst)], ("stg", 0)], [dst_key[id(dst)]])
            ts(dst[:], dst[:], TWO_PI, -np.pi, ALU.mult, ALU.add, [dst_key[id(dst)]], [dst_key[id(dst)]])
            ts(dst[:], dst[:], -3.1415925, 3.1415925, ALU.max, ALU.min, [dst_key[id(dst)]], [dst_key[id(dst)]])
            act(dst[:], dst[:], AF.Sin, [dst_key[id(dst)]], [dst_key[id(dst)]])
        dst_key = {id(sin2): "sin2", id(cos2): "cos2"}
        sincos(sin2, 0.5)
        sincos(cos2, 0.75)

        fm = plan["fm"]

        def ep_fm(n, bank):
            s1 = next_stg()
            tt("dve", stg[:, s1, :], ps[bank][:, :], rstd[:], ALU.mult, [("ps", bank), "rstd"], [("stg", s1)])
            if n < len(fm) and fm[n][3]:
                b2 = n % 2
                act(ub[:, b2, :], stg[:, s1, :], AF.Copy, [("stg", s1)], [("ub", b2)])
                mm(ps[5][:, :], perm_b, ub[:, b2, :], True, True, [("ub", b2), "cst"], [("ps", 5)])
                s2 = next_stg()
                tt("dve", stg[:, s2, :], ps[5][:, :], sin2[:], ALU.mult, [("ps", 5), "sin2"], [("stg", s2)])
                tt("pool", stg[:, s1, :], stg[:, s1, :], cos2[:], ALU.mult, [("stg", s1), "cos2"], [("stg", s1)])
                tt("dve", stg[:, s1, :], stg[:, s1, :], stg[:, s2, :], ALU.add, [("stg", s1), ("stg", s2)], [("stg", s1)])
            dma("sp", pfm_b[t0 // S][n, :, t0 % S:t0 % S + T], stg[:, s1, :], [("stg", s1)], [("pfm", l)], ("stgo", s1))
        gemm(lambda n: w_fm[l, n], DC, lambda c: xb[:, c, :], lambda c: ("xb", c), NFM, ep_fm,
             wb=lambda n: wb_fm[l][n], wkey=("fm", l), first=(t0 == 0))
        for q4 in range(4):
            mm(ps[6][:, q4:q4 + 1], rstd[:, q4 * 128:(q4 + 1) * 128], cstf[:, 1, 0:1], True, True, ["rstd", "cstf"], [("ps", 6)])
        P.op("dve", lambda e: e.tensor_copy(out=rcol[:, 0:4], in_=ps[6][:, 0:4]), [("ps", 6)], ["rcol"])
        state["slab"] = (state["slab"] + 1) // 2 * 2 % 4
        for n in range(NTM):
            si = state["slab"]
            state["slab"] = (si + 2) % 4
            wv = slabs[:, si:si + 2, :, :].rearrange("p a c n -> p (a c n)").rearrange("p (c n) -> p c n", n=256)
            dma("pool", wv[:, :DC, :], w_tm[l, n], [], [("slab", si), ("slab", si + 1)], ("slab", si))
            for q4 in range(4):
                bank = state["bank"]
                state["bank"] = (bank + 1) % 3
                for c in range(DC):
                    mm(ps[bank][:, :256], xb[:, c, q4 * 128:(q4 + 1) * 128], wv[:, c, :], c == 0, c == DC - 1,
                       [("slab", si), ("slab", si + 1), ("xb", c)], [("ps", bank)])
                s1 = next_stg()
                act(stg[:, s1, :256], ps[bank][:, :256], AF.Copy, [("ps", bank), "rcol"], [("stg", s1)], scale=rcol[:, q4:q4 + 1])
                dma("sp", ptm[t0 + q4 * 128:t0 + (q4 + 1) * 128, n * 256:(n + 1) * 256], stg[:, s1, :256],
                    [("stg", s1)], [("ptm", l)], ("stgo", s1))

    sb_pos = sb("posi", [128, 512], I32)

    def phase_final(t0):
        norm_pass(3 * L)
        for c in range(DC):
            s1 = next_stg()
            stt(stg[:, s1, :], hT[:, c, :], gv[:, 3 * L, c:c + 1], rstd[:], ALU.mult, ALU.mult,
                [("hT", c), "gv", "rstd"], [("stg", s1)])
            dma("sp", yT[c, :, t0:t0 + T], stg[:, s1, :], [("stg", s1)], ["yT"], ("stgo", s1))

    o = 0
    mk_all = arena(o, [hc["masks"].shape[0], 512], BF16); o += hc["masks"].shape[0] * 1024
    qTb = arena(o, [1, S], BF16); o += S * 2
    kTb = arena(o, [2, S], BF16); o += 2 * S * 2
    vb = arena(o, [2, NKB, 128], BF16); o += 2 * S * 2
    PTb = arena(o, [4, 512], BF16); o += 4 * 1024
    msb = arena(o, [2, 512], BF16); o += 2 * 1024
    selT = arena(o, [1, S], BF16); o += S * 2
    expd = arena(o, [1, S], BF16); o += S * 2
    c2s = arena(o, [ncb, n_slc], BF16); o += ncb * n_slc * 2
    kccT = arena(o, [1, 256], BF16); o += 512
    vccb = arena(o, [ncb, 128], BF16); o += ncb * 256
    hid = arena(o, [2, 2, 256], BF16); o += 2048
    w1b = arena(o, [1, CMP_LEN, CMP_HID], BF16); o += CMP_LEN * CMP_HID * 2
    w2b = arena(o, [2, 2, 128], BF16); o += 1024
    peb = arena(o, [2, CMP_LEN], BF16); o += 2 * CMP_LEN * 2
    o = (o + 3) // 4 * 4
    selA = arena(o, [S // 128, n_slc], F32); o += S // 128 * n_slc * 4
    selB = arena(o, [S // 128, n_slc], F32); o += S // 128 * n_slc * 4
    gat = arena(o, [3, 512], F32); o += 3 * 2048
    accs = arena(o, [1, 512], F32); o += 2048
    lf = arena(o, [4, NKB], F32); o += 4 * NKB * 4
    fbias = arena(o, [1, NKB], F32); o += NKB * 4
    biaspe = arena(o, [1, 4], F32); o += 16
    sc = arena(o, [4, n_slc], F32); o += 4 * n_slc * 4
    mx8 = arena(o, [1, 16], F32); o += 64
    rsum = arena(o, [1, 512], F32); o += 2048
    foxb_s = arena(o, [1, L * HM], F32); o += L * HM * 4
    assert o <= ARB, o
    ARENA = ["arena"]

    def load_head(slot, which, idx_fm, b, l):
        buf = qTb if which == "q" else kTb
        dma("pool", buf[:, slot, :], pfm_b[b][idx_fm, :, :], [("pfm", l)], [(which, slot)], (which, slot))

    def load_v(slot, idx_tm, b, l):
        src = ptm[b * S:(b + 1) * S, idx_tm * 128:(idx_tm + 1) * 128].rearrange("(j p) d -> p j d", p=128)
        dma("pool", vb[:, slot, :, :], src, [("ptm", l)], [("v", slot)], ("v", slot))

    def finish(o_bank, d_bank, first, gate_i):
        s1 = next_stg()
        if d_bank is None:
            P.op("dve", lambda e: e.tensor_copy(out=accs[:, 0, :], in_=ps[o_bank][:, :]), [("ps", o_bank)], ["accs"])
            return
        ts(stg[:, s1, :], ps[d_bank][:, :], 1e-30, None, ALU.max, None, [("ps", d_bank)], [("stg", s1)])
        recip(stg[:, s1, :], stg[:, s1, :], [("stg", s1)], [("stg", s1)])
        if gate_i is not None:
            tt("dve", stg[:, s1, :], stg[:, s1, :], gat[:, gate_i, :], ALU.mult, [("stg", s1), "gat"], [("stg", s1)])
        if first:
            tt("dve", accs[:, 0, :], ps[o_bank][:, :], stg[:, s1, :], ALU.mult, [("ps", o_bank), ("stg", s1)], ["accs"])
        else:
            tt("dve", stg[:, s1, :], ps[o_bank][:, :], stg[:, s1, :], ALU.mult, [("ps", o_bank), ("stg", s1)], [("stg", s1)])
            tt("dve", accs[:, 0, :], accs[:, 0, :], stg[:, s1, :], ALU.add, ["accs", ("stg", s1)], ["accs"])

    def store_acc(chunk, b, qc, l):
        dma("sp", mixT[chunk, :, b * S + qc * 512: b * S + (qc + 1) * 512], accs[:, 0, :], ["accs"], [("mix", l)], "accst")

    pstate = dict(s=0, pt=0)

    def softmax_tiles(qslot, kslot, vslot, qc, tiles, o_bank, d_bank, bias_fn=None, vsrc=None, ksrc=None):
        nt = len(tiles)
        for i, (jb, mi, dyn) in enumerate(tiles):
            sbk = 6 + pstate["s"]
            pstate["s"] ^= 1
            pt = pstate["pt"]
            pstate["pt"] = (pt + 1) % 4
            lhs_k = kTb[:, kslot, jb * 128:(jb + 1) * 128] if ksrc is None else ksrc(jb)
            mm(ps[sbk][:, :], lhs_k, qTb[:, qslot, qc * 512:(qc + 1) * 512], True, True,
               [("k", kslot), ("q", qslot), "kcc"], [("ps", sbk)])
            if bias_fn is None:
                act(PTb[:, pt, :], ps[sbk][:, :], AF.Exp, [("ps", sbk)], [("pt", pt)], scale=scale)
            else:
                for q4 in range(4):
                    act(PTb[:, pt, q4 * 128:(q4 + 1) * 128], ps[sbk][:, q4 * 128:(q4 + 1) * 128], AF.Exp,
                        [("ps", sbk), "fbias"], [("pt", pt)], scale=scale, bias=bias_fn(jb, qc * 4 + q4))
            if dyn is not None:
                dyn(jb, pt)
            if mi is not None:
                tt("pool", PTb[:, pt, :], PTb[:, pt, :], mk_all[:, mi, :], ALU.mult, [("pt", pt), "masks"], [("pt", pt)])
            lhs_v = vb[:, vslot, jb, :] if vsrc is None else vsrc(jb)
            mm(ps[o_bank][:, :], lhs_v, PTb[:, pt, :], i == 0, i == nt - 1, [("pt", pt), ("v", vslot), "vcc"], [("ps", o_bank)])
            if d_bank is not None:
                mm(ps[d_bank][:, :], ones_b, PTb[:, pt, :], i == 0, i == nt - 1, [("pt", pt), "cst"], [("ps", d_bank)])

    def band_tiles(name, qc, lo_blocks):
        out = []
        for jb in range(NKB):
            r = jb * 128 - qc * 512
            if r > 384:
                break
            ent = midx.get((name, r))
            if ent is None:
                continue
            out.append((jb, None if ent[1] else ent[0], None))
        return out

    def attention(l):
        P.barrier()
        for m0 in range(0, hc["masks"].shape[0], 16):
            m1 = min(hc["masks"].shape[0], m0 + 16)
            dma("pool", mk_all[:, m0:m1, :], masks_d[m0:m1].rearrange("m p q -> p m q"), [],
                ["masks"], ("mk", m0))
        dma("pool", expd[:n_slc, 0, :], expand_d, [], ["expd"], "expd")
        dma("pool", c2s[:, :, :], c2s_d, [], ["c2s"], "c2s")
        dma("sp", selA[:, :, :], selA_d, [], ["selA"], "selA")
        dma("sp", selB[:, :, :], selB_d, [], ["selB"], "selB")
        dma("sp", foxb_s[:, 0, :], foxb, [], ["foxb"], "foxb")
        for kv in range(2):
            dma("pool", w2b[:, kv, :, :], cw2[l, kv], [], ["w2"], ("w2", kv))
        dma("pool", peb[:, :, :], cpe[l].rearrange("a p c -> p a c"), [], ["pe"], "pe")
        fmi, tmi = plan["fmidx"], plan["tmidx"]
        small = NFM - 1
        for b in range(B):
            for h in range(HM):
                load_head(0, "q", fmi[("qb", h)], b, l)
                load_head(0, "k", fmi[("kb", h)], b, l)
                load_v(0, tmi[("vb", h)], b, l)
                fcol = plan["fcol0"] + h
                src = ptm[b * S:(b + 1) * S, fcol:fcol + 1].rearrange("(j p) o -> p (j o)", p=128)
                dma("sp", lf[:, 0, :], src, [("ptm", l)], ["lf0"], "lf0", allow_slow_non_contiguous=True)
                ts(lf[:, 0, :], lf[:, 0, :], foxb_s[:, 0, l * HM + h:l * HM + h + 1], None, ALU.add, None, ["lf0", "foxb"], ["lf0"])
                ts(lf[:, 0, :], lf[:, 0, :], -1.0, None, ALU.mult, None, ["lf0"], ["lf0"])
                act(lf[:, 0, :], lf[:, 0, :], AF.Exp, ["lf0"], ["lf0"])
                act(lf[:, 0, :], lf[:, 0, :], AF.Ln, ["lf0"], ["lf0"], bias=1.0)
                ts(lf[:, 0, :], lf[:, 0, :], -1.0, None, ALU.mult, None, ["lf0"], ["lf0"])
                P.op("dve", lambda e: e.memset(lf[:, 1, 0:1], 0.0), [], ["lf1"])
                for j in range(1, NKB):
                    tt("dve", lf[:, 1, j:j + 1], lf[:, 1, j - 1:j], lf[:, 0, j - 1:j], ALU.add, ["lf1", "lf0"], ["lf1"])
                mm(ps[4][:, :NKB], triL_f, lf[:, 0, :], True, False, ["lf0", "cstf"], [("ps", 4)])
                mm(ps[4][:, :NKB], ones_f, lf[:, 1, :], False, True, ["lf1", "cstf"], [("ps", 4)])
                P.op("dve", lambda e: e.tensor_copy(out=lf[:, 2, :], in_=ps[4][:, :NKB]), [("ps", 4)], ["lf2"])
                mm(ps[4][:, :NKB], s127_f, lf[:, 2, :], True, True, ["lf2", "cstf"], [("ps", 4)])
                P.op("dve", lambda e: e.tensor_copy(out=lf[:, 3, :], in_=ps[4][:, :NKB]), [("ps", 4)], ["lf3"])
                for qc in range(NQC):
                    tiles = band_tiles("causal", qc, None)

                    def bias_fn(jb, i, _=None):
                        return fbias[:, 0, i:i + 1]
                    new_tiles = []
                    for (jb, mi, _) in tiles:
                        new_tiles.append((jb, mi, None))
                    for ti, (jb, mi, _) in enumerate(new_tiles):
                        stt(fbias[:, 0, qc * 4:qc * 4 + 4], lf[:, 3, qc * 4:qc * 4 + 4], 1.0, lf[:, 2, jb:jb + 1].to_broadcast([128, 4]),
                            ALU.mult, ALU.subtract, ["lf3", "lf2"], ["fbias"])
                        ts(fbias[:, 0, qc * 4:qc * 4 + 4], fbias[:, 0, qc * 4:qc * 4 + 4], 0.0, None, ALU.min, None, ["fbias"], ["fbias"])
                        softmax_tiles_one(0, 0, 0, qc, jb, mi, ti, len(new_tiles), 4, 5, bias_fn)
                    finish(4, 5, True, None)
                    store_acc(HM + h, b, qc, l)
            for h in range(HM):
                load_head(0, "q", fmi[("qc", h)], b, l)
                load_head(0, "k", fmi[("kcc", h)], b, l)
                load_v(0, tmi[("vcc", h)], b, l)
                for qc in range(NQC):
                    softmax_tiles(0, 0, 0, qc, band_tiles("dil", qc, None), 4, 5)
                    finish(4, 5, True, None)
                    store_acc(2 * HM + h, b, qc, l)
            for h in range(HM):
                load_head(0, "q", fmi[("qd", h)], b, l)
                load_head(0, "k", fmi[("kd", h)], b, l)
                load_v(0, tmi[("vd", h)], b, l)
                for qc in range(NQC):
                    tiles = band_tiles("strict", qc, None)[::-1]
                    nt = len(tiles)
                    def sb_front(jb, mi):
                        sbk = 6 + pstate["s"]
                        pstate["s"] ^= 1
                        pt = pstate["pt"]
                        pt2 = (pt + 1) % 4
                        pstate["pt"] = (pt + 2) % 4
                        s1, s2 = next_stg(), next_stg()
                        mm(ps[sbk][:, :], kTb[:, 0, jb * 128:(jb + 1) * 128], qTb[:, 0, qc * 512:(qc + 1) * 512], True, True,
                           [("k", 0), ("q", 0)], [("ps", sbk)])
                        act(stg[:, s1, :], ps[sbk][:, :], AF.Exp, [("ps", sbk)], [("stg", s1)], scale=scale)
                        act(stg[:, s1, :], stg[:, s1, :], AF.Ln, [("stg", s1)], [("stg", s1)], bias=1.0)
                        if mi is not None:
                            tt("pool", stg[:, s1, :], stg[:, s1, :], mk_f(mi), ALU.mult, [("stg", s1), "masks"], [("stg", s1)])
                        act(PTb[:, pt, :], stg[:, s1, :], AF.Copy, [("stg", s1)], [("pt", pt)])
                        return (sbk, pt, pt2, s1, s2)

                    def sb_back(i, jb, mi, st):
                        sbk, pt, pt2, s1, s2 = st
                        mm(ps[5][:, :], triU_b, PTb[:, pt, :], True, i == 0, [("pt", pt), "cst"], [("ps", 5)])
                        if i > 0:
                            mm(ps[5][:, :], ones_b, ub[:, 0, :], False, True, [("ub", 0), "cst"], [("ps", 5)])
                        stt(stg[:, s2, :], ps[sbk][:, :], scale, stg[:, s1, :], ALU.mult, ALU.subtract, [("ps", sbk), ("stg", s1)], [("stg", s2)])
                        tt("dve", stg[:, s2, :], stg[:, s2, :], ps[5][:, :], ALU.subtract, [("stg", s2), ("ps", 5)], [("stg", s2)])
                        if i == 0:
                            cp("pool", rsum[:, 0, :], stg[:, s1, :], [("stg", s1)], ["rsum"])
                        else:
                            tt("pool", rsum[:, 0, :], rsum[:, 0, :], stg[:, s1, :], ALU.add, ["rsum", ("stg", s1)], ["rsum"])
                        cp("pool", ub[:, 0, :], rsum[:, 0, :], ["rsum"], [("ub", 0)])
                        act(PTb[:, pt2, :], stg[:, s2, :], AF.Exp, [("stg", s2)], [("pt", pt2)])
                        if mi is not None:
                            tt("pool", PTb[:, pt2, :], PTb[:, pt2, :], mk_all[:, mi, :], ALU.mult, [("pt", pt2), "masks"], [("pt", pt2)])
                        mm(ps[4][:, :], vb[:, 0, jb, :], PTb[:, pt2, :], i == 0, i == nt - 1, [("pt", pt2), ("v", 0)], [("ps", 4)])
                    fr = sb_front(tiles[0][0], tiles[0][1]) if nt else None
                    for i, (jb, mi, _) in enumerate(tiles):
                        nxt = sb_front(tiles[i + 1][0], tiles[i + 1][1]) if i + 1 < nt else None
                        sb_back(i, jb, mi, fr)
                        fr = nxt
                    if nt == 0:
                        P.op("dve", lambda e: e.memset(accs[:, 0, :], 0.0), [], ["accs"])
                    else:
                        finish(4, None, True, None)
                    store_acc(3 * HM + h, b, qc, l)
            for g in range(G):
                nsa_group(l, b, g)

    def mk_f(mi):
        return mk_all[:, mi, :]

    def softmax_tiles_one(qslot, kslot, vslot, qc, jb, mi, i, nt, o_bank, d_bank, bias_fn):
        sbk = 6 + pstate["s"]
        pstate["s"] ^= 1
        pt = pstate["pt"]
        pstate["pt"] = (pt + 1) % 4
        mm(ps[sbk][:, :], kTb[:, kslot, jb * 128:(jb + 1) * 128], qTb[:, qslot, qc * 512:(qc + 1) * 512], True, True,
           [("k", kslot), ("q", qslot)], [("ps", sbk)])
        for q4 in range(4):
            act(PTb[:, pt, q4 * 128:(q4 + 1) * 128], ps[sbk][:, q4 * 128:(q4 + 1) * 128], AF.Exp,
                [("ps", sbk), "fbias"], [("pt", pt)], scale=scale, bias=bias_fn(jb, qc * 4 + q4))
        if mi is not None:
            tt("pool", PTb[:, pt, :], PTb[:, pt, :], mk_all[:, mi, :], ALU.mult, [("pt", pt), "masks"], [("pt", pt)])
        mm(ps[o_bank][:, :], vb[:, vslot, jb, :], PTb[:, pt, :], i == 0, i == nt - 1, [("pt", pt), ("v", vslot)], [("ps", o_bank)])
        mm(ps[d_bank][:, :], ones_b, PTb[:, pt, :], i == 0, i == nt - 1, [("pt", pt), "cst"], [("ps", d_bank)])

    def nsa_group(l, b, g):
        fmi, tmi = plan["fmidx"], plan["tmidx"]
        small = NFM - 1
        for kv in range(2):
            load_head(1, "k", fmi[("kc" if kv == 0 else "vc", g)], b, l)
            dma("pool", w1b[:, 0, :, :], cw1[l, kv], [], [("w1", 0)], ("w1", 0))
            for ht in range(2):
                for li in range(CMP_LEN):
                    mm(ps[4][:, ht:ht + 1], w1b[:, 0, li, ht * 128:(ht + 1) * 128], peb[:, kv, li:li + 1], li == 0, li == CMP_LEN - 1,
                       [("w1", 0), "pe"], [("ps", 4)])
            P.op("dve", lambda e: e.tensor_copy(out=biaspe[:, 0, 0:2], in_=ps[4][:, 0:2]), [("ps", 4)], ["biaspe"])
            for ht in range(2):
                for li in range(CMP_LEN):
                    rhs = kTb[:, 1, li:li + CMP_STRIDE * (n_cmp - 1) + 1:CMP_STRIDE]
                    mm(ps[5][:, :n_cmp], w1b[:, 0, li, ht * 128:(ht + 1) * 128], rhs, li == 0, li == CMP_LEN - 1,
                       [("w1", 0), ("k", 1)], [("ps", 5)])
                s1, s2 = next_stg(), next_stg()
                ts(stg[:, s1, :n_cmp], ps[5][:, :n_cmp], biaspe[:, 0, ht:ht + 1], None, ALU.add, None, [("ps", 5), "biaspe"], [("stg", s1)])
                tt("dve", stg[:, s2, :n_cmp], stg[:, s1, :n_cmp], stg[:, s1, :n_cmp], ALU.mult, [("stg", s1)], [("stg", s2)])
                ts(stg[:, s2, :n_cmp], stg[:, s2, :n_cmp], 0.044715, 1.0, ALU.mult, ALU.add, [("stg", s2)], [("stg", s2)])
                tt("dve", stg[:, s2, :n_cmp], stg[:, s2, :n_cmp], stg[:, s1, :n_cmp], ALU.mult, [("stg", s2), ("stg", s1)], [("stg", s2)])
                act(stg[:, s2, :n_cmp], stg[:, s2, :n_cmp], AF.Sigmoid, [("stg", s2)], [("stg", s2)], scale=1.5957691216)
                mset(hid[:, kv, ht, :], 0.0, [("hid", kv)])
                tt("dve", hid[:, kv, ht, :n_cmp], stg[:, s2, :n_cmp], stg[:, s1, :n_cmp], ALU.mult, [("stg", s2), ("stg", s1)], [("hid", kv)])
            if kv == 0:
                for ht in range(2):
                    mm(ps[5][:, :256], w2b[:, 0, ht, :], hid[:, 0, ht, :], ht == 0, ht == 1, [("hid", 0), "w2"], [("ps", 5)])
                P.op("dve", lambda e: e.tensor_copy(out=kccT[:, 0, :], in_=ps[5][:, :256]), [("ps", 5)], ["kcc"])
            else:
                for cb in range(ncb):
                    for ht in range(2):
                        mm(ps[5][:, :128], hid[:, 1, ht, cb * 128:(cb + 1) * 128], w2b[:, 1, ht, :], ht == 0, ht == 1,
                           [("hid", 1), "w2"], [("ps", 5)])
                    cp("dve", vccb[:, cb, :], ps[5][:, :128], [("ps", 5)], ["vcc"])

        def cmp_tiles(qc):
            out = []
            for cb in range(ncb):
                ent = midx.get(("cmp", cb, qc))
                if ent is not None:
                    out.append((cb, ent[0], None))
            return out
        for hh in range(HPG):
            load_head(hh % 2, "q", fmi[("qa", g * HPG + hh)], b, l) if False else None
        for qc in range(NQC):
            tl = cmp_tiles(qc)
            first_imp = True
            for hh in range(HPG):
                h = g * HPG + hh
                if qc == 0 or True:
                    pass
                qs = 0
                load_q_chunk(fmi[("qa", h)], b, l, qc)
                if not tl:
                    continue
                pts = []
                for i, (cb, mi, _) in enumerate(tl):
                    sbk = 6 + pstate["s"]
                    pstate["s"] ^= 1
                    pt = pstate["pt"]
                    pstate["pt"] = (pt + 1) % 4
                    pts.append(pt)
                    mm(ps[sbk][:, :], kccT[:, 0, cb * 128:(cb + 1) * 128], qchunk[:, 0, :], True, True, ["kcc", "qchunk"], [("ps", sbk)])
                    act(PTb[:, pt, :], ps[sbk][:, :], AF.Exp, [("ps", sbk)], [("pt", pt)], scale=scale)
                    tt("pool", PTb[:, pt, :], PTb[:, pt, :], mk_all[:, mi, :], ALU.mult, [("pt", pt), "masks"], [("pt", pt)])
                    mm(ps[5][:, :], ones_b, PTb[:, pt, :], i == 0, i == len(tl) - 1, [("pt", pt), "cst"], [("ps", 5)])
                s1 = next_stg()
                ts(stg[:, s1, :], ps[5][:, :], 1e-30, None, ALU.max, None, [("ps", 5)], [("stg", s1)])
                recip(stg[:, s1, :], stg[:, s1, :], [("stg", s1)], [("stg", s1)])
                for i, (cb, mi, _) in enumerate(tl):
                    pt = pts[i]
                    tt("dve", PTb[:, pt, :], PTb[:, pt, :], stg[:, s1, :], ALU.mult, [("pt", pt), ("stg", s1)], [("pt", pt)])
                    last = (hh == HPG - 1) and (i == len(tl) - 1)
                    mm(ps[4][:n_slc, :], c2s[:, cb, :], PTb[:, pt, :], first_imp, last, [("pt", pt), "c2s"], [("ps", 4)])
                    first_imp = False
            s1 = next_stg()
            if tl:
                cp("dve", stg[:n_slc, s1, :], ps[4][:n_slc, :], [("ps", 4)], [("stg", s1)])
            else:
                mset(stg[:n_slc, s1, :], 0.0, [("stg", s1)])
            for q4 in range(4):
                qb = qc * 4 + q4
                trn(ps[5][:, :n_slc], stg[:n_slc, s1, q4 * 128:(q4 + 1) * 128], ident_f[:n_slc, :n_slc], [("stg", s1), "cstf"], [("ps", 5)])
                tt("dve", sc[:, 0, :], ps[5][:, :n_slc], selA[:, qb, :], ALU.mult, [("ps", 5), "selA"], ["sc0"])
                tt("dve", sc[:, 0, :], sc[:, 0, :], selB[:, qb, :], ALU.add, ["sc0", "selB"], ["sc0"])
                P.op("dve", lambda e: e.max(out=mx8[:, 0, 0:8], in_=sc[:, 0, :]), ["sc0"], ["mx8"])
                P.op("dve", lambda e: e.match_replace(out=sc[:, 1, :], in_to_replace=mx8[:, 0, 0:8], in_values=sc[:, 0, :], imm_value=-1e30), ["mx8", "sc0"], ["sc1"])
                P.op("dve", lambda e: e.max(out=mx8[:, 0, 8:16], in_=sc[:, 1, :]), ["sc1"], ["mx8"])
                P.op("dve", lambda e: e.match_replace(out=sc[:, 2, :], in_to_replace=mx8[:, 0, 8:16], in_values=sc[:, 1, :], imm_value=-1e30), ["mx8", "sc1"], ["sc2"])
                tt("dve", sc[:, 3, :], sc[:, 0, :], sc[:, 2, :], ALU.not_equal, ["sc0", "sc2"], ["sc3"])
                trn(ps[5][:n_slc, 128:256], sc[:, 3, :], ident_f, ["sc3", "cstf"], [("ps", 5)])
                cp("dve", selT[:n_slc, 0, qb * 128:(qb + 1) * 128], ps[5][:n_slc, 128:256], [("ps", 5)], ["selT"])
        load_head(1, "k", fmi[("ks", g)], b, l)
        load_v(1, tmi[("vs", g)], b, l)
        load_head(0, "k", fmi[("kw", g)], b, l)
        load_v(0, tmi[("vw", g)], b, l)
        for hh in range(HPG):
            h = g * HPG + hh
            load_head(0, "q", fmi[("qa", h)], b, l)
            for qc in range(NQC):
                for br in range(3):
                    dma("sp", gat[:, br, :], pfm_b[b][small, 3 * h + br:3 * h + br + 1, qc * 512:(qc + 1) * 512].partition_broadcast(128),
                        [("pfm", l)], ["gat"], ("gat", br))
                act(gat[:, :, :], gat[:, :, :], AF.Sigmoid, ["gat"], ["gat"])
                tl = cmp_tiles(qc)
                if tl:
                    softmax_tiles(0, 0, 0, qc, tl, 4, 5, ksrc=lambda cb: kccT[:, 0, cb * 128:(cb + 1) * 128], vsrc=lambda cb: vccb[:, cb, :])
                    finish(4, 5, True, 0)
                else:
                    P.op("dve", lambda e: e.memset(accs[:, 0, :], 0.0), [], ["accs"])

                def dyn(jb, pt, qc=qc):
                    mm(ps[3][:, :], expd[:n_slc, 0, jb * 128:(jb + 1) * 128], selT[:n_slc, 0, qc * 512:(qc + 1) * 512], True, True,
                       ["expd", "selT"], [("ps", 3)])
                    tt("dve", PTb[:, pt, :], PTb[:, pt, :], ps[3][:, :], ALU.mult, [("pt", pt), ("ps", 3)], [("pt", pt)])
                tiles = [(jb, mi, dyn) for (jb, mi, _) in band_tiles("causal", qc, None)]
                softmax_tiles(0, 1, 1, qc, tiles, 4, 5)
                finish(4, 5, False, 1)
                softmax_tiles(0, 0, 0, qc, band_tiles("win", qc, None), 4, 5)
                finish(4, 5, False, 2)
                store_acc(h, b, qc, l)

    qchunk = sb("qchunk", [128, 1, 512], BF16)

    def load_q_chunk(idx_fm, b, l, qc):
        dma("pool", qchunk[:, 0, :], pfm_b[b][idx_fm, :, qc * 512:(qc + 1) * 512], [("pfm", l)], ["qchunk"], "qchunk")

    def dense_guard():
        P.barrier()

    for l in range(L + 1):
        for t0 in range(0, NT, T):
            if l == 0:
                load_hT(xT, t0)
            else:
                load_hT(hTd, t0)
                phase_C(l - 1, t0)
            if l < L:
                phase_A(l, t0)
                store_hT(hTd, t0, ("dram_h", t0))
            else:
                phase_final(t0)
        if l < L:
            attention(l)
            dense_guard()
    P.barrier()
    P.emit(es)
    es.close()
    return nc, hc, plan


def prep_inputs(cfg, hc, plan, x, p, positions, norm_attn, w_in, fox_bf, cmp_pe_k, cmp_w1_k, cmp_w2_k,
                cmp_pe_v, cmp_w1_v, cmp_w2_v, w_o, norm_mlp, w_up, w_down, norm_ple, w_ple_gate, w_ple_proj, norm_final):
    D, S, B, L, HM, G, FF, PLE = (cfg[k] for k in ["D", "S", "B", "L", "HM", "G", "FF", "PLE"])
    NT, DC = B * S, D // 128
    f = lambda a: np.ascontiguousarray(np.asarray(a, dtype=np.float32))
    m = {}
    m["xT"] = f(np.asarray(x).reshape(NT, DC, 128).transpose(1, 2, 0))
    m["pT"] = f(np.asarray(p).reshape(L, NT, PLE // 128, 128).transpose(0, 2, 3, 1))
    m["pos"] = np.ascontiguousarray(np.asarray(positions).reshape(1, NT).astype(np.int32))
    w_in = np.asarray(w_in)
    fm_cols = []
    for (_, _, c0, _) in plan["fm"]:
        fm_cols += list(range(c0, c0 + 128))
    sm = plan["small_cols"]
    wfm, wtm = [], []
    for l in range(L):
        Wf = np.zeros((D, plan["nfm"] * 128), np.float32)
        Wf[:, :len(fm_cols)] = w_in[l][:, fm_cols]
        Wf[:, len(fm_cols):len(fm_cols) + len(sm)] = w_in[l][:, sm]
        wfm.append(tile_w(Wf, 128))
        Wt = np.zeros((D, plan["ntm"] * 256), np.float32)
        Wt[:, :len(plan["tm_cols"])] = w_in[l][:, plan["tm_cols"]]
        wtm.append(tile_w(Wt, 256))
    m["w_fm"] = np.stack(wfm)
    m["w_tm"] = np.stack(wtm)
    m["w_o"] = np.stack([tile_w(np.asarray(w_o[l]), 128) for l in range(L)])
    m["w_up"] = np.stack([tile_w(np.asarray(w_up[l]), 128) for l in range(L)])
    m["w_dn"] = np.stack([tile_w(np.asarray(w_down[l]), 128) for l in range(L)])
    m["w_pg"] = np.stack([tile_w(np.asarray(w_ple_gate[l]), 128) for l in range(L)])
    m["w_pp"] = np.stack([tile_w(np.asarray(w_ple_proj[l]), 128) for l in range(L)])
    gv = np.zeros((3 * L + 1, D), np.float32)
    for l in range(L):
        gv[3 * l], gv[3 * l + 1], gv[3 * l + 2] = np.asarray(norm_attn[l]), np.asarray(norm_mlp[l]), np.asarray(norm_ple[l])
    gv[3 * L] = np.asarray(norm_final)
    m["gvec"] = f(gv.reshape(3 * L + 1, DC, 128).transpose(2, 0, 1))
    m["foxb"] = f(np.broadcast_to(np.asarray(fox_bf).reshape(1, L * HM), (128, L * HM)))
    w1 = np.stack([np.stack([np.asarray(cmp_w1_k[l]), np.asarray(cmp_w1_v[l])]) for l in range(L)])
    m["cw1"] = f(w1.reshape(L, 2, CMP_LEN, 128, CMP_HID).transpose(0, 1, 3, 2, 4))
    w2 = np.stack([np.stack([np.asarray(cmp_w2_k[l]), np.asarray(cmp_w2_v[l])]) for l in range(L)])
    m["cw2"] = f(w2.reshape(L, 2, 2, 128, 128).transpose(0, 1, 3, 2, 4))
    pe = np.stack([np.stack([np.asarray(cmp_pe_k[l]), np.asarray(cmp_pe_v[l])]) for l in range(L)])
    m["cpe"] = f(pe.transpose(0, 1, 3, 2))
    m["masks"] = f(hc["masks"])
    m["c2s"] = f(hc["c2s"])
    m["selA"] = f(hc["selA"])
    m["selB"] = f(hc["selB"])
    m["expand"] = f(hc["expand"])
    m["cst"] = f(np.stack([hc["perm"], hc["ident"], hc["triL"], hc["triU"], hc["sel127"], np.ones((128, 128), np.float32)]))
    m["invf"] = f(hc["invf"])
    return m


_CACHE = {}


def kernel(**inputs):
    full = CFG
    cfg = dict(full)
    cfg["B"] = 1
    key = tuple(sorted(cfg.items()))
    if key not in _CACHE:
        _CACHE[key] = build(cfg)
    nc, hc, plan = _CACHE[key]
    NB = full["B"]
    shared = None
    maps = []
    for b in range(NB):
        sub = dict(inputs)
        sub["x"] = np.asarray(inputs["x"])[b:b + 1]
        sub["p"] = np.asarray(inputs["p"])[:, b:b + 1]
        sub["positions"] = np.asarray(inputs["positions"])[b:b + 1]
        if shared is None:
            m = prep_inputs(cfg, hc, plan, **sub)
            shared = m
        else:
            m = dict(shared)
            D, S, L, PLE = cfg["D"], cfg["S"], cfg["L"], cfg["PLE"]
            m["xT"] = np.ascontiguousarray(sub["x"].reshape(S, D // 128, 128).transpose(1, 2, 0).astype(np.float32))
            m["pT"] = np.ascontiguousarray(sub["p"].reshape(L, S, PLE // 128, 128).transpose(0, 2, 3, 1).astype(np.float32))
            m["pos"] = np.ascontiguousarray(sub["positions"].reshape(1, S).astype(np.int32))
        maps.append(m)
    res = run_bass_kernel_spmd(nc, maps, core_ids=list(range(NB)))
    D, S = cfg["D"], cfg["S"]
    outs = []
    for b in range(NB):
        yT = res.results[b]["yT"]
        outs.append(np.ascontiguousarray(yT.reshape(D, S).T).reshape(1, S, D))
        if cfg.get("debug"):
            global DEBUG_OUT
            DEBUG_OUT = res.results[b]
    return np.concatenate(outs, axis=0).astype(np.float32)
```
